# Optimizing a Trainium2 kernel written in Bass

```python
import math
import jax
import jax.numpy as jnp
from jax import lax
import numpy as np

D_MODEL = 2048
BATCH = 2
SEQ = 8192
DEPTH = 2

ATTN_WIDTH = D_MODEL // 2
DIFF_HEAD_DIM = 64
DIFF_HEADS = ATTN_WIDTH // (2 * DIFF_HEAD_DIM)
S5_WIDTH = D_MODEL // 2
S5_GROUP = 16
S5_GROUPS = S5_WIDTH // S5_GROUP
S5_STATE = 64
IN_WIDTH = 3 * ATTN_WIDTH + S5_WIDTH
MIX_WIDTH = ATTN_WIDTH + S5_WIDTH
Q_BLOCK = 128
RWKV_HEAD = 64
RWKV_HEADS = D_MODEL // RWKV_HEAD
DECAY_LORA = 96
AAA_LORA = 96
GATE_LORA = 256
D_FF = -(-8 * D_MODEL // (3 * 256)) * 256
N_EVEN = (DEPTH + 1) // 2
N_ODD = DEPTH // 2
RMS_EPS = 1e-6
GN_EPS = 64e-5
F32 = jnp.float32

kernel_name = "hybrid_diffattn_s5_rwkv7_sandwich"


def _rmsnorm(x, gain, eps=RMS_EPS):
    xf = x.astype(F32)
    y = xf * lax.rsqrt(jnp.mean(xf * xf, axis=-1, keepdims=True) + eps)
    return (y * gain.astype(F32)).astype(x.dtype)


def _alibi_slopes(n_heads):
    return jnp.exp2(-8.0 * jnp.arange(1, n_heads + 1, dtype=F32) / n_heads)


def _diff_attention(q, k, v, lam, lambda_init, subln):
    bsz, seq_len = q.shape[0], q.shape[1]
    qh = jnp.transpose(q, (0, 2, 1, 3)).astype(F32)
    kh = jnp.transpose(k, (0, 2, 1, 3)).astype(F32)
    vh = jnp.transpose(v, (0, 2, 1, 3)).astype(F32)
    slopes = jnp.repeat(_alibi_slopes(DIFF_HEADS), 2)[:, None, None]
    scale = DIFF_HEAD_DIM ** -0.5
    k_pos = jnp.arange(seq_len)

    def q_block(i):
        start = i * Q_BLOCK
        qb = lax.dynamic_slice_in_dim(qh, start, Q_BLOCK, axis=2)
        s = jnp.einsum('bhqd,bhkd->bhqk', qb, kh) * scale
        dist = (start + jnp.arange(Q_BLOCK))[:, None] - k_pos[None, :]
        s = s - slopes * dist.astype(F32)
        s = jnp.where(dist >= 0, s, -jnp.inf)
        p = jax.nn.softmax(s, axis=-1).reshape(bsz, DIFF_HEADS, 2, Q_BLOCK, seq_len)
        p = p[:, :, 0] - lam * p[:, :, 1]
        return jnp.einsum('bhqk,bhke->bhqe', p, vh)

    out = lax.map(q_block, jnp.arange(seq_len // Q_BLOCK))
    out = jnp.transpose(out, (1, 0, 3, 2, 4)).reshape(bsz, seq_len, DIFF_HEADS, 2 * DIFF_HEAD_DIM)
    out = _rmsnorm(out, subln, eps=1e-5) * (1.0 - lambda_init)
    return out.reshape(bsz, seq_len, ATTN_WIDTH).astype(v.dtype)


def _complex_affine_combine(e1, e2):
    a1r, a1i, b1r, b1i = e1
    a2r, a2i, b2r, b2i = e2
    return (a2r * a1r - a2i * a1i,
            a2r * a1i + a2i * a1r,
            a2r * b1r - a2i * b1i + b2r,
            a2r * b1i + a2i * b1r + b2i)


def _s5(u, lam_re, lam_im, log_dt, b_re, b_im, c_re, c_im, d_skip, w_glu):
    bsz, seq_len = u.shape[0], u.shape[1]
    ut = jnp.transpose(u.astype(F32).reshape(bsz, seq_len, S5_GROUPS, S5_GROUP), (1, 0, 2, 3))
    lr = jnp.minimum(lam_re.astype(F32), -1e-4)
    li = lam_im.astype(F32)
    dt = jnp.exp(log_dt.astype(F32))[:, None]
    mag = jnp.exp(lr * dt)
    ab_re = mag * jnp.cos(li * dt)
    ab_im = mag * jnp.sin(li * dt)
    den = lr * lr + li * li
    n_re, n_im = ab_re - 1.0, ab_im
    f_re = (n_re * lr + n_im * li) / den
    f_im = (n_im * lr - n_re * li) / den
    br, bi = b_re.astype(F32), b_im.astype(F32)
    bb_re = f_re[..., None] * br - f_im[..., None] * bi
    bb_im = f_re[..., None] * bi + f_im[..., None] * br
    bu_re = jnp.einsum('lbgh,gph->lbgp', ut, bb_re)
    bu_im = jnp.einsum('lbgh,gph->lbgp', ut, bb_im)
    a_re = jnp.broadcast_to(ab_re[None, None], (seq_len, 1, S5_GROUPS, S5_STATE))
    a_im = jnp.broadcast_to(ab_im[None, None], (seq_len, 1, S5_GROUPS, S5_STATE))
    _, _, x_re, x_im = lax.associative_scan(_complex_affine_combine, (a_re, a_im, bu_re, bu_im), axis=0)
    y = (jnp.einsum('lbgp,ghp->lbgh', x_re, c_re.astype(F32))
         - jnp.einsum('lbgp,ghp->lbgh', x_im, c_im.astype(F32))
         + d_skip.astype(F32) * ut)
    y = jnp.transpose(y, (1, 0, 2, 3)).reshape(bsz, seq_len, S5_WIDTH)
    yg = jax.nn.gelu(y)
    out = yg * jax.nn.sigmoid(yg @ w_glu.astype(F32))
    return out.astype(u.dtype)


def _even_mixer(h, w_in, lambda_qk, subln, lam_re, lam_im, log_dt, b_re, b_im, c_re, c_im,
                d_skip, w_glu, w_out, layer_idx):
    bsz, seq_len, _ = h.shape
    proj = h @ w_in
    q = proj[..., :ATTN_WIDTH].reshape(bsz, seq_len, 2 * DIFF_HEADS, DIFF_HEAD_DIM)
    k = proj[..., ATTN_WIDTH:2 * ATTN_WIDTH].reshape(bsz, seq_len, 2 * DIFF_HEADS, DIFF_HEAD_DIM)
    v = proj[..., 2 * ATTN_WIDTH:3 * ATTN_WIDTH].reshape(bsz, seq_len, DIFF_HEADS, 2 * DIFF_HEAD_DIM)
    u = proj[..., 3 * ATTN_WIDTH:]
    lambda_init = 0.8 - 0.6 * math.exp(-0.3 * layer_idx)
    lq = lambda_qk.astype(F32)
    lam = jnp.exp(jnp.sum(lq[0] * lq[1])) - jnp.exp(jnp.sum(lq[2] * lq[3])) + lambda_init
    attn_out = _diff_attention(q, k, v, lam, lambda_init, subln)
    ssm_out = _s5(u, lam_re, lam_im, log_dt, b_re, b_im, c_re, c_im, d_skip, w_glu)
    return jnp.concatenate([attn_out, ssm_out.astype(attn_out.dtype)], axis=-1) @ w_out


def _rwkv7_time_mix(x, mu, w_r, w_k, w_v, w0, w1, w2, a0, a1, a2, g1, g2, k_k, k_a, r_k,
                    ln_w, ln_b, w_o):
    bsz, seq_len, dm = x.shape
    x_prev = jnp.pad(x, ((0, 0), (1, 0), (0, 0)))[:, :-1]
    xx = x_prev - x
    xr, xw, xk, xv, xa, xg = (x + xx * mu[j] for j in range(6))
    r = xr @ w_r
    k = xk @ w_k
    v = xv @ w_v
    w_log = -jax.nn.softplus(-(w0 + jnp.tanh(xw @ w1) @ w2).astype(F32)) - 0.5
    decay = jnp.exp(-jnp.exp(w_log))
    a = jax.nn.sigmoid((a0 + (xa @ a1) @ a2).astype(F32))
    g = jax.nn.sigmoid(xg @ g1) @ g2

    def heads(z):
        return z.astype(F32).reshape(bsz, seq_len, RWKV_HEADS, RWKV_HEAD)

    kk = heads(k * k_k)
    kk = kk / jnp.maximum(jnp.sqrt(jnp.sum(kk * kk, axis=-1, keepdims=True)), 1e-12)
    k = k.astype(F32) * (1.0 + (a - 1.0) * k_a.astype(F32))
    rh, wh, kh, vh, ah = heads(r), heads(decay), heads(k), heads(v), heads(a)

    def time_major(z):
        return jnp.transpose(z, (1, 0, 2, 3))

    def step(state, inp):
        r_t, w_t, k_t, v_t, a_t, b_t = inp
        sa = jnp.einsum('bhij,bhj->bhi', state, a_t)
        state = (state * w_t[:, :, None, :] + sa[..., None] * b_t[:, :, None, :]
                 + v_t[..., None] * k_t[:, :, None, :])
        return state, jnp.einsum('bhij,bhj->bhi', state, r_t)

    s0 = jnp.zeros((bsz, RWKV_HEADS, RWKV_HEAD, RWKV_HEAD), F32)
    seq_inputs = (time_major(rh), time_major(wh), time_major(kh), time_major(vh),
                  time_major(-kk), time_major(kk * ah))
    _, y = lax.scan(step, s0, seq_inputs)
    y = time_major(y)
    mean = jnp.mean(y, axis=-1, keepdims=True)
    var = jnp.mean(jnp.square(y - mean), axis=-1, keepdims=True)
    y = ((y - mean) * lax.rsqrt(var + GN_EPS)).reshape(bsz, seq_len, dm)
    y = y * ln_w.astype(F32) + ln_b.astype(F32)
    bonus = jnp.sum(rh * kh * r_k.astype(F32), axis=-1, keepdims=True) * vh
    out = (y + bonus.reshape(bsz, seq_len, dm)) * g.astype(F32)
    return out.astype(x.dtype) @ w_o


def _swiglu(h, w_gate, w_up, w_down):
    return (jax.nn.silu(h @ w_gate) * (h @ w_up)) @ w_down


def setup_inputs(seed: int = 0) -> dict:
    key = jax.random.key(seed)
    keys = iter(jax.random.split(key, 48))

    def normal(shape, scale=1.0):
        return jax.random.normal(next(keys), shape, F32) * scale

    def dense(shape, fan_in):
        return normal(shape, fan_in ** -0.5)

    E, O = N_EVEN, N_ODD
    ratio = jnp.arange(D_MODEL, dtype=F32) / (D_MODEL - 1)
    return {
        'x': normal((BATCH, SEQ, D_MODEL)),
        'norm_gains': 1.0 + normal((DEPTH, 4, D_MODEL), 0.05),
        'ev_w_in': dense((E, D_MODEL, IN_WIDTH), D_MODEL),
        'ev_lambda_qk': normal((E, 4, DIFF_HEAD_DIM), 0.1),
        'ev_attn_subln': 1.0 + normal((E, 2 * DIFF_HEAD_DIM), 0.05),
        'ev_s5_lambda_re': -0.5 + normal((E, S5_GROUPS, S5_STATE), 0.02),
        'ev_s5_lambda_im': jnp.pi * jnp.arange(S5_STATE, dtype=F32) + normal((E, S5_GROUPS, S5_STATE), 0.02),
        'ev_s5_log_dt': jax.random.uniform(next(keys), (E, S5_GROUPS), F32,
                                           minval=math.log(1e-3), maxval=math.log(1e-1)),
        'ev_s5_b_re': dense((E, S5_GROUPS, S5_STATE, S5_GROUP), 2 * S5_GROUP),
        'ev_s5_b_im': dense((E, S5_GROUPS, S5_STATE, S5_GROUP), 2 * S5_GROUP),
        'ev_s5_c_re': dense((E, S5_GROUPS, S5_GROUP, S5_STATE), S5_STATE),
        'ev_s5_c_im': dense((E, S5_GROUPS, S5_GROUP, S5_STATE), S5_STATE),
        'ev_s5_d': normal((E, S5_GROUPS, S5_GROUP)),
        'ev_s5_w_glu': dense((E, S5_WIDTH, S5_WIDTH), S5_WIDTH),
        'ev_w_out': dense((E, MIX_WIDTH, D_MODEL), MIX_WIDTH),
        'od_mu': jax.random.uniform(next(keys), (O, 6, D_MODEL), F32),
        'od_w_r': dense((O, D_MODEL, D_MODEL), D_MODEL),
        'od_w_k': dense((O, D_MODEL, D_MODEL), D_MODEL),
        'od_w_v': dense((O, D_MODEL, D_MODEL), D_MODEL),
        'od_w0': -6.0 + 5.0 * ratio ** 0.85 + normal((O, D_MODEL), 0.1),
        'od_w1': dense((O, D_MODEL, DECAY_LORA), D_MODEL),
        'od_w2': dense((O, DECAY_LORA, D_MODEL), DECAY_LORA) * 0.1,
        'od_a0': normal((O, D_MODEL), 0.1),
        'od_a1': dense((O, D_MODEL, AAA_LORA), D_MODEL),
        'od_a2': dense((O, AAA_LORA, D_MODEL), AAA_LORA),
        'od_g1': dense((O, D_MODEL, GATE_LORA), D_MODEL),
        'od_g2': dense((O, GATE_LORA, D_MODEL), GATE_LORA),
        'od_k_k': 0.85 + normal((O, D_MODEL), 0.05),
        'od_k_a': 1.0 + normal((O, D_MODEL), 0.05),
        'od_r_k': normal((O, RWKV_HEADS, RWKV_HEAD), 0.1),
        'od_ln_w': 1.0 + normal((O, D_MODEL), 0.05),
        'od_ln_b': normal((O, D_MODEL), 0.02),
        'od_w_o': dense((O, D_MODEL, D_MODEL), D_MODEL),
        'ffn_w_gate': dense((DEPTH, D_MODEL, D_FF), D_MODEL),
        'ffn_w_up': dense((DEPTH, D_MODEL, D_FF), D_MODEL),
        'ffn_w_down': dense((DEPTH, D_FF, D_MODEL), D_FF),
    }


def reference(x, norm_gains, ev_w_in, ev_lambda_qk, ev_attn_subln, ev_s5_lambda_re,
              ev_s5_lambda_im, ev_s5_log_dt, ev_s5_b_re, ev_s5_b_im, ev_s5_c_re, ev_s5_c_im,
              ev_s5_d, ev_s5_w_glu, ev_w_out, od_mu, od_w_r, od_w_k, od_w_v, od_w0, od_w1,
              od_w2, od_a0, od_a1, od_a2, od_g1, od_g2, od_k_k, od_k_a, od_r_k, od_ln_w,
              od_ln_b, od_w_o, ffn_w_gate, ffn_w_up, ffn_w_down):
    for layer in range(DEPTH):
        gains = norm_gains[layer]
        h = _rmsnorm(x, gains[0])
        if layer % 2 == 0:
            e = layer // 2
            m = _even_mixer(h, ev_w_in[e], ev_lambda_qk[e], ev_attn_subln[e], ev_s5_lambda_re[e],
                            ev_s5_lambda_im[e], ev_s5_log_dt[e], ev_s5_b_re[e], ev_s5_b_im[e],
                            ev_s5_c_re[e], ev_s5_c_im[e], ev_s5_d[e], ev_s5_w_glu[e],
                            ev_w_out[e], layer)
        else:
            o = layer // 2
            m = _rwkv7_time_mix(h, od_mu[o], od_w_r[o], od_w_k[o], od_w_v[o], od_w0[o], od_w1[o],
                                od_w2[o], od_a0[o], od_a1[o], od_a2[o], od_g1[o], od_g2[o],
                                od_k_k[o], od_k_a[o], od_r_k[o], od_ln_w[o], od_ln_b[o], od_w_o[o])
        x = x + _rmsnorm(m, gains[1])
        f = _swiglu(_rmsnorm(x, gains[2]), ffn_w_gate[layer], ffn_w_up[layer], ffn_w_down[layer])
        x = x + _rmsnorm(f, gains[3])
    return x
```

```python
import math
from contextlib import ExitStack
import numpy as np
import ml_dtypes
import concourse.bass as bass
import concourse.mybir as mybir
from concourse.bass_utils import run_bass_kernel_spmd

F32 = mybir.dt.float32
BF16 = mybir.dt.bfloat16
F32R = mybir.dt.float32r
AF = mybir.ActivationFunctionType
ALU = mybir.AluOpType
AX = mybir.AxisListType
NPBF = ml_dtypes.bfloat16

D = 2048
NCORES = 8
DFF = 5632
NDMA_SEM = 8


class Buf:
    __slots__ = ("name", "w", "r")

    def __init__(self, name):
        self.name = name
        self.w = None
        self.r = []


class Tile:
    __slots__ = ("t", "b")

    def __init__(self, t, b):
        self.t = t
        self.b = b


class Sched:
    def __init__(self, nc):
        self.nc = nc
        self.q = {e: [] for e in ("pe", "dve", "act", "pool", "sp")}
        self.sems = {}
        self.val = {}
        for e in ("pe", "dve", "act", "pool"):
            self._mksem(e)
        for i in range(NDMA_SEM):
            self._mksem(("spd", i))
            self._mksem(("poold", i))
            self._mksem(("actd", i))
        self.dma_rr = {"sp": 0, "pool": 0, "act": 0}
        self.waited = {e: {} for e in self.q}

    def _mksem(self, key):
        name = "s_" + (key if isinstance(key, str) else f"{key[0]}{key[1]}")
        self.sems[key] = self.nc.alloc_semaphore(name=name)
        self.val[key] = 0

    def _deps(self, eng, reads, writes):
        need = {}

        def add(dep):
            if dep is None:
                return
            deng, key, v = dep
            if deng == "pe" and eng == "pe" and key == "pe":
                return
            if need.get(key, 0) < v:
                need[key] = v

        for b in reads:
            add(b.w)
        for b in writes:
            add(b.w)
            for r in b.r:
                add(r)
        out = []
        for key, v in need.items():
            if self.waited[eng].get(key, 0) >= v:
                continue
            self.waited[eng][key] = v
            out.append((key, v))
        return out

    def op(self, eng, fn, reads=(), writes=(), dma=False):
        reads = [b for b in reads if b is not None]
        writes = [b for b in writes if b is not None]
        waits = self._deps(eng, reads, writes)
        if dma:
            qn = {"sp": "spd", "pool": "poold", "act": "actd"}[eng]
            i = self.dma_rr[eng]
            self.dma_rr[eng] = (i + 1) % NDMA_SEM
            key = (qn, i)
            inc = 16
        else:
            key = eng
            inc = 1
        self.val[key] += inc
        v = self.val[key]
        sem = self.sems[key]
        wl = [(self.sems[k], vv) for k, vv in waits]

        def emit(e):
            for s, vv in wl:
                e.wait_ge(s, vv)
            fn(e).then_inc(sem, inc)

        self.q[eng].append(emit)
        tag = (eng, key, v)
        for b in reads:
            b.r.append(tag)
        for b in writes:
            b.w = tag
            b.r = []
        return tag

    def barrier(self):
        snap = [(k_, self.sems[k_], v) for k_, v in self.val.items() if v > 0]
        for eng in self.q:
            wl = []
            for k_, sem, v in snap:
                if self.waited[eng].get(k_, 0) >= v:
                    continue
                self.waited[eng][k_] = v
                wl.append((sem, v))

            def emit(e, wl=wl):
                for sm, v in wl:
                    e.wait_ge(sm, v)
            self.q[eng].append(emit)

    def finish(self, final_bufs):
        need = {}
        for b in final_bufs:
            if b.w is not None:
                _, key, v = b.w
                need[key] = max(need.get(key, 0), v)
        wl = [(self.sems[k], v) for k, v in need.items()]

        def emit(e):
            for s, v in wl:
                e.wait_ge(s, v)
        self.q["sp"].append(emit)

    def emit_all(self):
        nc = self.nc
        q = self.q
        with nc.Block() as block:
            @block.tensor
            def _(e):
                for f in q["pe"]:
                    f(e)

            @block.vector
            def _(e):
                for f in q["dve"]:
                    f(e)

            @block.scalar
            def _(e):
                for f in q["act"]:
                    f(e)

            @block.gpsimd
            def _(e):
                for f in q["pool"]:
                    f(e)

            @block.sync
            def _(e):
                for f in q["sp"]:
                    f(e)


class K:
    def __init__(self, name="k"):
        self.nc = bass.Bass("TRN2", target_bir_lowering=False)
        self.es = ExitStack()
        self.S = Sched(self.nc)
        self.outs = []
        self.n = 0
        self.fused = False
        self.io = {}
        self.prefix = ""
        self.tiles = {}
        self._eps = {}

    def ext_in(self, name, shape, dt=F32):
        return self.nc.dram_tensor(name, list(shape), dt, kind="ExternalInput").ap()

    def ext_out(self, name, shape, dt=F32):
        return self.nc.dram_tensor(name, list(shape), dt, kind="ExternalOutput").ap()

    def dint(self, name, shape, dt=F32):
        return self.nc.dram_tensor(name, list(shape), dt).ap()

    def din(self, name, shape, dt=F32):
        if name in self.io:
            return self.io[name]
        assert not self.fused, name
        return self.ext_in(name, shape, dt)

    def dout(self, name, shape, dt=F32):
        if name in self.io:
            return self.io[name]
        assert not self.fused, name
        return self.ext_out(name, shape, dt)

    def begin_phase(self, prefix, io):
        self.prefix = prefix
        self.io = io
        self.tiles = {}
        self._eps = {}
        self.es = ExitStack()

    def end_phase(self):
        self.S.barrier()
        self.es.close()

    def buf(self, name=None):
        self.n += 1
        return Buf(name or f"b{self.n}")

    def cbuf(self, name):
        key = ("b", name)
        if key not in self.tiles:
            self.tiles[key] = self.buf(name)
        return self.tiles[key]

    def sb(self, name, shape, dt=F32):
        key = ("s", name)
        if key in self.tiles:
            return self.tiles[key]
        t = self.es.enter_context(self.nc.sbuf_tensor("s_" + self.prefix + name, list(shape), dt))
        self.tiles[key] = Tile(t, self.buf(name))
        return self.tiles[key]

    def sbn(self, name, shape, dt, n):
        return [self.sb(f"{name}{i}", shape, dt) for i in range(n)]

    def ps(self, name, shape, dt=F32):
        key = ("p", name)
        if key in self.tiles:
            return self.tiles[key]
        t = self.es.enter_context(self.nc.psum_tensor("p_" + self.prefix + name, list(shape), dt))
        self.tiles[key] = Tile(t, self.buf(name))
        return self.tiles[key]

    def dma(self, eng, out, in_, reads=(), writes=(), final=False):
        tag_b = None
        if final:
            tag_b = self.buf("out")
            self.outs.append(tag_b)
            writes = list(writes) + [tag_b]
        self.S.op(eng, lambda e: e.dma_start(out=out, in_=in_), reads, writes, dma=True)

    def mm(self, out, lhsT, rhs, start, stop, reads, writes, **kw):
        self.S.op("pe", lambda e: e.matmul(out, lhsT=lhsT, rhs=rhs, start=start, stop=stop, **kw),
                  reads, writes)

    def tr(self, out, in_, ident, reads, writes):
        self.S.op("pe", lambda e: e.transpose(out=out, in_=in_, identity=ident), reads, writes)

    def act(self, out, in_, func, reads, writes, **kw):
        self.S.op("act", lambda e: e.activation(out=out, in_=in_, func=func, **kw), reads, writes)

    def tt(self, eng, out, in0, in1, op, reads, writes):
        self.S.op(eng, lambda e: e.tensor_tensor(out=out, in0=in0, in1=in1, op=op), reads, writes)

    def ts(self, eng, out, in0, s1, s2, op0, op1, reads, writes, **kw):
        if op1 is None:
            self.S.op(eng, lambda e: e.tensor_scalar(out=out, in0=in0, scalar1=s1, scalar2=None,
                                                     op0=op0, **kw), reads, writes)
        else:
            self.S.op(eng, lambda e: e.tensor_scalar(out=out, in0=in0, scalar1=s1, scalar2=s2,
                                                     op0=op0, op1=op1, **kw), reads, writes)

    def stt(self, out, in0, scalar, in1, op0, op1, reads, writes):
        self.S.op("dve", lambda e: e.scalar_tensor_tensor(out=out, in0=in0, scalar=scalar, in1=in1,
                                                          op0=op0, op1=op1), reads, writes)

    def copy(self, eng, out, in_, reads, writes):
        if eng == "act":
            self.S.op("act", lambda e: e.copy(out=out, in_=in_), reads, writes)
        else:
            self.S.op(eng, lambda e: e.tensor_copy(out=out, in_=in_), reads, writes)

    def memset(self, eng, ap, val, writes):
        self.S.op(eng, lambda e: e.memset(ap, val), (), writes)

    def recip(self, out, in_, reads, writes):
        self.S.op("dve", lambda e: e.reciprocal(out=out, in_=in_), reads, writes)

    def reduce(self, out, in_, op, reads, writes, axis=AX.X):
        self.S.op("dve", lambda e: e.tensor_reduce(out=out, in_=in_, axis=axis, op=op), reads, writes)

    def scan(self, out, d0, d1, init, reads, writes):
        self.S.op("dve", lambda e: e.tensor_tensor_scan(out=out, data0=d0, data1=d1, initial=init,
                                                        op0=ALU.mult, op1=ALU.add), reads, writes)

    def make_ident(self, dt=F32, name="ident"):
        idt = self.sb(name, [128, 128], dt)
        self.memset("pool", idt.t[:], 1.0, [idt.b])
        self.S.op("pool", lambda e: e.affine_select(out=idt.t[:], in_=idt.t[:], pattern=[[1, 128]],
                                                    compare_op=ALU.is_equal, fill=0.0, base=0,
                                                    channel_multiplier=-1), [idt.b], [idt.b])
        return idt

    def done(self):
        if self.fused:
            self.end_phase()
            return None
        self.S.finish(self.outs)
        self.S.emit_all()
        self.es.close()
        return self.nc

    def finish_fused(self):
        self.S.finish(self.outs)
        self.S.emit_all()
        return self.nc


def run(nc, in_maps):
    res = run_bass_kernel_spmd(nc, in_maps, core_ids=list(range(len(in_maps))))
    return res.results


def rms_stats(k, src_ap, junk, ss, rstd, eps, n, reads, tmpname=""):
    k.act(junk.t[:, 0:n], src_ap, AF.Square, reads, [junk.b, ss.b], accum_out=ss.t[:, 0:1])
    k.act(rstd.t[:, 0:1], ss.t[:, 0:1], AF.Sqrt, [ss.b], [rstd.b], scale=1.0 / n, bias=k.eps_tile(eps))
    k.recip(rstd.t[:, 0:1], rstd.t[:, 0:1], [rstd.b], [rstd.b])


def rms_stats_dve(k, src_ap, junk_ap, junk_b, ss, rstd, eps, n, reads):
    k.S.op("dve", lambda e: e.scalar_tensor_tensor(out=junk_ap, in0=src_ap, scalar=1.0, in1=src_ap, op0=ALU.mult,
                                                   op1=ALU.mult, accum_out=ss.t[:, 0:1]), reads, [junk_b, ss.b])
    k.act(rstd.t[:, 0:1], ss.t[:, 0:1], AF.Ln, [ss.b], [rstd.b], scale=1.0 / n, bias=k.eps_tile(eps))
    k.act(rstd.t[:, 0:1], rstd.t[:, 0:1], AF.Exp, [rstd.b], [rstd.b], scale=-0.5)


def _eps_tile(self, eps):
    if eps not in self._eps:
        t = self.sb(f"eps{len(self._eps)}", [128, 1], F32)
        self.memset("pool", t.t[:], float(eps), [t.b])
        self._eps[eps] = t
    return self._eps[eps].t[:, 0:1]


K.eps_tile = _eps_tile


def norm_transpose_tile(k, x_ap, xb, gain, hb, junk, ss, rstd, ptr, hT_ap_fn, ident, reads_x, hT_buf,
                        eps=1e-6, evac_eng="act"):
    rms_stats(k, x_ap, junk, ss, rstd, eps, D, reads_x)
    k.stt(hb.t[:], x_ap, rstd.t[:, 0:1], gain.t[:], ALU.mult, ALU.mult, reads_x + [rstd.b, gain.b], [hb.b])
    for kc in range(16):
        k.tr(ptr.t[:, kc * 128:(kc + 1) * 128], hb.t[:, kc * 128:(kc + 1) * 128], ident.t[:],
             [hb.b, ident.b], [ptr.b])
    k.copy(evac_eng, hT_ap_fn(), ptr.t[:].rearrange("p (c t) -> p c t", c=16), [ptr.b], [hT_buf])


def build_L1(T, k=None):
    NT = T // 128
    TG = min(512, T)
    NG = T // TG
    k = k or K()
    x = k.din("x", [T, D])
    g0 = k.din("g0", [1, D])
    wA = k.din("wA", [24, 128, 16, 128])
    wV = k.din("wV", [2, 128, 16, 512])
    qkT = k.dout("qkT", [16, 128, T], BF16)
    uT = k.dout("uT", [8, 128, T], F32)
    v = k.dout("v", [T, 1024], BF16)

    TS = min(T, 2048)
    for sg in range(T // TS):
        _L1_group(k, TS, x[sg * TS:(sg + 1) * TS, :], g0, wA, wV, qkT[:, :, sg * TS:(sg + 1) * TS],
                  uT[:, :, sg * TS:(sg + 1) * TS], v[sg * TS:(sg + 1) * TS, :])
    return k.done()


def _L1_group(k, T, x, g0, wA, wV, qkT, uT, v):
    NT = T // 128
    TG = min(512, T)
    NG = T // TG
    ident = k.make_ident(BF16, "identb")
    gain = k.sb("gain", [128, D])
    k.dma("sp", gain.t[:], g0.partition_broadcast(128), [], [gain.b])
    hT = k.sb("hT", [128, 16, T], BF16)
    hTb = [k.cbuf(f"hT{i}") for i in range(NT)]
    xs = k.sbn("xs", [128, D], F32, 2)
    hbs = k.sbn("hb", [128, D], BF16, 2)
    junk = k.sb("junk", [128, D], BF16)
    sss = k.sbn("ss", [128, 1], F32, 2)
    rstds = k.sbn("rstd", [128, 1], F32, 2)
    ptrs = [k.ps(f"ptr{i}", [128, D], BF16) for i in range(2)]

    for i in range(NT):
        xt = xs[i % 2]
        k.dma("sp", xt.t[:], x[i * 128:(i + 1) * 128, :], [], [xt.b])
        norm_transpose_tile(k, xt.t[:], xt, gain, hbs[i % 2], junk, sss[i % 2], rstds[i % 2], ptrs[i % 2],
                            lambda i=i: hT.t[:, :, i * 128:(i + 1) * 128], ident, [xt.b], hTb[i])

    wts = k.sbn("wa", [128, 16, 128], BF16, 3)
    pacc = [k.ps(f"pacc{i}", [128, 512], F32) for i in range(2)]
    obf = k.sbn("obf", [128, 512], BF16, 2)
    of32 = k.sbn("of32", [128, 512], F32, 2)
    cnt = 0
    for cb in range(24):
        wt = wts[cb % 3]
        k.dma("pool", wt.t[:], wA[cb], [], [wt.b])
        for tg in range(NG):
            pa = pacc[cnt % 2]
            rd = [wt.b] + hTb[tg * (TG // 128):(tg + 1) * (TG // 128)]
            for kc in range(16):
                k.mm(pa.t[:, 0:TG], wt.t[:, kc, :], hT.t[:, kc, tg * TG:(tg + 1) * TG], kc == 0, kc == 15,
                     rd, [pa.b])
            if cb < 16:
                o = obf[cnt % 2]
                k.copy("act" if cnt % 2 else "dve", o.t[:, 0:TG], pa.t[:, 0:TG], [pa.b], [o.b])
                k.dma("sp", qkT[cb, :, tg * TG:(tg + 1) * TG], o.t[:, 0:TG], [o.b], [], final=True)
            else:
                o = of32[cnt % 2]
                k.copy("act" if cnt % 2 else "dve", o.t[:, 0:TG], pa.t[:, 0:TG], [pa.b], [o.b])
                k.dma("sp", uT[cb - 16, :, tg * TG:(tg + 1) * TG], o.t[:, 0:TG], [o.b], [], final=True)
            cnt += 1
    wvs = k.sbn("wv", [128, 16, 512], BF16, 2)
    for cg in range(2):
        wt = wvs[cg]
        k.dma("pool", wt.t[:], wV[cg], [], [wt.b])
        for i in range(NT):
            pa = pacc[cnt % 2]
            for kc in range(16):
                k.mm(pa.t[:], hT.t[:, kc, i * 128:(i + 1) * 128], wt.t[:, kc, :], kc == 0, kc == 15,
                     [wt.b, hTb[i]], [pa.b])
            o = obf[cnt % 2]
            k.copy("act" if cnt % 2 else "dve", o.t[:], pa.t[:], [pa.b], [o.b])
            k.dma("sp", v[i * 128:(i + 1) * 128, cg * 512:(cg + 1) * 512], o.t[:], [o.b], [], final=True)
            cnt += 1


def host_L1_weights(w_in):
    w = w_in.reshape(16, 128, 4096)
    qku = np.concatenate([w[:, :, 0:2048], w[:, :, 3072:4096]], axis=2)
    wA = np.ascontiguousarray(qku.reshape(16, 128, 24, 128).transpose(2, 1, 0, 3))
    wv = w[:, :, 2048:3072]
    wV = np.ascontiguousarray(wv.reshape(16, 128, 2, 512).transpose(2, 1, 0, 3))
    return wA, wV


def build_FFN(T, wdma="pool", k=None):
    TG = min(512, T)
    NG = T // TG
    NTG = TG // 128
    NF = DFF // 128
    k = k or K()
    x1 = k.din("x1", [T, D])
    g2 = k.din("g2", [1, D])
    g3 = k.din("g3", [1, D])
    wg = k.din("wg", [NF, 128, 16, 128])
    wu = k.din("wu", [NF, 128, 16, 128])
    wd = k.din("wd", [16, 128, NF, 128])
    x2 = k.dout("x2", [T, D])

    ident = k.make_ident(F32, "identf")
    gain2 = k.sb("gain2", [128, D])
    gain3 = k.sb("gain3", [128, D])
    k.dma("sp", gain2.t[:], g2.partition_broadcast(128), [], [gain2.b])
    k.dma("sp", gain3.t[:], g3.partition_broadcast(128), [], [gain3.b])
    xs = k.sbn("xs", [128, D], F32, 2)
    hb = k.sb("hb", [128, D], F32)
    junk = k.sb("junk", [128, D], BF16)
    ss = k.sb("ss", [128, 1]); rstd = k.sb("rstd", [128, 1])
    h2T = k.sb("h2T", [128, 16, TG], BF16)
    actT = k.sb("actT", [128, NF, TG], BF16)
    actb = [k.buf(f"act{i}") for i in range(NF)]
    wgs = k.sbn("wgs", [128, 16, 128], BF16, 2)
    wus = k.sbn("wus", [128, 16, 128], BF16, 2)
    wds = k.sbn("wds", [128, NF, 128], BF16, 2)
    sg = k.sbn("sg", [128, TG], F32, 2)
    fT = k.sb("fT", [128, 16, TG], F32)
    fTb = [k.buf(f"fT{i}") for i in range(16)]
    acc = [k.ps(f"acc{i}", [128, 512], F32) for i in range(4)]
    big = k.ps("big", [128, D], F32)

    for g in range(NG):
        for i in range(NTG):
            row = (g * NTG + i) * 128
            xt = xs[i % 2]
            k.dma("sp", xt.t[:], x1[row:row + 128, :], [], [xt.b])
            rms_stats(k, xt.t[:], junk, ss, rstd, 1e-6, D, [xt.b])
            k.stt(hb.t[:], xt.t[:], rstd.t[:, 0:1], gain2.t[:], ALU.mult, ALU.mult, [xt.b, rstd.b, gain2.b], [hb.b])
            for kc in range(16):
                k.tr(big.t[:, kc * 128:(kc + 1) * 128], hb.t[:, kc * 128:(kc + 1) * 128], ident.t[:],
                     [hb.b, ident.b], [big.b])
            k.copy("act", h2T.t[:, :, i * 128:(i + 1) * 128], big.t[:].rearrange("p (c t) -> p c t", c=16),
                   [big.b], [h2T.b])
        for fc in range(NF):
            wgt = wgs[fc % 2]; wut = wus[fc % 2]
            k.dma(wdma, wgt.t[:], wg[fc], [], [wgt.b])
            k.dma(wdma, wut.t[:], wu[fc], [], [wut.b])
            pg = acc[(fc % 2) * 2]; pu = acc[(fc % 2) * 2 + 1]
            for kc in range(16):
                k.mm(pg.t[:, 0:TG], wgt.t[:, kc, :], h2T.t[:, kc, :], kc == 0, kc == 15, [wgt.b, h2T.b], [pg.b])
            for kc in range(16):
                k.mm(pu.t[:, 0:TG], wut.t[:, kc, :], h2T.t[:, kc, :], kc == 0, kc == 15, [wut.b, h2T.b], [pu.b])
            s = sg[fc % 2]
            k.act(s.t[:, 0:TG], pg.t[:, 0:TG], AF.Silu, [pg.b], [s.b])
            k.tt("dve", actT.t[:, fc, :], s.t[:, 0:TG], pu.t[:, 0:TG], ALU.mult, [s.b, pu.b], [actb[fc]])
        for fb in range(16):
            wdt = wds[fb % 2]
            k.dma(wdma, wdt.t[:], wd[fb], [], [wdt.b])
            pa = acc[fb % 4]
            for kc in range(NF):
                k.mm(pa.t[:, 0:TG], wdt.t[:, kc, :], actT.t[:, kc, :], kc == 0, kc == NF - 1,
                     [wdt.b, actb[kc]], [pa.b])
            k.copy("act" if fb % 2 else "dve", fT.t[:, fb, :], pa.t[:, 0:TG], [pa.b], [fTb[fb]])
        for i in range(NTG):
            row = (g * NTG + i) * 128
            xt = xs[i % 2]
            k.dma("sp", xt.t[:], x1[row:row + 128, :], [], [xt.b])
            for fb in range(16):
                k.tr(big.t[:, fb * 128:(fb + 1) * 128], fT.t[:, fb, i * 128:(i + 1) * 128], ident.t[:],
                     [fTb[fb], ident.b], [big.b])
            rms_stats(k, big.t[:], junk, ss, rstd, 1e-6, D, [big.b])
            k.stt(hb.t[:], big.t[:], rstd.t[:, 0:1], gain3.t[:], ALU.mult, ALU.mult, [big.b, rstd.b, gain3.b], [hb.b])
            k.tt("dve", xt.t[:], xt.t[:], hb.t[:], ALU.add, [xt.b, hb.b], [xt.b])
            k.dma("sp", x2[row:row + 128, :], xt.t[:], [xt.b], [], final=True)
    return k.done()


def host_FFN_weights(w_gate, w_up, w_down):
    NF = DFF // 128
    def gu(w):
        return np.ascontiguousarray(w.reshape(16, 128, NF, 128).transpose(2, 1, 0, 3))
    wd = np.ascontiguousarray(w_down.reshape(NF, 128, 16, 128).transpose(2, 1, 0, 3))
    return gu(w_gate), gu(w_up), wd


LAMBDA_INIT0 = 0.8 - 0.6 * math.exp(-0.3 * 0)
ATT_SCALE = 64 ** -0.5


def build_ATT(SEQ, NSEQ, k=None, items=None):
    NQT = SEQ // 512
    NKT = SEQ // 128
    ND = 4 * NQT + 3
    k = k or K()
    tri_d = k.din("tri", [128, 128], BF16)
    lq_d = k.din("lq", [1, 256], F32)
    subln_d = k.din("subln", [1, 128], F32)
    if items is None:
        qT = k.din("qT", [NSEQ, 2, 64, SEQ], BF16)
        kT = k.din("kT", [NSEQ, 2, 64, SEQ], BF16)
        v = k.din("v", [NSEQ, SEQ, 128], BF16)
        qaug = k.din("qaug", [4, SEQ], BF16)
        kaug = k.din("kaug", [4, SEQ], BF16)
        biastab = k.din("biastab", [128, ND], F32)
        attn = k.dout("attn", [NSEQ, SEQ, 128], BF16)
        items = [dict(q=qT[s_], k=kT[s_], v=v[s_], qaug=qaug, kaug=kaug, bt=biastab, out=attn[s_]) for s_ in range(NSEQ)]

    bt = k.sb("bt", [128, ND])
    tri = k.sb("tri", [128, 128], BF16); k.dma("sp", tri.t[:], tri_d[:, :], [], [tri.b])
    lq = k.sb("lq", [128, 256]); k.dma("sp", lq.t[:], lq_d.partition_broadcast(128), [], [lq.b])
    sub = k.sb("sub", [128, 128]); k.dma("sp", sub.t[:], subln_d.partition_broadcast(128), [], [sub.b])
    k.ts("dve", sub.t[:], sub.t[:], 1.0 - LAMBDA_INIT0, None, ALU.mult, None, [sub.b], [sub.b])
    lp = k.sb("lp", [128, 2, 64]); l2 = k.sb("l2", [128, 2]); nlam = k.sb("nlam", [128, 1])
    lqv = lq.t[:].rearrange("p (a b c) -> p a b c", a=2, b=2)
    k.tt("dve", lp.t[:], lqv[:, :, 0, :], lqv[:, :, 1, :], ALU.mult, [lq.b], [lp.b])
    k.reduce(l2.t[:], lp.t[:], ALU.add, [lp.b], [l2.b])
    k.act(l2.t[:], l2.t[:], AF.Exp, [l2.b], [l2.b])
    k.tt("dve", nlam.t[:], l2.t[:, 1:2], l2.t[:, 0:1], ALU.subtract, [l2.b], [nlam.b])
    k.ts("dve", nlam.t[:], nlam.t[:], -LAMBDA_INIT0, None, ALU.add, None, [nlam.b], [nlam.b])

    qa = [k.sb(f"qa{m}", [68, SEQ], BF16) for m in range(2)]
    ka = [k.sb(f"ka{m}", [68, SEQ], BF16) for m in range(2)]
    va = k.sb("va", [128, NKT, 129], BF16)
    pS = [[k.ps(f"pS{m}{b}", [128, 512], F32) for b in range(2)] for m in range(2)]
    pO = [k.ps(f"pO{q}", [128, 512], F32) for q in range(4)]
    P = [[k.sb(f"P{m}{b}", [128, 512], BF16) for b in range(2)] for m in range(2)]
    rec = k.sbn("rec", [128, 2], F32, 2)
    rl = k.sbn("rl", [128, 1], F32, 2)
    t1 = k.sbn("t1", [128, 128], F32, 2)
    dd = k.sbn("dd", [128, 128], F32, 2)
    junkf = k.sb("junkf", [128, 128], F32)
    ss = k.sbn("ss", [128, 1], F32, 2)
    rstd = k.sbn("rstd", [128, 1], F32, 2)
    ob = k.sbn("ob", [128, 128], BF16, 2)
    fin = 0
    for it in items:
        k.dma("sp", bt.t[:], it["bt"], [], [bt.b])
        for m in range(2):
            k.dma("sp", qa[m].t[0:64, :], it["q"][m], [], [qa[m].b])
            k.dma("sp", qa[m].t[64:68, :], it["qaug"], [], [qa[m].b])
            k.dma("sp", ka[m].t[0:64, :], it["k"][m], [], [ka[m].b])
            k.dma("sp", ka[m].t[64:68, :], it["kaug"], [], [ka[m].b])
        k.memset("pool", va.t[:, :, 128:129], 1.0, [va.b])
        k.dma("sp", va.t[:, :, 0:128], it["v"].rearrange("(j p) e -> p j e", p=128), [], [va.b])
        pairs = [(i, j) for i in range(NQT) for j in range(4 * i + 4)]

        def emit_qk(n):
            i, j = pairs[n]
            q0 = max(j - 4 * i, 0) * 128
            for m in range(2):
                ps_ = pS[m][n % 2]
                k.mm(ps_.t[:, q0:512], ka[m].t[:, j * 128:(j + 1) * 128],
                     qa[m].t[:, i * 512 + q0:(i + 1) * 512], True, True, [ka[m].b, qa[m].b], [ps_.b])

        emit_qk(0)
        for n, (i, j) in enumerate(pairs):
            if n + 1 < len(pairs):
                emit_qk(n + 1)
            jj = j - 4 * i
            q0 = max(jj, 0) * 128
            d = 4 * i - j + 3
            for m in range(2):
                ps_ = pS[m][n % 2]
                pt = P[m][n % 2]
                k.act(pt.t[:, q0:512], ps_.t[:, q0:512], AF.Exp, [ps_.b, bt.b], [pt.b],
                      scale=ATT_SCALE, bias=bt.t[:, d:d + 1])
                if jj >= 0:
                    k.tt("dve", pt.t[:, q0:q0 + 128], pt.t[:, q0:q0 + 128], tri.t[:], ALU.mult,
                         [pt.b, tri.b], [pt.b])
                for qq in range(max(jj, 0), 4):
                    k.mm(pO[qq].t[:, m * 129:(m + 1) * 129], pt.t[:, qq * 128:(qq + 1) * 128],
                         va.t[:, j, :], (j == 0 and m == 0), (j == 4 * i + qq and m == 1),
                         [pt.b, va.b], [pO[qq].b], skip_group_check=True)
            if jj >= 0:
                qq = jj
                f = fin % 2
                fin += 1
                po = pO[qq]
                k.recip(rec[f].t[:], po.t[:, 128:258:129], [po.b], [rec[f].b])
                k.ts("dve", rl[f].t[:], rec[f].t[:, 1:2], nlam.t[:, 0:1], None, ALU.mult, None,
                     [rec[f].b, nlam.b], [rl[f].b])
                k.ts("dve", t1[f].t[:], po.t[:, 0:128], rec[f].t[:, 0:1], None, ALU.mult, None,
                     [po.b, rec[f].b], [t1[f].b])
                k.stt(dd[f].t[:], po.t[:, 129:257], rl[f].t[:, 0:1], t1[f].t[:], ALU.mult, ALU.add,
                      [po.b, rl[f].b, t1[f].b], [dd[f].b])
                rms_stats_dve(k, dd[f].t[:], junkf.t[:], junkf.b, ss[f], rstd[f], 1e-5, 128, [dd[f].b])
                k.stt(ob[f].t[:], dd[f].t[:], rstd[f].t[:, 0:1], sub.t[:], ALU.mult, ALU.mult,
                      [dd[f].b, rstd[f].b, sub.b], [ob[f].b])
                r0 = i * 512 + qq * 128
                k.dma("sp", it["out"][r0:r0 + 128, :], ob[f].t[:], [ob[f].b], [], final=True)
    return k.done()


def split_bf16(x, n=2):
    parts = []
    r = np.asarray(x, np.float64)
    for _ in range(n):
        p = r.astype(np.float32).astype(NPBF)
        parts.append(p)
        r = r - p.astype(np.float64)
    return parts


def host_ATT_consts(head, SEQ):
    slope = 2.0 ** (-8.0 * (head + 1) / 8)
    t = np.arange(SEQ)
    a = -slope * (t % 512) / ATT_SCALE
    b = slope * (t % 128) / ATT_SCALE
    ah, al = split_bf16(a)
    bh, bl = split_bf16(b)
    one = np.ones(SEQ, NPBF)
    qaug = np.stack([ah, al, one, one])
    kaug = np.stack([one, one, bh, bl])
    ND = 4 * (SEQ // 512) + 3
    biastab = np.broadcast_to((-slope * 128.0 * (np.arange(ND) - 3)).astype(np.float32)[None, :], (128, ND)).copy()
    tri = (np.arange(128)[None, :] >= np.arange(128)[:, None]).astype(NPBF)
    return qaug, kaug, biastab, tri


TWO_PI = 2.0 * math.pi
MAGIC = 12582912.0


def _range_reduce_sin(k, out, ang, tmp, shape_reads, writes, shift=0.0):
    a_ap, a_b = ang
    o_ap, o_b = out
    t_ap, t_b = tmp
    k.ts("dve", t_ap, a_ap, shift, 1.0 / TWO_PI, ALU.add, ALU.mult, [a_b], [t_b])
    k.ts("dve", o_ap, t_ap, MAGIC, None, ALU.add, None, [t_b], [o_b])
    k.ts("dve", o_ap, o_ap, -MAGIC, None, ALU.add, None, [o_b], [o_b])
    k.tt("dve", t_ap, t_ap, o_ap, ALU.subtract, [t_b, o_b], [t_b])
    k.ts("dve", t_ap, t_ap, TWO_PI, 3.1415925, ALU.mult, ALU.min, [t_b], [t_b])
    k.ts("dve", t_ap, t_ap, -3.1415925, None, ALU.max, None, [t_b], [t_b])
    k.act(o_ap, t_ap, AF.Sin, [t_b], [o_b])


def build_S5(SEQ, NSEQ, k=None, blocks=None):
    L = 512
    NB = SEQ // L
    k = k or K()
    if blocks is None:
        uT_ = k.din("uT", [NSEQ, 128, SEQ])
        ygT_ = k.dout("ygT", [NSEQ, 128, SEQ], BF16)
        blocks = [dict(uT=[uT_[s_] for s_ in range(NSEQ)], out=[ygT_[s_] for s_ in range(NSEQ)],
                       lre=k.din("lre", [128, 4]), lim=k.din("lim", [128, 4]), ldt=k.din("ldt", [128, 4]),
                       bre=k.din("bre", [4, 128, 16]), bim=k.din("bim", [4, 128, 16]),
                       ctre=k.din("ctre", [4, 128, 128]), ctim=k.din("ctim", [4, 128, 128]),
                       dcol=k.din("dcol", [128, 1]))]
    for B_ in blocks:
        lre_d, lim_d, ldt_d, bre_d, bim_d = B_["lre"], B_["lim"], B_["ldt"], B_["bre"], B_["bim"]
        ctre_d, ctim_d, dcol_d = B_["ctre"], B_["ctim"], B_["dcol"]
        uT_l, ygT_l = B_["uT"], B_["out"]
        NSEQ = len(uT_l)
        ident = k.make_ident(F32, "identf")
        lre = k.sb("lre", [128, 4]); lim = k.sb("lim", [128, 4]); ldt = k.sb("ldt", [128, 4])
        dcol = k.sb("dcol", [128, 1])
        for t_, d_ in ((lre, lre_d), (lim, lim_d), (ldt, ldt_d), (dcol, dcol_d)):
            k.dma("sp", t_.t[:], d_, [], [t_.b])
        cols = {n: k.sb(n, [128, 4]) for n in
                "lr dt rho th c1 s1 tmp tmp2 abre abim nre den fre fim cL sL nsL angL".split()}
        C = cols

        def T2(n):
            return C[n].t[:], C[n].b
        k.ts("dve", C["lr"].t[:], lre.t[:], -1e-4, None, ALU.min, None, [lre.b], [C["lr"].b])
        k.act(C["dt"].t[:], ldt.t[:], AF.Exp, [ldt.b], [C["dt"].b])
        k.tt("dve", C["tmp"].t[:], C["lr"].t[:], C["dt"].t[:], ALU.mult, [C["lr"].b, C["dt"].b], [C["tmp"].b])
        k.act(C["rho"].t[:], C["tmp"].t[:], AF.Exp, [C["tmp"].b], [C["rho"].b])
        k.tt("dve", C["th"].t[:], lim.t[:], C["dt"].t[:], ALU.mult, [lim.b, C["dt"].b], [C["th"].b])
        _range_reduce_sin(k, T2("s1"), T2("th"), T2("tmp"), None, None, 0.0)
        _range_reduce_sin(k, T2("c1"), T2("th"), T2("tmp"), None, None, math.pi / 2)
        k.ts("dve", C["angL"].t[:], C["th"].t[:], float(L), None, ALU.mult, None, [C["th"].b], [C["angL"].b])
        _range_reduce_sin(k, T2("sL"), T2("angL"), T2("tmp"), None, None, 0.0)
        _range_reduce_sin(k, T2("cL"), T2("angL"), T2("tmp"), None, None, math.pi / 2)
        k.ts("dve", C["nsL"].t[:], C["sL"].t[:], -1.0, None, ALU.mult, None, [C["sL"].b], [C["nsL"].b])
        k.tt("dve", C["abre"].t[:], C["rho"].t[:], C["c1"].t[:], ALU.mult, [C["rho"].b, C["c1"].b], [C["abre"].b])
        k.tt("dve", C["abim"].t[:], C["rho"].t[:], C["s1"].t[:], ALU.mult, [C["rho"].b, C["s1"].b], [C["abim"].b])
        k.ts("dve", C["nre"].t[:], C["abre"].t[:], -1.0, None, ALU.add, None, [C["abre"].b], [C["nre"].b])
        k.tt("dve", C["den"].t[:], C["lr"].t[:], C["lr"].t[:], ALU.mult, [C["lr"].b], [C["den"].b])
        k.tt("dve", C["tmp"].t[:], lim.t[:], lim.t[:], ALU.mult, [lim.b], [C["tmp"].b])
        k.tt("dve", C["den"].t[:], C["den"].t[:], C["tmp"].t[:], ALU.add, [C["den"].b, C["tmp"].b], [C["den"].b])
        k.recip(C["den"].t[:], C["den"].t[:], [C["den"].b], [C["den"].b])
        k.tt("dve", C["tmp"].t[:], C["nre"].t[:], C["lr"].t[:], ALU.mult, [C["nre"].b, C["lr"].b], [C["tmp"].b])
        k.tt("dve", C["tmp2"].t[:], C["abim"].t[:], lim.t[:], ALU.mult, [C["abim"].b, lim.b], [C["tmp2"].b])
        k.tt("dve", C["fre"].t[:], C["tmp"].t[:], C["tmp2"].t[:], ALU.add, [C["tmp"].b, C["tmp2"].b], [C["fre"].b])
        k.tt("dve", C["fre"].t[:], C["fre"].t[:], C["den"].t[:], ALU.mult, [C["fre"].b, C["den"].b], [C["fre"].b])
        k.tt("dve", C["tmp"].t[:], C["abim"].t[:], C["lr"].t[:], ALU.mult, [C["abim"].b, C["lr"].b], [C["tmp"].b])
        k.tt("dve", C["tmp2"].t[:], C["nre"].t[:], lim.t[:], ALU.mult, [C["nre"].b, lim.b], [C["tmp2"].b])
        k.tt("dve", C["fim"].t[:], C["tmp"].t[:], C["tmp2"].t[:], ALU.subtract, [C["tmp"].b, C["tmp2"].b], [C["fim"].b])
        k.tt("dve", C["fim"].t[:], C["fim"].t[:], C["den"].t[:], ALU.mult, [C["fim"].b, C["den"].b], [C["fim"].b])

        iota_i = k.sb("iota_i", [128, L], mybir.dt.int32)
        k.S.op("pool", lambda e: e.iota(iota_i.t[:], pattern=[[1, L]], base=0, channel_multiplier=0), [], [iota_i.b])
        iota = k.sb("iota", [128, L])
        k.copy("dve", iota.t[:], iota_i.t[:], [iota_i.b], [iota.b])
        ones = k.sb("ones", [128, L]); k.memset("pool", ones.t[:], 1.0, [ones.b])
        ptr = k.ps("ptr", [128, 128], F32)
        bbT = [[k.sb(f"bbT{r}{st}", [128, 128], BF16) for st in range(4)] for r in range(2)]
        ctb = [[k.sb(f"ctb{r}{st}", [128, 128], BF16) for st in range(4)] for r in range(2)]
        ctab = [k.sb(f"ctab{st}", [128, L]) for st in range(4)]
        stab = [k.sb(f"stab{st}", [128, L]) for st in range(4)]
        nstab = [k.sb(f"nstab{st}", [128, L]) for st in range(4)]
        rho = [k.sb(f"rho{st}", [128, L]) for st in range(4)]
        ang = k.sb("ang", [128, L]); tmpL = k.sb("tmpL", [128, L])
        br = k.sb("br", [128, 16]); bi = k.sb("bi", [128, 16]); bt1 = k.sb("bt1", [128, 16]); bb = k.sb("bb", [128, 16])
        Z = k.sb("Z", [128, 128]); ctf = k.sb("ctf", [128, 128])
        for st in range(4):
            k.dma("sp", br.t[:], bre_d[st], [], [br.b])
            k.dma("sp", bi.t[:], bim_d[st], [], [bi.b])
            for r in range(2):
                a_, b_ = (br, bi) if r == 0 else (bi, br)
                k.ts("dve", bt1.t[:], b_.t[:], C["fim"].t[:, st:st + 1], None, ALU.mult, None, [b_.b, C["fim"].b], [bt1.b])
                if r == 0:
                    k.ts("dve", bt1.t[:], bt1.t[:], -1.0, None, ALU.mult, None, [bt1.b], [bt1.b])
                k.stt(bb.t[:], a_.t[:], C["fre"].t[:, st:st + 1], bt1.t[:], ALU.mult, ALU.add,
                      [a_.b, C["fre"].b, bt1.b], [bb.b])
                k.memset("pool", Z.t[:], 0.0, [Z.b])
                k.copy("dve", Z.t[0:64, 32 * st:32 * st + 16], bb.t[0:64, :], [bb.b], [Z.b])
                k.copy("dve", Z.t[64:128, 32 * st + 16:32 * st + 32], bb.t[64:128, :], [bb.b], [Z.b])
                k.tr(ptr.t[:], Z.t[:], ident.t[:], [Z.b, ident.b], [ptr.b])
                k.copy("dve", bbT[r][st].t[:], ptr.t[:], [ptr.b], [bbT[r][st].b])
                k.dma("sp", ctf.t[:], (ctre_d if r == 0 else ctim_d)[st], [], [ctf.b])
                k.copy("dve", ctb[r][st].t[:], ctf.t[:], [ctf.b], [ctb[r][st].b])
            k.ts("dve", ang.t[:], iota.t[:], C["th"].t[:, st:st + 1], None, ALU.mult, None, [iota.b, C["th"].b], [ang.b])
            _range_reduce_sin(k, (stab[st].t[:], stab[st].b), (ang.t[:], ang.b), (tmpL.t[:], tmpL.b), None, None, 0.0)
            _range_reduce_sin(k, (ctab[st].t[:], ctab[st].b), (ang.t[:], ang.b), (tmpL.t[:], tmpL.b), None, None, math.pi / 2)
            k.ts("dve", nstab[st].t[:], stab[st].t[:], -1.0, None, ALU.mult, None, [stab[st].b], [nstab[st].b])
            k.ts("dve", rho[st].t[:], ones.t[:], C["rho"].t[:, st:st + 1], None, ALU.mult, None, [ones.b, C["rho"].b], [rho[st].b])

        uf = k.sbn("uf", [128, L], F32, 2)
        ub = k.sbn("ub", [128, L], BF16, 2)
        pb = [[k.ps(f"pb{r}{b}", [128, L], F32) for b in range(2)] for r in range(2)]
        py = [k.ps(f"py{b}", [128, L], F32) for b in range(2)]
        t = {n: k.sbn(n, [128, L], F32, 2) for n in ("t1", "t2", "t3", "t4", "zre", "zim", "wre", "wim", "m1", "m2", "m3", "m4")}
        xre = k.sbn("xre", [128, L], BF16, 2)
        nxim = k.sbn("nxim", [128, L], BF16, 2)
        init = [[k.sb(f"init{r}{st}", [128, 1]) for st in range(4)] for r in range(2)]
        ca = k.sbn("ca", [128, 1], F32, 2)
        yv = k.sbn("yv", [128, L], F32, 2)
        g1 = k.sbn("g1", [128, L], F32, 2)
        g2 = k.sbn("g2", [128, L], F32, 2)
        yg = k.sbn("yg", [128, L], BF16, 2)
        u_i = 0
        for s in range(NSEQ):
            for st in range(4):
                for r in range(2):
                    k.memset("pool", init[r][st].t[:], 0.0, [init[r][st].b])
            for blk in range(NB):
                ufb = uf[blk % 2]; ubb = ub[blk % 2]; pyb = py[blk % 2]
                k.dma("sp", ufb.t[:], uT_l[s][:, blk * L:(blk + 1) * L], [], [ufb.b])
                k.copy("act", ubb.t[:], ufb.t[:], [ufb.b], [ubb.b])
                for st in range(4):
                    p = u_i % 2
                    u_i += 1
                    pre, pim = pb[0][p], pb[1][p]
                    k.mm(pre.t[:], bbT[0][st].t[:], ubb.t[:], True, True, [bbT[0][st].b, ubb.b], [pre.b])
                    k.mm(pim.t[:], bbT[1][st].t[:], ubb.t[:], True, True, [bbT[1][st].b, ubb.b], [pim.b])
                    T = {n: t[n][p] for n in t}
                    c_, s_, ns_ = ctab[st], stab[st], nstab[st]
                    k.tt("dve", T["t1"].t[:], c_.t[:], pre.t[:], ALU.mult, [c_.b, pre.b], [T["t1"].b])
                    k.tt("dve", T["t2"].t[:], s_.t[:], pim.t[:], ALU.mult, [s_.b, pim.b], [T["t2"].b])
                    k.tt("dve", T["t3"].t[:], c_.t[:], pim.t[:], ALU.mult, [c_.b, pim.b], [T["t3"].b])
                    k.tt("dve", T["t4"].t[:], s_.t[:], pre.t[:], ALU.mult, [s_.b, pre.b], [T["t4"].b])
                    k.tt("dve", T["zre"].t[:], T["t1"].t[:], T["t2"].t[:], ALU.add, [T["t1"].b, T["t2"].b], [T["zre"].b])
                    k.tt("dve", T["zim"].t[:], T["t3"].t[:], T["t4"].t[:], ALU.subtract, [T["t3"].b, T["t4"].b], [T["zim"].b])
                    ir, ii = init[0][st], init[1][st]
                    k.scan(T["wre"].t[:], rho[st].t[:], T["zre"].t[:], ir.t[:, 0:1], [rho[st].b, T["zre"].b, ir.b], [T["wre"].b])
                    k.scan(T["wim"].t[:], rho[st].t[:], T["zim"].t[:], ii.t[:, 0:1], [rho[st].b, T["zim"].b, ii.b], [T["wim"].b])
                    cL = C["cL"].t[:, st:st + 1]; sL = C["sL"].t[:, st:st + 1]; nsL = C["nsL"].t[:, st:st + 1]
                    wl_re = T["wre"].t[:, L - 1:L]; wl_im = T["wim"].t[:, L - 1:L]
                    rb = [T["wre"].b, T["wim"].b, C["cL"].b, C["sL"].b, C["nsL"].b]
                    k.ts("dve", ca[0].t[:], wl_re, cL, None, ALU.mult, None, rb, [ca[0].b])
                    k.stt(ir.t[:], wl_im, nsL, ca[0].t[:], ALU.mult, ALU.add, rb + [ca[0].b], [ir.b])
                    k.ts("dve", ca[1].t[:], wl_im, cL, None, ALU.mult, None, rb, [ca[1].b])
                    k.stt(ii.t[:], wl_re, sL, ca[1].t[:], ALU.mult, ALU.add, rb + [ca[1].b], [ii.b])
                    k.tt("dve", T["m1"].t[:], c_.t[:], T["wre"].t[:], ALU.mult, [c_.b, T["wre"].b], [T["m1"].b])
                    k.tt("dve", T["m2"].t[:], s_.t[:], T["wim"].t[:], ALU.mult, [s_.b, T["wim"].b], [T["m2"].b])
                    k.tt("dve", T["m3"].t[:], ns_.t[:], T["wre"].t[:], ALU.mult, [ns_.b, T["wre"].b], [T["m3"].b])
                    k.tt("dve", T["m4"].t[:], c_.t[:], T["wim"].t[:], ALU.mult, [c_.b, T["wim"].b], [T["m4"].b])
                    xr = xre[p]; nxi = nxim[p]
                    k.tt("dve", xr.t[:], T["m1"].t[:], T["m2"].t[:], ALU.subtract, [T["m1"].b, T["m2"].b], [xr.b])
                    k.tt("dve", nxi.t[:], T["m3"].t[:], T["m4"].t[:], ALU.subtract, [T["m3"].b, T["m4"].b], [nxi.b])
                    k.mm(pyb.t[:], ctb[0][st].t[:], xr.t[:], st == 0, False, [ctb[0][st].b, xr.b], [pyb.b])
                    k.mm(pyb.t[:], ctb[1][st].t[:], nxi.t[:], False, st == 3, [ctb[1][st].b, nxi.b], [pyb.b])
                q = blk % 2
                k.stt(yv[q].t[:], ufb.t[:], dcol.t[:, 0:1], pyb.t[:], ALU.mult, ALU.add, [ufb.b, dcol.b, pyb.b], [yv[q].b])
                k.tt("dve", g1[q].t[:], yv[q].t[:], yv[q].t[:], ALU.mult, [yv[q].b], [g1[q].b])
                k.ts("dve", g1[q].t[:], g1[q].t[:], 0.044715, 1.0, ALU.mult, ALU.add, [g1[q].b], [g1[q].b])
                k.tt("dve", g2[q].t[:], g1[q].t[:], yv[q].t[:], ALU.mult, [g1[q].b, yv[q].b], [g2[q].b])
                k.act(g2[q].t[:], g2[q].t[:], AF.Sigmoid, [g2[q].b], [g2[q].b], scale=2.0 * math.sqrt(2.0 / math.pi))
                k.tt("dve", yg[q].t[:], yv[q].t[:], g2[q].t[:], ALU.mult, [yv[q].b, g2[q].b], [yg[q].b])
                k.dma("sp", ygT_l[s][:, blk * L:(blk + 1) * L], yg[q].t[:], [yg[q].b], [], final=True)


    return k.done()


def host_S5_params(core, lam_re, lam_im, log_dt, b_re, b_im, c_re, c_im, d_skip):
    g0 = 8 * core
    sl = slice(g0, g0 + 8)
    lre = np.ascontiguousarray(lam_re[sl].reshape(4, 128).T)
    lim = np.ascontiguousarray(lam_im[sl].reshape(4, 128).T)
    ldt = np.ascontiguousarray(np.repeat(log_dt[sl], 64).reshape(4, 128).T)
    bre = np.ascontiguousarray(b_re[sl].reshape(4, 128, 16))
    bim = np.ascontiguousarray(b_im[sl].reshape(4, 128, 16))
    ctre = np.zeros((4, 128, 128), np.float32)
    ctim = np.zeros((4, 128, 128), np.float32)
    for st in range(4):
        for gl in range(2):
            g = g0 + 2 * st + gl
            col = 32 * st + 16 * gl
            ctre[st, gl * 64:(gl + 1) * 64, col:col + 16] = c_re[g].T
            ctim[st, gl * 64:(gl + 1) * 64, col:col + 16] = c_im[g].T
    dcol = np.ascontiguousarray(d_skip[sl].reshape(128, 1))
    return dict(lre=lre, lim=lim, ldt=ldt, bre=bre, bim=bim, ctre=ctre, ctim=ctim, dcol=dcol)


def out_proj_tile(k, mixT, mixb, col0, wo, big, xt, hb, junk, ss, rstd, gain1, x_src, x_dst, row, xload=None):
    for cg in range(4):
        for kc in range(16):
            k.mm(big.t[:, cg * 512:(cg + 1) * 512], mixT.t[:, kc, col0:col0 + 128], wo[cg].t[:, kc, :],
                 kc == 0, kc == 15, [wo[cg].b] + mixb, [big.b])
    if xload is None:
        k.dma("sp", xt.t[:], x_src[row:row + 128, :], [], [xt.b])
    else:
        xload(xt)
    rms_stats(k, big.t[:], junk, ss, rstd, 1e-6, D, [big.b])
    k.stt(hb.t[:], big.t[:], rstd.t[:, 0:1], gain1.t[:], ALU.mult, ALU.mult, [big.b, rstd.b, gain1.b], [hb.b])
    k.tt("dve", xt.t[:], xt.t[:], hb.t[:], ALU.add, [xt.b, hb.b], [xt.b])
    k.dma("sp", x_dst[row:row + 128, :], xt.t[:], [xt.b], [], final=True)


def build_OUT0(T, k=None):
    TG = min(512, T)
    NG = T // TG
    NTG = TG // 128
    k = k or K()
    x = k.din("x", [T, D])
    attn = k.din("attn", [T, 1024], BF16)
    ygT = k.din("ygT", [8, 128, T], BF16)
    wglu = k.din("wglu", [8, 128, 8, 128])
    wout = k.din("wout", [4, 128, 16, 512])
    g1 = k.din("g1", [1, D])
    x1 = k.dout("x1", [T, D])

    gain1 = k.sb("gain1", [128, D]); k.dma("sp", gain1.t[:], g1.partition_broadcast(128), [], [gain1.b])
    wgl = k.sb("wgl", [128, 8, 8, 128], BF16)
    for cb in range(8):
        k.dma("pool", wgl.t[:, cb], wglu[cb], [], [wgl.b])
    wo = [k.sb(f"wo{cg}", [128, 16, 512], BF16) for cg in range(4)]
    for cg in range(4):
        k.dma("pool", wo[cg].t[:], wout[cg], [], [wo[cg].b])
    identb = k.make_ident(BF16, "identb")
    att_t = k.sbn("att_t", [128, 1024], BF16, 2)
    ptb = k.ps("ptb", [128, 1024], BF16)
    mixT = k.sbn("mixT", [128, 16, TG], BF16, 2)
    ygs = k.sbn("ygs", [128, 8, TG], BF16, 2)
    sg = k.sbn("sg", [128, TG], F32, 2)
    acc = [k.ps(f"acc{i}", [128, 512], F32) for i in range(2)]
    big = k.ps("big", [128, D], F32)
    xs = k.sbn("xs", [128, D], F32, 2)
    hb = k.sb("hb", [128, D], F32)
    junk = k.sb("junk", [128, D], BF16)
    ss = k.sb("ss", [128, 1]); rstd = k.sb("rstd", [128, 1])
    for g in range(NG):
        mx = mixT[g % 2]; yg = ygs[g % 2]
        for i in range(NTG):
            row = (g * NTG + i) * 128
            at = att_t[i % 2]
            k.dma("sp", at.t[:], attn[row:row + 128, :], [], [at.b])
            for c_ in range(8):
                k.tr(ptb.t[:, c_ * 128:(c_ + 1) * 128], at.t[:, c_ * 128:(c_ + 1) * 128], identb.t[:],
                     [at.b, identb.b], [ptb.b])
            k.copy("act", mx.t[:, 0:8, i * 128:(i + 1) * 128], ptb.t[:].rearrange("p (c t) -> p c t", c=8),
                   [ptb.b], [mx.b])
        k.dma("sp", yg.t[:], ygT[:, :, g * TG:(g + 1) * TG].rearrange("c p t -> p c t"), [], [yg.b])
        for cb in range(8):
            pa = acc[cb % 2]
            for kc in range(8):
                k.mm(pa.t[:, 0:TG], wgl.t[:, cb, kc, :], yg.t[:, kc, :], kc == 0, kc == 7, [wgl.b, yg.b], [pa.b])
            s = sg[cb % 2]
            k.act(s.t[:, 0:TG], pa.t[:, 0:TG], AF.Sigmoid, [pa.b], [s.b])
            k.tt("dve", mx.t[:, 8 + cb, :], s.t[:, 0:TG], yg.t[:, cb, :], ALU.mult, [s.b, yg.b], [mx.b])
        for i in range(NTG):
            row = (g * NTG + i) * 128
            out_proj_tile(k, mx, [mx.b], i * 128, wo, big, xs[i % 2], hb, junk, ss, rstd, gain1, x, x1, row)
    return k.done()


def host_OUT_weights(w_glu, w_out):
    wglu = None
    if w_glu is not None:
        wglu = np.ascontiguousarray(w_glu.reshape(8, 128, 8, 128).transpose(2, 1, 0, 3))
    wout = np.ascontiguousarray(w_out.reshape(16, 128, 4, 512).transpose(2, 1, 0, 3))
    return wglu, wout


LW_SCALE = -math.exp(-0.5)


def build_RPROJ(T, k=None):
    TG = min(512, T)
    NG = T // TG
    NTG = TG // 128
    HW = TG + 128
    k = k or K()
    x1h = k.din("x1h", [T + 128, D])
    g0 = k.din("g0", [1, D])
    mucol_d = k.din("mucol", [128, 6, 16])
    wr_d = k.din("wr", [4, 128, 16, 512]); wk_d = k.din("wk", [4, 128, 16, 512]); wv_d = k.din("wv", [4, 128, 16, 512])
    w1_d = k.din("w1", [128, 16, 96]); a1_d = k.din("a1", [128, 16, 96]); g1_d = k.din("g1w", [128, 16, 256])
    w2_d = k.din("w2", [96, D]); a2_d = k.din("a2", [96, D]); g2_d = k.din("g2w", [128, 2, D])
    w0_d = k.din("w0", [1, D]); a0_d = k.din("a0", [1, D]); kk_d = k.din("k_k", [1, D]); ka_d = k.din("k_a", [1, D])
    outs = {n: k.dout(n, [T, D]) for n in ("r", "lw", "kp", "v", "kkn", "bb", "g")}

    ident = k.make_ident(F32, "identf")
    gain = k.sb("gain", [128, D]); k.dma("sp", gain.t[:], g0.partition_broadcast(128), [], [gain.b])
    kkrow = k.sb("kkrow", [128, D]); k.dma("sp", kkrow.t[:], kk_d.partition_broadcast(128), [], [kkrow.b])
    karow = k.sb("karow", [128, D]); k.dma("sp", karow.t[:], ka_d.partition_broadcast(128), [], [karow.b])
    mu = k.sb("mu", [128, 6, 16]); k.dma("sp", mu.t[:], mucol_d[:, :, :], [], [mu.b])
    omu = k.sb("omu", [128, 6, 16])
    k.ts("dve", omu.t[:], mu.t[:], -1.0, 1.0, ALU.mult, ALU.add, [mu.b], [omu.b])
    w1s = k.sb("w1s", [128, 16, 96], BF16); k.dma("pool", w1s.t[:], w1_d[:, :, :], [], [w1s.b])
    a1s = k.sb("a1s", [128, 16, 96], BF16); k.dma("pool", a1s.t[:], a1_d[:, :, :], [], [a1s.b])
    g1s = k.sb("g1s", [128, 16, 256], BF16); k.dma("pool", g1s.t[:], g1_d[:, :, :], [], [g1s.b])
    w2s = k.sb("w2s", [96, D], BF16); k.dma("pool", w2s.t[:], w2_d[:, :], [], [w2s.b])
    a2s = k.sb("a2s", [96, D], BF16); k.dma("pool", a2s.t[:], a2_d[:, :], [], [a2s.b])
    g2s = k.sb("g2s", [128, 2, D], BF16); k.dma("pool", g2s.t[:], g2_d[:, :, :], [], [g2s.b])
    ones1 = k.sb("ones1", [33, 128], BF16); k.memset("pool", ones1.t[:], 1.0, [ones1.b])
    xs = k.sb("xs", [128, D]); hb = k.sb("hb", [128, D])
    HI = k.sb("biasHI", [33, D], BF16); LO = k.sb("biasLO", [33, D], BF16)
    bias = {}
    for nm, d_, p_ in (("w0", w0_d, 0), ("a0", a0_d, 32)):
        k.dma("sp", xs.t[p_:p_ + 1, :], d_[:, :], [], [xs.b])
        k.copy("dve", HI.t[p_:p_ + 1, :], xs.t[p_:p_ + 1, :], [xs.b], [HI.b])
        k.tt("dve", LO.t[p_:p_ + 1, :], xs.t[p_:p_ + 1, :], HI.t[p_:p_ + 1, :], ALU.subtract, [xs.b, HI.b], [LO.b])
        bias[nm] = p_

    junk = hb
    ss = k.sb("ss", [128, 1]); rstd = k.sb("rstd", [128, 1])
    hT = k.sb("hT", [128, 16, HW], F32)
    xj = k.sb("xj", [128, 16, TG], BF16)
    tmp = k.sbn("tmp", [128, TG], F32, 2)
    twT = k.sb("twT", [96, TG], BF16); taT = k.sb("taT", [96, TG], BF16); tgT = k.sb("tgT", [128, 2, TG], BF16)
    Wb = k.sbn("Wb", [128, 16, 512], BF16, 2)
    big = k.ps("big", [128, D], F32)
    acc = [k.ps(f"acc{i}", [128, 512], F32) for i in range(4)]
    st = {n: k.sbn("st_" + n, [128, 512], F32, 2) for n in ("o", "a", "k", "kkn", "bb", "kp")}
    for n in ("kk", "sq", "am"):
        t1_ = k.sb("st_" + n, [128, 512], F32)
        st[n] = [t1_, t1_]
    sm = {n: k.sbn("sm_" + n, [128, 8], F32, 2) for n in ("ssq", "rn")}
    cnt = [0]

    def mix(j):
        for kc in range(16):
            t_ = tmp[kc % 2]
            k.act(t_.t[:, 0:TG], hT.t[:, kc, 127:127 + TG], AF.Copy, [hT.b, mu.b], [t_.b], scale=mu.t[:, j, kc:kc + 1])
            k.stt(xj.t[:, kc, :], hT.t[:, kc, 128:128 + TG], omu.t[:, j, kc:kc + 1], t_.t[:, 0:TG], ALU.mult, ALU.add,
                  [hT.b, omu.b, t_.b], [xj.b])

    def emit_out(name, row, cg, src):
        k.dma("sp", outs[name][row:row + 128, cg * 512:(cg + 1) * 512], src.t[:], [src.b], [], final=True)

    for g in range(NG):
        for i in range(NTG + 1):
            row = g * TG + i * 128
            k.dma("sp", xs.t[:], x1h[row:row + 128, :], [], [xs.b])
            rms_stats(k, xs.t[:], junk, ss, rstd, 1e-6, D, [xs.b])
            k.stt(hb.t[:], xs.t[:], rstd.t[:, 0:1], gain.t[:], ALU.mult, ALU.mult, [xs.b, rstd.b, gain.b], [hb.b])
            for kc in range(16):
                k.tr(big.t[:, kc * 128:(kc + 1) * 128], hb.t[:, kc * 128:(kc + 1) * 128], ident.t[:],
                     [hb.b, ident.b], [big.b])
            k.copy("act", hT.t[:, :, i * 128:(i + 1) * 128], big.t[:].rearrange("p (c t) -> p c t", c=16),
                   [big.b], [hT.b])
        mix(1)
        pa = acc[0]
        for kc in range(16):
            k.mm(pa.t[0:96, 0:TG], w1s.t[:, kc, :], xj.t[:, kc, :], kc == 0, kc == 15, [w1s.b, xj.b], [pa.b])
        k.act(twT.t[:, :], pa.t[0:96, 0:TG], AF.Tanh, [pa.b], [twT.b])
        mix(4)
        pa = acc[1]
        for kc in range(16):
            k.mm(pa.t[0:96, 0:TG], a1s.t[:, kc, :], xj.t[:, kc, :], kc == 0, kc == 15, [a1s.b, xj.b], [pa.b])
        k.copy("act", taT.t[:, :], pa.t[0:96, 0:TG], [pa.b], [taT.b])
        mix(5)
        for c in range(2):
            pa = acc[2 + c]
            for kc in range(16):
                k.mm(pa.t[:, 0:TG], g1s.t[:, kc, c * 128:(c + 1) * 128], xj.t[:, kc, :], kc == 0, kc == 15,
                     [g1s.b, xj.b], [pa.b])
            k.act(tgT.t[:, c, :], pa.t[:, 0:TG], AF.Sigmoid, [pa.b], [tgT.b])
        for i in range(NTG):
            row = g * TG + i * 128
            for cg in range(4):
                c_ = cnt[0]; cnt[0] += 1
                pa = acc[c_ % 4]
                cs = slice(cg * 512, (cg + 1) * 512)
                k.mm(pa.t[:], twT.t[:, i * 128:(i + 1) * 128], w2s.t[:, cs], True, False, [twT.b, w2s.b], [pa.b])
                k.mm(pa.t[:], ones1.t[0:1, :], HI.t[0:1, cs], False, False, [ones1.b, HI.b], [pa.b])
                k.mm(pa.t[:], ones1.t[0:1, :], LO.t[0:1, cs], False, True, [ones1.b, LO.b], [pa.b])
                o = st["o"][c_ % 2]
                k.act(o.t[:], pa.t[:], AF.Sigmoid, [pa.b], [o.b])
                k.ts("dve", o.t[:], o.t[:], LW_SCALE, None, ALU.mult, None, [o.b], [o.b])
                emit_out("lw", row, cg, o)
                c_ = cnt[0]; cnt[0] += 1
                pa = acc[c_ % 4]
                for c in range(2):
                    k.mm(pa.t[:], tgT.t[:, c, i * 128:(i + 1) * 128], g2s.t[:, c, cs], c == 0, c == 1,
                         [tgT.b, g2s.b], [pa.b])
                o = st["o"][c_ % 2]
                k.copy("act", o.t[:], pa.t[:], [pa.b], [o.b])
                emit_out("g", row, cg, o)
        for j, wd_, nm in ((0, wr_d, "r"), (3, wv_d, "v")):
            mix(j)
            for cg in range(4):
                wt = Wb[cg % 2]
                k.dma("pool", wt.t[:], wd_[cg], [], [wt.b])
                for i in range(NTG):
                    row = g * TG + i * 128
                    c_ = cnt[0]; cnt[0] += 1
                    pa = acc[c_ % 4]
                    for kc in range(16):
                        k.mm(pa.t[:], xj.t[:, kc, i * 128:(i + 1) * 128], wt.t[:, kc, :], kc == 0, kc == 15,
                             [xj.b, wt.b], [pa.b])
                    o = st["o"][c_ % 2]
                    k.copy("act" if c_ % 2 else "dve", o.t[:], pa.t[:], [pa.b], [o.b])
                    emit_out(nm, row, cg, o)
        mix(2)
        for cg in range(4):
            wt = Wb[cg % 2]
            k.dma("pool", wt.t[:], wk_d[cg], [], [wt.b])
            cs = slice(cg * 512, (cg + 1) * 512)
            for i in range(NTG):
                row = g * TG + i * 128
                c_ = cnt[0]; cnt[0] += 1
                p = c_ % 2
                pk = acc[(c_ % 2) * 2]; pA = acc[(c_ % 2) * 2 + 1]
                for kc in range(16):
                    k.mm(pk.t[:], xj.t[:, kc, i * 128:(i + 1) * 128], wt.t[:, kc, :], kc == 0, kc == 15,
                         [xj.b, wt.b], [pk.b])
                k.mm(pA.t[:], taT.t[:, i * 128:(i + 1) * 128], a2s.t[:, cs], True, False, [taT.b, a2s.b], [pA.b])
                k.mm(pA.t[:], ones1.t[32:33, :], HI.t[32:33, cs], False, False, [ones1.b, HI.b], [pA.b])
                k.mm(pA.t[:], ones1.t[32:33, :], LO.t[32:33, cs], False, True, [ones1.b, LO.b], [pA.b])
                a_ = st["a"][p]; k_ = st["k"][p]; kk_ = st["kk"][p]; sq_ = st["sq"][p]; kkn_ = st["kkn"][p]
                bb_ = st["bb"][p]; am_ = st["am"][p]; kp_ = st["kp"][p]; ssq = sm["ssq"][p]; rn = sm["rn"][p]
                k.act(a_.t[:], pA.t[:], AF.Sigmoid, [pA.b], [a_.b])
                k.copy("act", k_.t[:], pk.t[:], [pk.b], [k_.b])
                k.tt("dve", kk_.t[:], k_.t[:], kkrow.t[:, cs], ALU.mult, [k_.b, kkrow.b], [kk_.b])
                k.tt("dve", sq_.t[:], kk_.t[:], kk_.t[:], ALU.mult, [kk_.b], [sq_.b])
                k.reduce(ssq.t[:], sq_.t[:].rearrange("p (h c) -> p h c", h=8), ALU.add, [sq_.b], [ssq.b])
                k.act(rn.t[:], ssq.t[:], AF.Sqrt, [ssq.b], [rn.b])
                k.ts("dve", rn.t[:], rn.t[:], 1e-12, None, ALU.max, None, [rn.b], [rn.b])
                k.recip(rn.t[:], rn.t[:], [rn.b], [rn.b])
                k.tt("dve", kkn_.t[:].rearrange("p (h c) -> p h c", h=8), kk_.t[:].rearrange("p (h c) -> p h c", h=8),
                     rn.t[:].unsqueeze(2).to_broadcast([128, 8, 64]), ALU.mult, [kk_.b, rn.b], [kkn_.b])
                emit_out("kkn", row, cg, kkn_)
                k.tt("dve", bb_.t[:], kkn_.t[:], a_.t[:], ALU.mult, [kkn_.b, a_.b], [bb_.b])
                emit_out("bb", row, cg, bb_)
                k.stt(am_.t[:], a_.t[:], -1.0, karow.t[:, cs], ALU.add, ALU.mult, [a_.b, karow.b], [am_.b])
                k.stt(kp_.t[:], am_.t[:], 1.0, k_.t[:], ALU.add, ALU.mult, [am_.b, k_.b], [kp_.b])
                emit_out("kp", row, cg, kp_)
    return k.done()


def host_RPROJ_weights(mu, w_r, w_k, w_v, w1, a1, g1, w2, a2, g2):
    def big(w):
        return np.ascontiguousarray(w.reshape(16, 128, 4, 512).transpose(2, 1, 0, 3))
    def s1(w):
        return np.ascontiguousarray(w.reshape(16, 128, -1).transpose(1, 0, 2))
    mucol = np.ascontiguousarray(mu.reshape(6, 16, 128).transpose(2, 0, 1))
    g2w = np.ascontiguousarray(g2.reshape(2, 128, D).transpose(1, 0, 2))
    return dict(mucol=mucol, wr=big(w_r), wk=big(w_k), wv=big(w_v), w1=s1(w1), a1=s1(a1), g1w=s1(g1),
                w2=np.ascontiguousarray(w2), a2=np.ascontiguousarray(a2), g2w=g2w)


def build_RSCAN(NCH, NS=8, k=None, passes=None):
    W = NS * 64
    NBK = W // 256
    k = k or K()
    if passes is None:
        passes = [dict(din={n: k.din(n, [NCH, 128, W]) for n in ("r", "lw", "kp", "v", "kkn", "bb")},
                       y=k.dout("y", [NCH, 128, W]))]
    tri_d = k.din("tri", [128, 128])
    mst_d = k.din("m_strict_T", [128, 512])
    mit_d = k.din("m_incl_T", [128, 512])
    mlo_d = k.din("m_lo", [128, 512])

    ident = k.make_ident(F32, "identf")
    tri = k.sb("tri", [128, 128]); k.dma("sp", tri.t[:], tri_d[:, :], [], [tri.b])
    mst = k.sb("mst", [128, 512]); k.dma("sp", mst.t[:], mst_d[:, :], [], [mst.b])
    mit = k.sb("mit", [128, 512]); k.dma("sp", mit.t[:], mit_d[:, :], [], [mit.b])
    mlo = k.sb("mlo", [128, 512]); k.dma("sp", mlo.t[:], mlo_d[:, :], [], [mlo.b])
    onesq = k.sb("onesq", [128, 128]); k.memset("pool", onesq.t[:], 1.0, [onesq.b])

    inp = {n: k.sbn("i_" + n, [128, W], F32, 2) for n in ("r", "lw", "kp", "v", "kkn", "bb")}
    e = {n: k.sb("e_" + n, [128, W]) for n in ("cum", "t1", "t2", "G", "Gi", "Gp", "Gr", "AH", "BC", "KC", "RH", "BT", "KT",
                                                "W0", "U", "Y")}
    gC = k.sb("gC", [64, NS])
    ST = k.sb("ST", [64, W])
    STs = k.sb("STs", [64, W])
    tT = {n: k.sb("T_" + n, [64, NS, 128]) for n in ("AH", "BC", "KC", "RH")}
    A = {n: k.sb("A_" + n, [128, NS, 128]) for n in ("abT", "rbT", "akT", "rkT", "ab")}
    Xp = [k.sb(f"X{i}", [128, NS, 128]) for i in range(2)]
    XTp = [k.sb(f"XT{i}", [128, NS, 128]) for i in range(2)]
    TT = k.sb("TT", [128, NS, 128])
    pbig = [k.ps(f"pbig{i}", [128, NS * 128], F32) for i in range(3)]
    pD = k.ps("pD", [128, 512], F32)
    pE = k.ps("pE", [128, 512], F32)
    assert NS * 128 * 4 <= 4096 and W <= 512

    def hs(h):
        return slice(h * 64, (h + 1) * 64)

    def r32(ap):
        return ap.bitcast(F32R)
    Vr = k.sb("Vr", [128, W])

    for P_ in passes:
        k.memset("pool", ST.t[:], 0.0, [ST.b])
        for c in range(NCH):
            I = {n: inp[n][c % 2] for n in inp}
            for n in ("lw", "kkn", "bb", "kp", "r", "v"):
                k.dma("sp", I[n].t[:], P_["din"][n][c], [], [I[n].b])
            LW = I["lw"]
            k.mm(pD.t[:, 0:W], tri.t[:], LW.t[:], True, True, [tri.b, LW.b], [pD.b])
            k.mm(pE.t[:, 0:W], onesq.t[:], LW.t[:], True, True, [onesq.b, LW.b], [pE.b])
            pt = pbig[0]
            for h in range(NS):
                k.mm(pt.t[0:64, h:h + 1], LW.t[:, hs(h)], onesq.t[:, 0:1], True, True, [LW.b, onesq.b], [pt.b])
            k.act(gC.t[:], pt.t[0:64, 0:NS], AF.Exp, [pt.b], [gC.b])
            k.copy("act", e["cum"].t[:], pD.t[:, 0:W], [pD.b], [e["cum"].b])
            k.act(e["G"].t[:], pD.t[:, 0:W], AF.Exp, [pD.b], [e["G"].b])
            k.act(e["Gi"].t[:], pD.t[:, 0:W], AF.Exp, [pD.b], [e["Gi"].b], scale=-1.0)
            k.tt("dve", e["t1"].t[:], e["cum"].t[:], LW.t[:], ALU.subtract, [e["cum"].b, LW.b], [e["t1"].b])
            k.act(e["Gp"].t[:], e["t1"].t[:], AF.Exp, [e["t1"].b], [e["Gp"].b])
            k.tt("dve", e["t2"].t[:], pE.t[:, 0:W], e["cum"].t[:], ALU.subtract, [pE.b, e["cum"].b], [e["t2"].b])
            k.act(e["Gr"].t[:], e["t2"].t[:], AF.Exp, [e["t2"].b], [e["Gr"].b])
            k.stt(e["AH"].t[:], I["kkn"].t[:], -1.0, e["Gp"].t[:], ALU.mult, ALU.mult, [I["kkn"].b, e["Gp"].b], [e["AH"].b])
            k.tt("dve", e["BC"].t[:], I["bb"].t[:], e["Gi"].t[:], ALU.mult, [I["bb"].b, e["Gi"].b], [e["BC"].b])
            k.tt("dve", e["KC"].t[:], I["kp"].t[:], e["Gi"].t[:], ALU.mult, [I["kp"].b, e["Gi"].b], [e["KC"].b])
            k.tt("dve", e["RH"].t[:], I["r"].t[:], e["G"].t[:], ALU.mult, [I["r"].b, e["G"].b], [e["RH"].b])
            k.tt("dve", e["BT"].t[:], I["bb"].t[:], e["Gr"].t[:], ALU.mult, [I["bb"].b, e["Gr"].b], [e["BT"].b])
            k.tt("dve", e["KT"].t[:], I["kp"].t[:], e["Gr"].t[:], ALU.mult, [I["kp"].b, e["Gr"].b], [e["KT"].b])
            for qi, n in enumerate(("AH", "BC", "KC", "RH")):
                pt = pbig[qi % 3]
                for h in range(NS):
                    k.tr(pt.t[0:64, h * 128:(h + 1) * 128], e[n].t[:, hs(h)], ident.t[:], [e[n].b, ident.b], [pt.b])
                k.copy("act", r32(tT[n].t[:]), pt.t[0:64, :].rearrange("p (h t) -> p h t", h=NS), [pt.b], [tT[n].b])
            specs = (("abT", "BC", "AH", mst), ("akT", "KC", "AH", mst), ("rbT", "BC", "RH", mit),
                     ("rkT", "KC", "RH", mit), ("ab", "AH", "BC", mlo))
            for qi, (an, ln, rn, msk) in enumerate(specs):
                pt = pbig[(qi + 1) % 3]
                for h in range(NS):
                    k.mm(pt.t[:, h * 128:(h + 1) * 128], r32(tT[ln].t[:, h, :]), r32(tT[rn].t[:, h, :]), True, True,
                         [tT[ln].b, tT[rn].b], [pt.b])
                for bk in range(NBK):
                    k.tt("dve", r32(A[an].t[:, bk * 4:(bk + 1) * 4, :]),
                         pt.t[:, bk * 512:(bk + 1) * 512].rearrange("p (h t) -> p h t", h=4),
                         msk.t[:].rearrange("p (h t) -> p h t", h=4), ALU.mult, [pt.b, msk.b], [A[an].b])
            k.tt("dve", r32(TT.t[:]), A["abT"].t[:], ident.t[:].unsqueeze(1).to_broadcast([128, NS, 128]), ALU.add,
                 [A["abT"].b, ident.b], [TT.b])
            X, XT = A["ab"], A["abT"]
            for step in range(6):
                Xn, XTn = Xp[step % 2], XTp[step % 2]
                pX, pXT, pT = pbig[0], pbig[1], pbig[2]
                for h in range(NS):
                    k.mm(pX.t[:, h * 128:(h + 1) * 128], r32(XT.t[:, h, :]), r32(X.t[:, h, :]), True, True, [XT.b, X.b], [pX.b])
                k.copy("act", r32(Xn.t[:]), pX.t[:].rearrange("p (h t) -> p h t", h=NS), [pX.b], [Xn.b])
                if step < 5:
                    for h in range(NS):
                        k.mm(pXT.t[:, h * 128:(h + 1) * 128], r32(X.t[:, h, :]), r32(XT.t[:, h, :]), True, True, [XT.b, X.b], [pXT.b])
                    k.copy("act", r32(XTn.t[:]), pXT.t[:].rearrange("p (h t) -> p h t", h=NS), [pXT.b], [XTn.b])
                for h in range(NS):
                    k.mm(pT.t[:, h * 128:(h + 1) * 128], r32(Xn.t[:, h, :]), r32(TT.t[:, h, :]), True, True, [Xn.b, TT.b], [pT.b])
                k.tt("dve", r32(TT.t[:]), TT.t[:], pT.t[:].rearrange("p (h t) -> p h t", h=NS), ALU.add, [TT.b, pT.b], [TT.b])
                X, XT = Xn, XTn
            k.copy("dve", r32(Vr.t[:]), I["v"].t[:], [I["v"].b], [Vr.b])
            V = Vr
            for h in range(NS):
                k.mm(pD.t[:, hs(h)], r32(A["akT"].t[:, h, :]), r32(V.t[:, hs(h)]), True, False, [A["akT"].b, V.b], [pD.b])
                k.mm(pD.t[:, hs(h)], r32(tT["AH"].t[:, h, :]), r32(ST.t[:, hs(h)]), False, True, [tT["AH"].b, ST.b], [pD.b])
            k.copy("act", r32(e["W0"].t[:]), pD.t[:, 0:W], [pD.b], [e["W0"].b])
            for h in range(NS):
                k.mm(pE.t[:, hs(h)], r32(TT.t[:, h, :]), r32(e["W0"].t[:, hs(h)]), True, True, [TT.b, e["W0"].b], [pE.b])
            k.copy("act", r32(e["U"].t[:]), pE.t[:, 0:W], [pE.b], [e["U"].b])
            for h in range(NS):
                k.mm(pD.t[:, hs(h)], r32(tT["RH"].t[:, h, :]), r32(ST.t[:, hs(h)]), True, False, [tT["RH"].b, ST.b], [pD.b])
                k.mm(pD.t[:, hs(h)], r32(A["rbT"].t[:, h, :]), r32(e["U"].t[:, hs(h)]), False, False, [A["rbT"].b, e["U"].b], [pD.b])
                k.mm(pD.t[:, hs(h)], r32(A["rkT"].t[:, h, :]), r32(V.t[:, hs(h)]), False, True, [A["rkT"].b, V.b], [pD.b])
            k.copy("act", e["Y"].t[:], pD.t[:, 0:W], [pD.b], [e["Y"].b])
            k.dma("sp", P_["y"][c], e["Y"].t[:], [e["Y"].b], [], final=True)
            for h in range(NS):
                k.mm(pE.t[0:64, hs(h)], e["BT"].t[:, hs(h)], e["U"].t[:, hs(h)], True, False, [e["BT"].b, e["U"].b], [pE.b])
                k.mm(pE.t[0:64, hs(h)], e["KT"].t[:, hs(h)], I["v"].t[:, hs(h)], False, True, [e["KT"].b, I["v"].b], [pE.b])
            k.tt("dve", STs.t[:].rearrange("p (h c) -> p h c", h=NS), ST.t[:].rearrange("p (h c) -> p h c", h=NS),
                 gC.t[:].unsqueeze(2).to_broadcast([64, NS, 64]), ALU.mult, [ST.b, gC.b], [STs.b])
            k.tt("dve", r32(ST.t[:]), STs.t[:], pE.t[0:64, 0:W], ALU.add, [STs.b, pE.b], [ST.b])
    return k.done()


def host_RSCAN_consts():
    s = np.arange(128)[:, None]
    t = np.arange(128)[None, :]
    tri = (s <= t).astype(np.float32)
    mst = np.tile((s < t).astype(np.float32), (1, 4))
    mit = np.tile((s <= t).astype(np.float32), (1, 4))
    mlo = np.tile((t < s).astype(np.float32), (1, 4))
    return dict(tri=tri, m_strict_T=mst, m_incl_T=mit, m_lo=mlo)


GN_EPS = 64e-5


def build_OUT1(T, k=None, gather=None):
    NT = T // 128
    k = k or K()
    x1 = k.din("x1", [T, D])
    dins = {n: k.din(n, [T, D]) for n in ("y", "r", "kp", "v", "g")}
    lnw_d = k.din("ln_w", [1, D]); lnb_d = k.din("ln_b", [1, D]); rk_d = k.din("r_k", [1, D]); g1 = k.din("g1", [1, D])
    wout = k.din("wout", [4, 128, 16, 512])
    x2 = k.dout("x2", [T, D])

    ident = k.make_ident(F32, "identf")
    rows = {}
    for n, d_ in (("lnw", lnw_d), ("lnb", lnb_d), ("rk", rk_d), ("gain1", g1)):
        rows[n] = k.sb("row_" + n, [128, D]); k.dma("sp", rows[n].t[:], d_.partition_broadcast(128), [], [rows[n].b])
    wo = [k.sb(f"wo{cg}", [128, 16, 512], BF16) for cg in range(4)]
    for cg in range(4):
        k.dma("pool", wo[cg].t[:], wout[cg], [], [wo[cg].b])
    tl = {n: k.sb("t_" + n, [128, D]) for n in ("y", "r", "kp", "v", "g", "sq")}
    s32 = {n: k.sb("s_" + n, [128, 32]) for n in ("mean", "var", "s")}
    mixT = k.sbn("mixT", [128, 16, 128], BF16, 2)
    big = k.ps("big", [128, D], F32)
    xs = k.sb("xs", [128, D]); hb = k.sb("hb", [128, D]); junk = k.sb("junk", [128, D], BF16)
    ss = k.sb("ss", [128, 1]); rstd = k.sb("rstd", [128, 1])

    def v3(t_):
        return t_.t[:].rearrange("p (h c) -> p h c", h=32)

    def b3(t_):
        return t_.t[:].unsqueeze(2).to_broadcast([128, 32, 64])

    if gather is not None:
        gidx = k.sb("gidx", [128, NT], mybir.dt.int32)
        k.dma("sp", gidx.t[:], gather["idx"], [], [gidx.b])

        def gload(dst, src, i, eoff=0):
            k.S.op("pool", lambda e: e.indirect_dma_start(
                out=dst.t[:], out_offset=None, in_=src,
                in_offset=bass.IndirectOffsetOnAxis(ap=gidx.t[:, i:i + 1], axis=0), element_offset=eoff),
                [gidx.b], [dst.b], dma=True)
    for i in range(NT):
        row = i * 128
        for n in ("y", "r", "kp", "v", "g"):
            if gather is None:
                k.dma("sp", tl[n].t[:], dins[n][row:row + 128, :], [], [tl[n].b])
            else:
                gload(tl[n], dins[n], i)
        Y, R, KP, V, G, SQ = (tl[n] for n in ("y", "r", "kp", "v", "g", "sq"))
        k.reduce(s32["mean"].t[:], v3(Y), ALU.add, [Y.b], [s32["mean"].b])
        k.ts("dve", s32["mean"].t[:], s32["mean"].t[:], 1.0 / 64, None, ALU.mult, None, [s32["mean"].b], [s32["mean"].b])
        k.tt("dve", v3(Y), v3(Y), b3(s32["mean"]), ALU.subtract, [Y.b, s32["mean"].b], [Y.b])
        k.tt("dve", SQ.t[:], Y.t[:], Y.t[:], ALU.mult, [Y.b], [SQ.b])
        k.reduce(s32["var"].t[:], v3(SQ), ALU.add, [SQ.b], [s32["var"].b])
        k.act(s32["var"].t[:], s32["var"].t[:], AF.Sqrt, [s32["var"].b], [s32["var"].b], scale=1.0 / 64,
              bias=k.eps_tile(GN_EPS))
        k.recip(s32["var"].t[:], s32["var"].t[:], [s32["var"].b], [s32["var"].b])
        k.tt("dve", v3(Y), v3(Y), b3(s32["var"]), ALU.mult, [Y.b, s32["var"].b], [Y.b])
        k.tt("dve", Y.t[:], Y.t[:], rows["lnw"].t[:], ALU.mult, [Y.b, rows["lnw"].b], [Y.b])
        k.tt("dve", Y.t[:], Y.t[:], rows["lnb"].t[:], ALU.add, [Y.b, rows["lnb"].b], [Y.b])
        k.tt("dve", R.t[:], R.t[:], KP.t[:], ALU.mult, [R.b, KP.b], [R.b])
        k.tt("dve", R.t[:], R.t[:], rows["rk"].t[:], ALU.mult, [R.b, rows["rk"].b], [R.b])
        k.reduce(s32["s"].t[:], v3(R), ALU.add, [R.b], [s32["s"].b])
        k.tt("dve", v3(V), v3(V), b3(s32["s"]), ALU.mult, [V.b, s32["s"].b], [V.b])
        k.tt("dve", Y.t[:], Y.t[:], V.t[:], ALU.add, [Y.b, V.b], [Y.b])
        k.tt("dve", Y.t[:], Y.t[:], G.t[:], ALU.mult, [Y.b, G.b], [Y.b])
        mx = mixT[i % 2]
        for kc in range(16):
            k.tr(big.t[:, kc * 128:(kc + 1) * 128], Y.t[:, kc * 128:(kc + 1) * 128], ident.t[:], [Y.b, ident.b], [big.b])
        k.copy("act", mx.t[:], big.t[:].rearrange("p (c t) -> p c t", c=16), [big.b], [mx.b])
        xl = None if gather is None else (lambda xt, i=i: gload(xt, x1, i, gather["x1_eoff"]))
        out_proj_tile(k, mx, [mx.b], 0, wo, big, xs, hb, junk, ss, rstd, rows["gain1"], x1, x2, row, xload=xl)
    return k.done()


def build_MEGA(SEQ):
    T = SEQ
    NQT = SEQ // 512
    ND = 4 * NQT + 3
    NCH = SEQ // 128
    NF = DFF // 128
    k = K()
    k.fused = True
    E = {}

    def ein(name, shape, dt=F32):
        E[name] = k.ext_in(name, shape, dt)
        return E[name]

    x = ein("x", [T, D])
    gains = ein("gains", [8, D])
    ein("wA", [24, 128, 16, 128]); ein("wV", [2, 128, 16, 512])
    ein("qaug8", [8, 4, SEQ], BF16); ein("kaug8", [8, 4, SEQ], BF16); ein("biastab8", [8, 128, ND])
    ein("att_tri", [128, 128], BF16); ein("lq", [1, 256]); ein("subln", [1, 128])
    ein("lre8", [8, 128, 4]); ein("lim8", [8, 128, 4]); ein("ldt8", [8, 128, 4])
    ein("bre8", [8, 4, 128, 16]); ein("bim8", [8, 4, 128, 16])
    ein("ctre8", [8, 4, 128, 128]); ein("ctim8", [8, 4, 128, 128]); ein("dcol8", [8, 128, 1])
    ein("wglu", [8, 128, 8, 128]); ein("wout0", [4, 128, 16, 512])
    for l in range(2):
        ein(f"wg{l}", [NF, 128, 16, 128]); ein(f"wu{l}", [NF, 128, 16, 128]); ein(f"wd{l}", [16, 128, NF, 128])
    ein("mucol", [128, 6, 16])
    for n in ("wr", "wk", "wv"):
        ein(n, [4, 128, 16, 512])
    ein("w1", [128, 16, 96]); ein("a1", [128, 16, 96]); ein("g1w", [128, 16, 256])
    ein("w2", [96, D]); ein("a2", [96, D]); ein("g2w", [128, 2, D])
    for n in ("w0", "a0", "k_k", "k_a", "ln_w", "ln_b", "r_k"):
        ein(n, [1, D])
    ein("sc_tri", [128, 128]); ein("m_strict_T", [128, 512]); ein("m_incl_T", [128, 512]); ein("m_lo", [128, 512])
    ein("wout1", [4, 128, 16, 512])
    TO = T // 4
    out = k.ext_out("out", [TO, D])
    ein("own_idx", [128, TO // 128], mybir.dt.int32)

    qkT = k.dint("i_qkT", [16, 128, T], BF16)
    uT = k.dint("i_uT", [8, 128, T], F32)
    vint = k.dint("i_v", [T, 1024], BF16)
    attn = k.dint("i_attn", [T, 1024], BF16)
    ygT = k.dint("i_ygT", [8, 128, T], BF16)
    x1 = k.dint("i_x1", [T, D])
    x2h = k.dint("i_x2h", [T + 128, D])
    R = {n: k.dint("i_r_" + n, [T, D]) for n in ("r", "lw", "kp", "v", "kkn", "bb", "g")}
    yint = k.dint("i_y", [T, D])
    x3 = k.dint("i_x3", [TO, D])

    def g(i):
        return gains[i:i + 1, :]

    k.begin_phase("a_", dict(x=x, g0=g(0), wA=E["wA"], wV=E["wV"], qkT=qkT, uT=uT, v=vint))
    build_L1(T, k=k)
    k.begin_phase("b_", dict(tri=E["att_tri"], lq=E["lq"], subln=E["subln"]))
    items = [dict(q=qkT[h].rearrange("(m p) t -> m p t", m=2), k=qkT[8 + h].rearrange("(m p) t -> m p t", m=2),
                  v=vint[:, h * 128:(h + 1) * 128], qaug=E["qaug8"][h], kaug=E["kaug8"][h], bt=E["biastab8"][h],
                  out=attn[:, h * 128:(h + 1) * 128]) for h in range(8)]
    build_ATT(SEQ, 1, k=k, items=items)
    k.begin_phase("c_", {})
    blocks = [dict(uT=[uT[cb]], out=[ygT[cb]], lre=E["lre8"][cb], lim=E["lim8"][cb], ldt=E["ldt8"][cb],
                   bre=E["bre8"][cb], bim=E["bim8"][cb], ctre=E["ctre8"][cb], ctim=E["ctim8"][cb],
                   dcol=E["dcol8"][cb]) for cb in range(8)]
    build_S5(SEQ, 1, k=k, blocks=blocks)
    k.begin_phase("d_", dict(x=x, attn=attn, ygT=ygT, wglu=E["wglu"], wout=E["wout0"], g1=g(1), x1=x1))
    build_OUT0(T, k=k)
    k.begin_phase("e_", dict(x1=x1, g2=g(2), g3=g(3), wg=E["wg0"], wu=E["wu0"], wd=E["wd0"], x2=x2h[128:128 + T, :]))
    zt = k.sb("zt", [128, D])
    k.memset("pool", zt.t[:], 0.0, [zt.b])
    k.dma("sp", x2h[0:128, :], zt.t[:], [zt.b], [], final=True)
    build_FFN(T, k=k)
    io = dict(x1h=x2h, g0=g(4))
    for n in ("mucol", "wr", "wk", "wv", "w1", "a1", "g1w", "w2", "a2", "g2w", "w0", "a0", "k_k", "k_a"):
        io[n] = E[n]
    io.update(R)
    k.begin_phase("f_", io)
    build_RPROJ(T, k=k)
    k.begin_phase("g_", dict(tri=E["sc_tri"], m_strict_T=E["m_strict_T"], m_incl_T=E["m_incl_T"], m_lo=E["m_lo"]))

    def cv(ap, p):
        return ap.rearrange("(c t) d -> c t d", t=128)[:, :, p * 512:(p + 1) * 512]
    passes = [dict(din={n: cv(R[n], p) for n in ("r", "lw", "kp", "v", "kkn", "bb")}, y=cv(yint, p)) for p in range(4)]
    build_RSCAN(NCH, 8, k=k, passes=passes)
    k.begin_phase("h_", dict(x1=x2h, y=yint, r=R["r"], kp=R["kp"], v=R["v"], g=R["g"], ln_w=E["ln_w"],
                             ln_b=E["ln_b"], r_k=E["r_k"], g1=g(5), wout=E["wout1"], x2=x3))
    build_OUT1(TO, k=k, gather=dict(idx=E["own_idx"], x1_eoff=128 * D))
    k.begin_phase("i_", dict(x1=x3, g2=g(6), g3=g(7), wg=E["wg1"], wu=E["wu1"], wd=E["wd1"], x2=out))
    build_FFN(TO, k=k)
    return k.finish_fused()


_CACHE = {}


def _c(a):
    return np.ascontiguousarray(a)


def kernel(x, norm_gains, ev_w_in, ev_lambda_qk, ev_attn_subln, ev_s5_lambda_re,
           ev_s5_lambda_im, ev_s5_log_dt, ev_s5_b_re, ev_s5_b_im, ev_s5_c_re, ev_s5_c_im,
           ev_s5_d, ev_s5_w_glu, ev_w_out, od_mu, od_w_r, od_w_k, od_w_v, od_w0, od_w1,
           od_w2, od_a0, od_a1, od_a2, od_g1, od_g2, od_k_k, od_k_a, od_r_k, od_ln_w,
           od_ln_b, od_w_o, ffn_w_gate, ffn_w_up, ffn_w_down):
    f32 = lambda a: np.asarray(a, dtype=np.float32)
    x = f32(x)
    B, SEQ, _ = x.shape
    if SEQ not in _CACHE:
        _CACHE[SEQ] = build_MEGA(SEQ)
    nc = _CACHE[SEQ]
    W = {}
    W["gains"] = _c(f32(norm_gains).reshape(8, D))
    W["wA"], W["wV"] = host_L1_weights(f32(ev_w_in)[0])
    cons = [host_ATT_consts(h, SEQ) for h in range(8)]
    W["qaug8"] = np.stack([c[0] for c in cons]); W["kaug8"] = np.stack([c[1] for c in cons])
    W["biastab8"] = np.stack([c[2] for c in cons]); W["att_tri"] = cons[0][3]
    W["lq"] = _c(f32(ev_lambda_qk)[0].reshape(1, 256)); W["subln"] = _c(f32(ev_attn_subln)[0].reshape(1, 128))
    s5 = [host_S5_params(c, f32(ev_s5_lambda_re)[0], f32(ev_s5_lambda_im)[0], f32(ev_s5_log_dt)[0],
                         f32(ev_s5_b_re)[0], f32(ev_s5_b_im)[0], f32(ev_s5_c_re)[0], f32(ev_s5_c_im)[0],
                         f32(ev_s5_d)[0]) for c in range(8)]
    for n in ("lre", "lim", "ldt", "bre", "bim", "ctre", "ctim", "dcol"):
        W[n + "8"] = np.stack([d[n] for d in s5])
    W["wglu"], W["wout0"] = host_OUT_weights(f32(ev_s5_w_glu)[0], f32(ev_w_out)[0])
    for l in range(2):
        W[f"wg{l}"], W[f"wu{l}"], W[f"wd{l}"] = host_FFN_weights(f32(ffn_w_gate)[l], f32(ffn_w_up)[l], f32(ffn_w_down)[l])
    W.update(host_RPROJ_weights(f32(od_mu)[0], f32(od_w_r)[0], f32(od_w_k)[0], f32(od_w_v)[0], f32(od_w1)[0],
                                f32(od_a1)[0], f32(od_g1)[0], f32(od_w2)[0], f32(od_a2)[0], f32(od_g2)[0]))
    for n, a in (("w0", od_w0), ("a0", od_a0), ("k_k", od_k_k), ("k_a", od_k_a), ("ln_w", od_ln_w), ("ln_b", od_ln_b),
                 ("r_k", od_r_k)):
        W[n] = _c(f32(a)[0].reshape(1, D))
    rc = host_RSCAN_consts()
    W["sc_tri"] = rc["tri"]; W["m_strict_T"] = rc["m_strict_T"]; W["m_incl_T"] = rc["m_incl_T"]; W["m_lo"] = rc["m_lo"]
    _, W["wout1"] = host_OUT_weights(None, f32(od_w_o)[0])
    CPB = NCORES // B
    in_maps = []
    TO = SEQ // CPB
    for c in range(NCORES):
        d = dict(W)
        d["x"] = _c(x[c // CPB])
        rows = (c % CPB) * TO + np.arange(TO, dtype=np.int32)
        d["own_idx"] = _c(rows.reshape(TO // 128, 128).T)
        in_maps.append(d)
    res = run(nc, in_maps)
    return np.concatenate([res[c]["out"] for c in range(NCORES)], 0).reshape(B, SEQ, D)
```

```python
import math
from contextlib import ExitStack
import numpy as np
import ml_dtypes
import concourse.bass as bass
import concourse.mybir as mybir
from concourse.bass_utils import run_bass_kernel_spmd

F32 = mybir.dt.float32
BF16 = mybir.dt.bfloat16
F32R = mybir.dt.float32r
AF = mybir.ActivationFunctionType
ALU = mybir.AluOpType
AX = mybir.AxisListType
NPBF = ml_dtypes.bfloat16

D = 2048
NCORES = 8
DFF = 5632
NDMA_SEM = 8


class Buf:
    __slots__ = ("name", "w", "r")

    def __init__(self, name):
        self.name = name
        self.w = None
        self.r = []


class Tile:
    __slots__ = ("t", "b")

    def __init__(self, t, b):
        self.t = t
        self.b = b


class Sched:
    def __init__(self, nc):
        self.nc = nc
        self.q = {e: [] for e in ("pe", "dve", "act", "pool", "sp")}
        self.sems = {}
        self.val = {}
        for e in ("pe", "dve", "act", "pool"):
            self._mksem(e)
        for i in range(NDMA_SEM):
            self._mksem(("spd", i))
            self._mksem(("poold", i))
            self._mksem(("actd", i))
        self.dma_rr = {"sp": 0, "pool": 0, "act": 0}
        self.waited = {e: {} for e in self.q}

    def _mksem(self, key):
        name = "s_" + (key if isinstance(key, str) else f"{key[0]}{key[1]}")
        self.sems[key] = self.nc.alloc_semaphore(name=name)
        self.val[key] = 0

    def _deps(self, eng, reads, writes):
        need = {}

        def add(dep):
            if dep is None:
                return
            deng, key, v = dep
            if deng == "pe" and eng == "pe" and key == "pe":
                return
            if need.get(key, 0) < v:
                need[key] = v

        for b in reads:
            add(b.w)
        for b in writes:
            add(b.w)
            for r in b.r:
                add(r)
        out = []
        for key, v in need.items():
            if self.waited[eng].get(key, 0) >= v:
                continue
            self.waited[eng][key] = v
            out.append((key, v))
        return out

    def op(self, eng, fn, reads=(), writes=(), dma=False):
        reads = [b for b in reads if b is not None]
        writes = [b for b in writes if b is not None]
        waits = self._deps(eng, reads, writes)
        if dma:
            qn = {"sp": "spd", "pool": "poold", "act": "actd"}[eng]
            i = self.dma_rr[eng]
            self.dma_rr[eng] = (i + 1) % NDMA_SEM
            key = (qn, i)
            inc = 16
        else:
            key = eng
            inc = 1
        self.val[key] += inc
        v = self.val[key]
        sem = self.sems[key]
        wl = [(self.sems[k], vv) for k, vv in waits]

        def emit(e):
            for s, vv in wl:
                e.wait_ge(s, vv)
            fn(e).then_inc(sem, inc)

        self.q[eng].append(emit)
        tag = (eng, key, v)
        for b in reads:
            b.r.append(tag)
        for b in writes:
            b.w = tag
            b.r = []
        return tag

    def barrier(self):
        snap = [(k_, self.sems[k_], v) for k_, v in self.val.items() if v > 0]
        for eng in self.q:
            wl = []
            for k_, sem, v in snap:
                if self.waited[eng].get(k_, 0) >= v:
                    continue
                self.waited[eng][k_] = v
                wl.append((sem, v))

            def emit(e, wl=wl):
                for sm, v in wl:
                    e.wait_ge(sm, v)
            self.q[eng].append(emit)

    def finish(self, final_bufs):
        need = {}
        for b in final_bufs:
            if b.w is not None:
                _, key, v = b.w
                need[key] = max(need.get(key, 0), v)
        wl = [(self.sems[k], v) for k, v in need.items()]

        def emit(e):
            for s, v in wl:
                e.wait_ge(s, v)
        self.q["sp"].append(emit)

    def emit_all(self):
        nc = self.nc
        q = self.q
        with nc.Block() as block:
            @block.tensor
            def _(e):
                for f in q["pe"]:
                    f(e)

            @block.vector
            def _(e):
                for f in q["dve"]:
                    f(e)

            @block.scalar
            def _(e):
                for f in q["act"]:
                    f(e)

            @block.gpsimd
            def _(e):
                for f in q["pool"]:
                    f(e)

            @block.sync
            def _(e):
                for f in q["sp"]:
                    f(e)


class K:
    def __init__(self, name="k"):
        self.nc = bass.Bass("TRN2", target_bir_lowering=False)
        self.es = ExitStack()
        self.S = Sched(self.nc)
        self.outs = []
        self.n = 0
        self.fused = False
        self.io = {}
        self.prefix = ""
        self.tiles = {}
        self._eps = {}

    def ext_in(self, name, shape, dt=F32):
        return self.nc.dram_tensor(name, list(shape), dt, kind="ExternalInput").ap()

    def ext_out(self, name, shape, dt=F32):
        return self.nc.dram_tensor(name, list(shape), dt, kind="ExternalOutput").ap()

    def dint(self, name, shape, dt=F32):
        return self.nc.dram_tensor(name, list(shape), dt).ap()

    def din(self, name, shape, dt=F32):
        if name in self.io:
            return self.io[name]
        assert not self.fused, name
        return self.ext_in(name, shape, dt)

    def dout(self, name, shape, dt=F32):
        if name in self.io:
            return self.io[name]
        assert not self.fused, name
        return self.ext_out(name, shape, dt)

    def begin_phase(self, prefix, io):
        self.prefix = prefix
        self.io = io
        self.tiles = {}
        self._eps = {}
        self.es = ExitStack()

    def end_phase(self):
        self.S.barrier()
        self.es.close()

    def buf(self, name=None):
        self.n += 1
        return Buf(name or f"b{self.n}")

    def cbuf(self, name):
        key = ("b", name)
        if key not in self.tiles:
            self.tiles[key] = self.buf(name)
        return self.tiles[key]

    def sb(self, name, shape, dt=F32):
        key = ("s", name)
        if key in self.tiles:
            return self.tiles[key]
        t = self.es.enter_context(self.nc.sbuf_tensor("s_" + self.prefix + name, list(shape), dt))
        self.tiles[key] = Tile(t, self.buf(name))
        return self.tiles[key]

    def sbn(self, name, shape, dt, n):
        return [self.sb(f"{name}{i}", shape, dt) for i in range(n)]

    def ps(self, name, shape, dt=F32):
        key = ("p", name)
        if key in self.tiles:
            return self.tiles[key]
        t = self.es.enter_context(self.nc.psum_tensor("p_" + self.prefix + name, list(shape), dt))
        self.tiles[key] = Tile(t, self.buf(name))
        return self.tiles[key]

    def dma(self, eng, out, in_, reads=(), writes=(), final=False):
        tag_b = None
        if final:
            tag_b = self.buf("out")
            self.outs.append(tag_b)
            writes = list(writes) + [tag_b]
        self.S.op(eng, lambda e: e.dma_start(out=out, in_=in_), reads, writes, dma=True)

    def mm(self, out, lhsT, rhs, start, stop, reads, writes, **kw):
        self.S.op("pe", lambda e: e.matmul(out, lhsT=lhsT, rhs=rhs, start=start, stop=stop, **kw),
                  reads, writes)

    def tr(self, out, in_, ident, reads, writes):
        self.S.op("pe", lambda e: e.transpose(out=out, in_=in_, identity=ident), reads, writes)

    def act(self, out, in_, func, reads, writes, **kw):
        self.S.op("act", lambda e: e.activation(out=out, in_=in_, func=func, **kw), reads, writes)

    def tt(self, eng, out, in0, in1, op, reads, writes):
        self.S.op(eng, lambda e: e.tensor_tensor(out=out, in0=in0, in1=in1, op=op), reads, writes)

    def ts(self, eng, out, in0, s1, s2, op0, op1, reads, writes, **kw):
        if op1 is None:
            self.S.op(eng, lambda e: e.tensor_scalar(out=out, in0=in0, scalar1=s1, scalar2=None,
                                                     op0=op0, **kw), reads, writes)
        else:
            self.S.op(eng, lambda e: e.tensor_scalar(out=out, in0=in0, scalar1=s1, scalar2=s2,
                                                     op0=op0, op1=op1, **kw), reads, writes)

    def stt(self, out, in0, scalar, in1, op0, op1, reads, writes):
        self.S.op("dve", lambda e: e.scalar_tensor_tensor(out=out, in0=in0, scalar=scalar, in1=in1,
                                                          op0=op0, op1=op1), reads, writes)

    def copy(self, eng, out, in_, reads, writes):
        if eng == "act":
            self.S.op("act", lambda e: e.copy(out=out, in_=in_), reads, writes)
        else:
            self.S.op(eng, lambda e: e.tensor_copy(out=out, in_=in_), reads, writes)

    def memset(self, eng, ap, val, writes):
        self.S.op(eng, lambda e: e.memset(ap, val), (), writes)

    def recip(self, out, in_, reads, writes):
        self.S.op("dve", lambda e: e.reciprocal(out=out, in_=in_), reads, writes)

    def reduce(self, out, in_, op, reads, writes, axis=AX.X):
        self.S.op("dve", lambda e: e.tensor_reduce(out=out, in_=in_, axis=axis, op=op), reads, writes)

    def scan(self, out, d0, d1, init, reads, writes):
        self.S.op("dve", lambda e: e.tensor_tensor_scan(out=out, data0=d0, data1=d1, initial=init,
                                                        op0=ALU.mult, op1=ALU.add), reads, writes)

    def make_ident(self, dt=F32, name="ident"):
        idt = self.sb(name, [128, 128], dt)
        self.memset("pool", idt.t[:], 1.0, [idt.b])
        self.S.op("pool", lambda e: e.affine_select(out=idt.t[:], in_=idt.t[:], pattern=[[1, 128]],
                                                    compare_op=ALU.is_equal, fill=0.0, base=0,
                                                    channel_multiplier=-1), [idt.b], [idt.b])
        return idt

    def done(self):
        if self.fused:
            self.end_phase()
            return None
        self.S.finish(self.outs)
        self.S.emit_all()
        self.es.close()
        return self.nc

    def finish_fused(self):
        self.S.finish(self.outs)
        self.S.emit_all()
        return self.nc


def run(nc, in_maps):
    res = run_bass_kernel_spmd(nc, in_maps, core_ids=list(range(len(in_maps))))
    return res.results


def rms_stats(k, src_ap, junk, ss, rstd, eps, n, reads, tmpname=""):
    k.act(junk.t[:, 0:n], src_ap, AF.Square, reads, [junk.b, ss.b], accum_out=ss.t[:, 0:1])
    k.act(rstd.t[:, 0:1], ss.t[:, 0:1], AF.Sqrt, [ss.b], [rstd.b], scale=1.0 / n, bias=k.eps_tile(eps))
    k.recip(rstd.t[:, 0:1], rstd.t[:, 0:1], [rstd.b], [rstd.b])


def rms_stats_dve(k, src_ap, junk_ap, junk_b, ss, rstd, eps, n, reads):
    k.S.op("dve", lambda e: e.scalar_tensor_tensor(out=junk_ap, in0=src_ap, scalar=1.0, in1=src_ap, op0=ALU.mult,
                                                   op1=ALU.mult, accum_out=ss.t[:, 0:1]), reads, [junk_b, ss.b])
    k.act(rstd.t[:, 0:1], ss.t[:, 0:1], AF.Ln, [ss.b], [rstd.b], scale=1.0 / n, bias=k.eps_tile(eps))
    k.act(rstd.t[:, 0:1], rstd.t[:, 0:1], AF.Exp, [rstd.b], [rstd.b], scale=-0.5)


def _eps_tile(self, eps):
    if eps not in self._eps:
        t = self.sb(f"eps{len(self._eps)}", [128, 1], F32)
        self.memset("pool", t.t[:], float(eps), [t.b])
        self._eps[eps] = t
    return self._eps[eps].t[:, 0:1]


K.eps_tile = _eps_tile


def norm_transpose_tile(k, x_ap, xb, gain, hb, junk, ss, rstd, ptr, hT_ap_fn, ident, reads_x, hT_buf,
                        eps=1e-6, evac_eng="act"):
    rms_stats(k, x_ap, junk, ss, rstd, eps, D, reads_x)
    k.stt(hb.t[:], x_ap, rstd.t[:, 0:1], gain.t[:], ALU.mult, ALU.mult, reads_x + [rstd.b, gain.b], [hb.b])
    for kc in range(16):
        k.tr(ptr.t[:, kc * 128:(kc + 1) * 128], hb.t[:, kc * 128:(kc + 1) * 128], ident.t[:],
             [hb.b, ident.b], [ptr.b])
    k.copy(evac_eng, hT_ap_fn(), ptr.t[:].rearrange("p (c t) -> p c t", c=16), [ptr.b], [hT_buf])


def build_L1(T, k=None):
    NT = T // 128
    TG = min(512, T)
    NG = T // TG
    k = k or K()
    x = k.din("x", [T, D])
    g0 = k.din("g0", [1, D])
    wA = k.din("wA", [24, 128, 16, 128])
    wV = k.din("wV", [2, 128, 16, 512])
    qkT = k.dout("qkT", [16, 128, T], BF16)
    uT = k.dout("uT", [8, 128, T], F32)
    v = k.dout("v", [T, 1024], BF16)

    TS = min(T, 2048)
    for sg in range(T // TS):
        _L1_group(k, TS, x[sg * TS:(sg + 1) * TS, :], g0, wA, wV, qkT[:, :, sg * TS:(sg + 1) * TS],
                  uT[:, :, sg * TS:(sg + 1) * TS], v[sg * TS:(sg + 1) * TS, :])
    return k.done()


def _L1_group(k, T, x, g0, wA, wV, qkT, uT, v):
    NT = T // 128
    TG = min(512, T)
    NG = T // TG
    ident = k.make_ident(BF16, "identb")
    gain = k.sb("gain", [128, D])
    k.dma("sp", gain.t[:], g0.partition_broadcast(128), [], [gain.b])
    hT = k.sb("hT", [128, 16, T], BF16)
    hTb = [k.cbuf(f"hT{i}") for i in range(NT)]
    xs = k.sbn("xs", [128, D], F32, 2)
    hbs = k.sbn("hb", [128, D], BF16, 2)
    junk = k.sb("junk", [128, D], BF16)
    sss = k.sbn("ss", [128, 1], F32, 2)
    rstds = k.sbn("rstd", [128, 1], F32, 2)
    ptrs = [k.ps(f"ptr{i}", [128, D], BF16) for i in range(2)]

    for i in range(NT):
        xt = xs[i % 2]
        k.dma("sp", xt.t[:], x[i * 128:(i + 1) * 128, :], [], [xt.b])
        norm_transpose_tile(k, xt.t[:], xt, gain, hbs[i % 2], junk, sss[i % 2], rstds[i % 2], ptrs[i % 2],
                            lambda i=i: hT.t[:, :, i * 128:(i + 1) * 128], ident, [xt.b], hTb[i])

    wts = k.sbn("wa", [128, 16, 128], BF16, 3)
    pacc = [k.ps(f"pacc{i}", [128, 512], F32) for i in range(2)]
    obf = k.sbn("obf", [128, 512], BF16, 2)
    of32 = k.sbn("of32", [128, 512], F32, 2)
    cnt = 0
    for cb in range(24):
        wt = wts[cb % 3]
        k.dma("pool", wt.t[:], wA[cb], [], [wt.b])
        for tg in range(NG):
            pa = pacc[cnt % 2]
            rd = [wt.b] + hTb[tg * (TG // 128):(tg + 1) * (TG // 128)]
            for kc in range(16):
                k.mm(pa.t[:, 0:TG], wt.t[:, kc, :], hT.t[:, kc, tg * TG:(tg + 1) * TG], kc == 0, kc == 15,
                     rd, [pa.b])
            if cb < 16:
                o = obf[cnt % 2]
                k.copy("act" if cnt % 2 else "dve", o.t[:, 0:TG], pa.t[:, 0:TG], [pa.b], [o.b])
                k.dma("sp", qkT[cb, :, tg * TG:(tg + 1) * TG], o.t[:, 0:TG], [o.b], [], final=True)
            else:
                o = of32[cnt % 2]
                k.copy("act" if cnt % 2 else "dve", o.t[:, 0:TG], pa.t[:, 0:TG], [pa.b], [o.b])
                k.dma("sp", uT[cb - 16, :, tg * TG:(tg + 1) * TG], o.t[:, 0:TG], [o.b], [], final=True)
            cnt += 1
    wvs = k.sbn("wv", [128, 16, 512], BF16, 2)
    for cg in range(2):
        wt = wvs[cg]
        k.dma("pool", wt.t[:], wV[cg], [], [wt.b])
        for i in range(NT):
            pa = pacc[cnt % 2]
            for kc in range(16):
                k.mm(pa.t[:], hT.t[:, kc, i * 128:(i + 1) * 128], wt.t[:, kc, :], kc == 0, kc == 15,
                     [wt.b, hTb[i]], [pa.b])
            o = obf[cnt % 2]
            k.copy("act" if cnt % 2 else "dve", o.t[:], pa.t[:], [pa.b], [o.b])
            k.dma("sp", v[i * 128:(i + 1) * 128, cg * 512:(cg + 1) * 512], o.t[:], [o.b], [], final=True)
            cnt += 1


def host_L1_weights(w_in):
    w = w_in.reshape(16, 128, 4096)
    qku = np.concatenate([w[:, :, 0:2048], w[:, :, 3072:4096]], axis=2)
    wA = np.ascontiguousarray(qku.reshape(16, 128, 24, 128).transpose(2, 1, 0, 3))
    wv = w[:, :, 2048:3072]
    wV = np.ascontiguousarray(wv.reshape(16, 128, 2, 512).transpose(2, 1, 0, 3))
    return wA, wV


def build_FFN(T, wdma="pool", k=None):
    TG = min(512, T)
    NG = T // TG
    NTG = TG // 128
    NF = DFF // 128
    k = k or K()
    x1 = k.din("x1", [T, D])
    g2 = k.din("g2", [1, D])
    g3 = k.din("g3", [1, D])
    wg = k.din("wg", [NF, 128, 16, 128])
    wu = k.din("wu", [NF, 128, 16, 128])
    wd = k.din("wd", [16, 128, NF, 128])
    x2 = k.dout("x2", [T, D])

    ident = k.make_ident(F32, "identf")
    gain2 = k.sb("gain2", [128, D])
    gain3 = k.sb("gain3", [128, D])
    k.dma("sp", gain2.t[:], g2.partition_broadcast(128), [], [gain2.b])
    k.dma("sp", gain3.t[:], g3.partition_broadcast(128), [], [gain3.b])
    xs = k.sbn("xs", [128, D], F32, 2)
    hb = k.sb("hb", [128, D], F32)
    junk = k.sb("junk", [128, D], BF16)
    ss = k.sb("ss", [128, 1]); rstd = k.sb("rstd", [128, 1])
    h2T = k.sb("h2T", [128, 16, TG], BF16)
    actT = k.sb("actT", [128, NF, TG], BF16)
    actb = [k.buf(f"act{i}") for i in range(NF)]
    wgs = k.sbn("wgs", [128, 16, 128], BF16, 2)
    wus = k.sbn("wus", [128, 16, 128], BF16, 2)
    wds = k.sbn("wds", [128, NF, 128], BF16, 2)
    sg = k.sbn("sg", [128, TG], F32, 2)
    fT = k.sb("fT", [128, 16, TG], F32)
    fTb = [k.buf(f"fT{i}") for i in range(16)]
    acc = [k.ps(f"acc{i}", [128, 512], F32) for i in range(4)]
    big = k.ps("big", [128, D], F32)

    for g in range(NG):
        for i in range(NTG):
            row = (g * NTG + i) * 128
            xt = xs[i % 2]
            k.dma("sp", xt.t[:], x1[row:row + 128, :], [], [xt.b])
            rms_stats(k, xt.t[:], junk, ss, rstd, 1e-6, D, [xt.b])
            k.stt(hb.t[:], xt.t[:], rstd.t[:, 0:1], gain2.t[:], ALU.mult, ALU.mult, [xt.b, rstd.b, gain2.b], [hb.b])
            for kc in range(16):
                k.tr(big.t[:, kc * 128:(kc + 1) * 128], hb.t[:, kc * 128:(kc + 1) * 128], ident.t[:],
                     [hb.b, ident.b], [big.b])
            k.copy("act", h2T.t[:, :, i * 128:(i + 1) * 128], big.t[:].rearrange("p (c t) -> p c t", c=16),
                   [big.b], [h2T.b])
        for fc in range(NF):
            wgt = wgs[fc % 2]; wut = wus[fc % 2]
            k.dma(wdma, wgt.t[:], wg[fc], [], [wgt.b])
            k.dma(wdma, wut.t[:], wu[fc], [], [wut.b])
            pg = acc[(fc % 2) * 2]; pu = acc[(fc % 2) * 2 + 1]
            for kc in range(16):
                k.mm(pg.t[:, 0:TG], wgt.t[:, kc, :], h2T.t[:, kc, :], kc == 0, kc == 15, [wgt.b, h2T.b], [pg.b])
            for kc in range(16):
                k.mm(pu.t[:, 0:TG], wut.t[:, kc, :], h2T.t[:, kc, :], kc == 0, kc == 15, [wut.b, h2T.b], [pu.b])
            s = sg[fc % 2]
            k.act(s.t[:, 0:TG], pg.t[:, 0:TG], AF.Silu, [pg.b], [s.b])
            k.tt("dve", actT.t[:, fc, :], s.t[:, 0:TG], pu.t[:, 0:TG], ALU.mult, [s.b, pu.b], [actb[fc]])
        for fb in range(16):
            wdt = wds[fb % 2]
            k.dma(wdma, wdt.t[:], wd[fb], [], [wdt.b])
            pa = acc[fb % 4]
            for kc in range(NF):
                k.mm(pa.t[:, 0:TG], wdt.t[:, kc, :], actT.t[:, kc, :], kc == 0, kc == NF - 1,
                     [wdt.b, actb[kc]], [pa.b])
            k.copy("act" if fb % 2 else "dve", fT.t[:, fb, :], pa.t[:, 0:TG], [pa.b], [fTb[fb]])
        for i in range(NTG):
            row = (g * NTG + i) * 128
            xt = xs[i % 2]
            k.dma("sp", xt.t[:], x1[row:row + 128, :], [], [xt.b])
            for fb in range(16):
                k.tr(big.t[:, fb * 128:(fb + 1) * 128], fT.t[:, fb, i * 128:(i + 1) * 128], ident.t[:],
                     [fTb[fb], ident.b], [big.b])
            rms_stats(k, big.t[:], junk, ss, rstd, 1e-6, D, [big.b])
            k.stt(hb.t[:], big.t[:], rstd.t[:, 0:1], gain3.t[:], ALU.mult, ALU.mult, [big.b, rstd.b, gain3.b], [hb.b])
            k.tt("dve", xt.t[:], xt.t[:], hb.t[:], ALU.add, [xt.b, hb.b], [xt.b])
            k.dma("sp", x2[row:row + 128, :], xt.t[:], [xt.b], [], final=True)
    return k.done()


def host_FFN_weights(w_gate, w_up, w_down):
    NF = DFF // 128
    def gu(w):
        return np.ascontiguousarray(w.reshape(16, 128, NF, 128).transpose(2, 1, 0, 3))
    wd = np.ascontiguousarray(w_down.reshape(NF, 128, 16, 128).transpose(2, 1, 0, 3))
    return gu(w_gate), gu(w_up), wd


LAMBDA_INIT0 = 0.8 - 0.6 * math.exp(-0.3 * 0)
ATT_SCALE = 64 ** -0.5
ATT_SKIP = 80.0


def build_ATT(SEQ, NSEQ, k=None, items=None, slope=None):
    NQT = SEQ // 512
    NKT = SEQ // 128
    ND = 4 * NQT + 3
    k = k or K()
    tri_d = k.din("tri", [128, 128], BF16)
    lq_d = k.din("lq", [1, 256], F32)
    subln_d = k.din("subln", [1, 128], F32)
    if items is None:
        qT = k.din("qT", [NSEQ, 2, 64, SEQ], BF16)
        kT = k.din("kT", [NSEQ, 2, 64, SEQ], BF16)
        v = k.din("v", [NSEQ, SEQ, 128], BF16)
        qaug = k.din("qaug", [4, SEQ], BF16)
        kaug = k.din("kaug", [4, SEQ], BF16)
        biastab = k.din("biastab", [128, ND], F32)
        attn = k.dout("attn", [NSEQ, SEQ, 128], BF16)
        items = [dict(q=qT[s_], k=kT[s_], v=v[s_], qaug=qaug, kaug=kaug, bt=biastab, out=attn[s_], slope=slope) for s_ in range(NSEQ)]

    bt = k.sb("bt", [128, ND])
    tri = k.sb("tri", [128, 128], BF16); k.dma("sp", tri.t[:], tri_d[:, :], [], [tri.b])
    lq = k.sb("lq", [128, 256]); k.dma("sp", lq.t[:], lq_d.partition_broadcast(128), [], [lq.b])
    sub = k.sb("sub", [128, 128]); k.dma("sp", sub.t[:], subln_d.partition_broadcast(128), [], [sub.b])
    k.ts("dve", sub.t[:], sub.t[:], 1.0 - LAMBDA_INIT0, None, ALU.mult, None, [sub.b], [sub.b])
    lp = k.sb("lp", [128, 2, 64]); l2 = k.sb("l2", [128, 2]); nlam = k.sb("nlam", [128, 1])
    lqv = lq.t[:].rearrange("p (a b c) -> p a b c", a=2, b=2)
    k.tt("dve", lp.t[:], lqv[:, :, 0, :], lqv[:, :, 1, :], ALU.mult, [lq.b], [lp.b])
    k.reduce(l2.t[:], lp.t[:], ALU.add, [lp.b], [l2.b])
    k.act(l2.t[:], l2.t[:], AF.Exp, [l2.b], [l2.b])
    k.tt("dve", nlam.t[:], l2.t[:, 1:2], l2.t[:, 0:1], ALU.subtract, [l2.b], [nlam.b])
    k.ts("dve", nlam.t[:], nlam.t[:], -LAMBDA_INIT0, None, ALU.add, None, [nlam.b], [nlam.b])

    qa = [k.sb(f"qa{m}", [68, SEQ], BF16) for m in range(2)]
    ka = [k.sb(f"ka{m}", [68, SEQ], BF16) for m in range(2)]
    va = k.sb("va", [128, NKT, 129], BF16)
    pS = [[k.ps(f"pS{m}{b}", [128, 512], F32) for b in range(2)] for m in range(2)]
    pO = [k.ps(f"pO{q}", [128, 512], F32) for q in range(4)]
    P = [[k.sb(f"P{m}{b}", [128, 512], BF16) for b in range(2)] for m in range(2)]
    rec = k.sbn("rec", [128, 2], F32, 2)
    rl = k.sbn("rl", [128, 1], F32, 2)
    t1 = k.sbn("t1", [128, 128], F32, 2)
    dd = k.sbn("dd", [128, 128], F32, 2)
    junkf = k.sb("junkf", [128, 128], F32)
    ss = k.sbn("ss", [128, 1], F32, 2)
    rstd = k.sbn("rstd", [128, 1], F32, 2)
    ob = k.sbn("ob", [128, 128], BF16, 2)
    fin = 0
    for it in items:
        k.dma("sp", bt.t[:], it["bt"], [], [bt.b])
        for m in range(2):
            k.dma("sp", qa[m].t[0:64, :], it["q"][m], [], [qa[m].b])
            k.dma("sp", qa[m].t[64:68, :], it["qaug"], [], [qa[m].b])
            k.dma("sp", ka[m].t[0:64, :], it["k"][m], [], [ka[m].b])
            k.dma("sp", ka[m].t[64:68, :], it["kaug"], [], [ka[m].b])
        k.memset("pool", va.t[:, :, 128:129], 1.0, [va.b])
        k.dma("sp", va.t[:, :, 0:128], it["v"].rearrange("(j p) e -> p j e", p=128), [], [va.b])
        slope_ = it.get("slope")
        pairs = []
        for i in range(NQT):
            js = [j for j in range(4 * i + 4)
                  if slope_ is None or slope_ * (512 * i - (128 * j + 127)) <= ATT_SKIP]
            pairs += [(i, j, j == js[0]) for j in js]

        def emit_qk(n):
            i, j, _ = pairs[n]
            q0 = max(j - 4 * i, 0) * 128
            for m in range(2):
                ps_ = pS[m][n % 2]
                k.mm(ps_.t[:, q0:512], ka[m].t[:, j * 128:(j + 1) * 128],
                     qa[m].t[:, i * 512 + q0:(i + 1) * 512], True, True, [ka[m].b, qa[m].b], [ps_.b])

        emit_qk(0)
        for n, (i, j, first) in enumerate(pairs):
            if n + 1 < len(pairs):
                emit_qk(n + 1)
            jj = j - 4 * i
            q0 = max(jj, 0) * 128
            d = 4 * i - j + 3
            for m in range(2):
                ps_ = pS[m][n % 2]
                pt = P[m][n % 2]
                k.act(pt.t[:, q0:512], ps_.t[:, q0:512], AF.Exp, [ps_.b, bt.b], [pt.b],
                      scale=ATT_SCALE, bias=bt.t[:, d:d + 1])
                if jj >= 0:
                    k.tt("dve", pt.t[:, q0:q0 + 128], pt.t[:, q0:q0 + 128], tri.t[:], ALU.mult,
                         [pt.b, tri.b], [pt.b])
                for qq in range(max(jj, 0), 4):
                    k.mm(pO[qq].t[:, m * 129:(m + 1) * 129], pt.t[:, qq * 128:(qq + 1) * 128],
                         va.t[:, j, :], (first and m == 0), (j == 4 * i + qq and m == 1),
                         [pt.b, va.b], [pO[qq].b], skip_group_check=True)
            if jj >= 0:
                qq = jj
                f = fin % 2
                fin += 1
                po = pO[qq]
                k.recip(rec[f].t[:], po.t[:, 128:258:129], [po.b], [rec[f].b])
                k.ts("dve", rl[f].t[:], rec[f].t[:, 1:2], nlam.t[:, 0:1], None, ALU.mult, None,
                     [rec[f].b, nlam.b], [rl[f].b])
                k.ts("dve", t1[f].t[:], po.t[:, 0:128], rec[f].t[:, 0:1], None, ALU.mult, None,
                     [po.b, rec[f].b], [t1[f].b])
                k.stt(dd[f].t[:], po.t[:, 129:257], rl[f].t[:, 0:1], t1[f].t[:], ALU.mult, ALU.add,
                      [po.b, rl[f].b, t1[f].b], [dd[f].b])
                rms_stats_dve(k, dd[f].t[:], junkf.t[:], junkf.b, ss[f], rstd[f], 1e-5, 128, [dd[f].b])
                k.stt(ob[f].t[:], dd[f].t[:], rstd[f].t[:, 0:1], sub.t[:], ALU.mult, ALU.mult,
                      [dd[f].b, rstd[f].b, sub.b], [ob[f].b])
                r0 = i * 512 + qq * 128
                k.dma("sp", it["out"][r0:r0 + 128, :], ob[f].t[:], [ob[f].b], [], final=True)
    return k.done()


def split_bf16(x, n=2):
    parts = []
    r = np.asarray(x, np.float64)
    for _ in range(n):
        p = r.astype(np.float32).astype(NPBF)
        parts.append(p)
        r = r - p.astype(np.float64)
    return parts


def host_ATT_consts(head, SEQ):
    slope = 2.0 ** (-8.0 * (head + 1) / 8)
    t = np.arange(SEQ)
    a = -slope * (t % 512) / ATT_SCALE
    b = slope * (t % 128) / ATT_SCALE
    ah, al = split_bf16(a)
    bh, bl = split_bf16(b)
    one = np.ones(SEQ, NPBF)
    qaug = np.stack([ah, al, one, one])
    kaug = np.stack([one, one, bh, bl])
    ND = 4 * (SEQ // 512) + 3
    biastab = np.broadcast_to((-slope * 128.0 * (np.arange(ND) - 3)).astype(np.float32)[None, :], (128, ND)).copy()
    tri = (np.arange(128)[None, :] >= np.arange(128)[:, None]).astype(NPBF)
    return qaug, kaug, biastab, tri


TWO_PI = 2.0 * math.pi
MAGIC = 12582912.0


def _range_reduce_sin(k, out, ang, tmp, shape_reads, writes, shift=0.0):
    a_ap, a_b = ang
    o_ap, o_b = out
    t_ap, t_b = tmp
    k.ts("dve", t_ap, a_ap, shift, 1.0 / TWO_PI, ALU.add, ALU.mult, [a_b], [t_b])
    k.ts("dve", o_ap, t_ap, MAGIC, None, ALU.add, None, [t_b], [o_b])
    k.ts("dve", o_ap, o_ap, -MAGIC, None, ALU.add, None, [o_b], [o_b])
    k.tt("dve", t_ap, t_ap, o_ap, ALU.subtract, [t_b, o_b], [t_b])
    k.ts("dve", t_ap, t_ap, TWO_PI, 3.1415925, ALU.mult, ALU.min, [t_b], [t_b])
    k.ts("dve", t_ap, t_ap, -3.1415925, None, ALU.max, None, [t_b], [t_b])
    k.act(o_ap, t_ap, AF.Sin, [t_b], [o_b])


def build_S5(SEQ, NSEQ, k=None, blocks=None):
    L = 512
    NB = SEQ // L
    k = k or K()
    if blocks is None:
        uT_ = k.din("uT", [NSEQ, 128, SEQ])
        ygT_ = k.dout("ygT", [NSEQ, 128, SEQ], BF16)
        blocks = [dict(uT=[uT_[s_] for s_ in range(NSEQ)], out=[ygT_[s_] for s_ in range(NSEQ)],
                       lre=k.din("lre", [128, 4]), lim=k.din("lim", [128, 4]), ldt=k.din("ldt", [128, 4]),
                       bre=k.din("bre", [4, 128, 16]), bim=k.din("bim", [4, 128, 16]),
                       ctre=k.din("ctre", [4, 128, 128]), ctim=k.din("ctim", [4, 128, 128]),
                       dcol=k.din("dcol", [128, 1]))]
    for B_ in blocks:
        lre_d, lim_d, ldt_d, bre_d, bim_d = B_["lre"], B_["lim"], B_["ldt"], B_["bre"], B_["bim"]
        ctre_d, ctim_d, dcol_d = B_["ctre"], B_["ctim"], B_["dcol"]
        uT_l, ygT_l = B_["uT"], B_["out"]
        NSEQ = len(uT_l)
        ident = k.make_ident(F32, "identf")
        lre = k.sb("lre", [128, 4]); lim = k.sb("lim", [128, 4]); ldt = k.sb("ldt", [128, 4])
        dcol = k.sb("dcol", [128, 1])
        for t_, d_ in ((lre, lre_d), (lim, lim_d), (ldt, ldt_d), (dcol, dcol_d)):
            k.dma("sp", t_.t[:], d_, [], [t_.b])
        cols = {n: k.sb(n, [128, 4]) for n in
                "lr dt rho th c1 s1 tmp tmp2 abre abim nre den fre fim cL sL nsL angL".split()}
        C = cols

        def T2(n):
            return C[n].t[:], C[n].b
        k.ts("dve", C["lr"].t[:], lre.t[:], -1e-4, None, ALU.min, None, [lre.b], [C["lr"].b])
        k.act(C["dt"].t[:], ldt.t[:], AF.Exp, [ldt.b], [C["dt"].b])
        k.tt("dve", C["tmp"].t[:], C["lr"].t[:], C["dt"].t[:], ALU.mult, [C["lr"].b, C["dt"].b], [C["tmp"].b])
        k.act(C["rho"].t[:], C["tmp"].t[:], AF.Exp, [C["tmp"].b], [C["rho"].b])
        k.tt("dve", C["th"].t[:], lim.t[:], C["dt"].t[:], ALU.mult, [lim.b, C["dt"].b], [C["th"].b])
        _range_reduce_sin(k, T2("s1"), T2("th"), T2("tmp"), None, None, 0.0)
        _range_reduce_sin(k, T2("c1"), T2("th"), T2("tmp"), None, None, math.pi / 2)
        k.ts("dve", C["angL"].t[:], C["th"].t[:], float(L), None, ALU.mult, None, [C["th"].b], [C["angL"].b])
        _range_reduce_sin(k, T2("sL"), T2("angL"), T2("tmp"), None, None, 0.0)
        _range_reduce_sin(k, T2("cL"), T2("angL"), T2("tmp"), None, None, math.pi / 2)
        k.ts("dve", C["nsL"].t[:], C["sL"].t[:], -1.0, None, ALU.mult, None, [C["sL"].b], [C["nsL"].b])
        k.tt("dve", C["abre"].t[:], C["rho"].t[:], C["c1"].t[:], ALU.mult, [C["rho"].b, C["c1"].b], [C["abre"].b])
        k.tt("dve", C["abim"].t[:], C["rho"].t[:], C["s1"].t[:], ALU.mult, [C["rho"].b, C["s1"].b], [C["abim"].b])
        k.ts("dve", C["nre"].t[:], C["abre"].t[:], -1.0, None, ALU.add, None, [C["abre"].b], [C["nre"].b])
        k.tt("dve", C["den"].t[:], C["lr"].t[:], C["lr"].t[:], ALU.mult, [C["lr"].b], [C["den"].b])
        k.tt("dve", C["tmp"].t[:], lim.t[:], lim.t[:], ALU.mult, [lim.b], [C["tmp"].b])
        k.tt("dve", C["den"].t[:], C["den"].t[:], C["tmp"].t[:], ALU.add, [C["den"].b, C["tmp"].b], [C["den"].b])
        k.recip(C["den"].t[:], C["den"].t[:], [C["den"].b], [C["den"].b])
        k.tt("dve", C["tmp"].t[:], C["nre"].t[:], C["lr"].t[:], ALU.mult, [C["nre"].b, C["lr"].b], [C["tmp"].b])
        k.tt("dve", C["tmp2"].t[:], C["abim"].t[:], lim.t[:], ALU.mult, [C["abim"].b, lim.b], [C["tmp2"].b])
        k.tt("dve", C["fre"].t[:], C["tmp"].t[:], C["tmp2"].t[:], ALU.add, [C["tmp"].b, C["tmp2"].b], [C["fre"].b])
        k.tt("dve", C["fre"].t[:], C["fre"].t[:], C["den"].t[:], ALU.mult, [C["fre"].b, C["den"].b], [C["fre"].b])
        k.tt("dve", C["tmp"].t[:], C["abim"].t[:], C["lr"].t[:], ALU.mult, [C["abim"].b, C["lr"].b], [C["tmp"].b])
        k.tt("dve", C["tmp2"].t[:], C["nre"].t[:], lim.t[:], ALU.mult, [C["nre"].b, lim.b], [C["tmp2"].b])
        k.tt("dve", C["fim"].t[:], C["tmp"].t[:], C["tmp2"].t[:], ALU.subtract, [C["tmp"].b, C["tmp2"].b], [C["fim"].b])
        k.tt("dve", C["fim"].t[:], C["fim"].t[:], C["den"].t[:], ALU.mult, [C["fim"].b, C["den"].b], [C["fim"].b])

        iota_i = k.sb("iota_i", [128, L], mybir.dt.int32)
        k.S.op("pool", lambda e: e.iota(iota_i.t[:], pattern=[[1, L]], base=0, channel_multiplier=0), [], [iota_i.b])
        iota = k.sb("iota", [128, L])
        k.copy("dve", iota.t[:], iota_i.t[:], [iota_i.b], [iota.b])
        ones = k.sb("ones", [128, L]); k.memset("pool", ones.t[:], 1.0, [ones.b])
        ptr = k.ps("ptr", [128, 128], F32)
        bbT = [[k.sb(f"bbT{r}{st}", [128, 128], BF16) for st in range(4)] for r in range(2)]
        ctb = [[k.sb(f"ctb{r}{st}", [128, 128], BF16) for st in range(4)] for r in range(2)]
        ctab = [k.sb(f"ctab{st}", [128, L]) for st in range(4)]
        stab = [k.sb(f"stab{st}", [128, L]) for st in range(4)]
        nstab = [k.sb(f"nstab{st}", [128, L]) for st in range(4)]
        rho = [k.sb(f"rho{st}", [128, L]) for st in range(4)]
        ang = k.sb("ang", [128, L]); tmpL = k.sb("tmpL", [128, L])
        br = k.sb("br", [128, 16]); bi = k.sb("bi", [128, 16]); bt1 = k.sb("bt1", [128, 16]); bb = k.sb("bb", [128, 16])
        Z = k.sb("Z", [128, 128]); ctf = k.sb("ctf", [128, 128])
        for st in range(4):
            k.dma("sp", br.t[:], bre_d[st], [], [br.b])
            k.dma("sp", bi.t[:], bim_d[st], [], [bi.b])
            for r in range(2):
                a_, b_ = (br, bi) if r == 0 else (bi, br)
                k.ts("dve", bt1.t[:], b_.t[:], C["fim"].t[:, st:st + 1], None, ALU.mult, None, [b_.b, C["fim"].b], [bt1.b])
                if r == 0:
                    k.ts("dve", bt1.t[:], bt1.t[:], -1.0, None, ALU.mult, None, [bt1.b], [bt1.b])
                k.stt(bb.t[:], a_.t[:], C["fre"].t[:, st:st + 1], bt1.t[:], ALU.mult, ALU.add,
                      [a_.b, C["fre"].b, bt1.b], [bb.b])
                k.memset("pool", Z.t[:], 0.0, [Z.b])
                k.copy("dve", Z.t[0:64, 32 * st:32 * st + 16], bb.t[0:64, :], [bb.b], [Z.b])
                k.copy("dve", Z.t[64:128, 32 * st + 16:32 * st + 32], bb.t[64:128, :], [bb.b], [Z.b])
                k.tr(ptr.t[:], Z.t[:], ident.t[:], [Z.b, ident.b], [ptr.b])
                k.copy("dve", bbT[r][st].t[:], ptr.t[:], [ptr.b], [bbT[r][st].b])
                k.dma("sp", ctf.t[:], (ctre_d if r == 0 else ctim_d)[st], [], [ctf.b])
                k.copy("dve", ctb[r][st].t[:], ctf.t[:], [ctf.b], [ctb[r][st].b])
            k.ts("dve", ang.t[:], iota.t[:], C["th"].t[:, st:st + 1], None, ALU.mult, None, [iota.b, C["th"].b], [ang.b])
            _range_reduce_sin(k, (stab[st].t[:], stab[st].b), (ang.t[:], ang.b), (tmpL.t[:], tmpL.b), None, None, 0.0)
            _range_reduce_sin(k, (ctab[st].t[:], ctab[st].b), (ang.t[:], ang.b), (tmpL.t[:], tmpL.b), None, None, math.pi / 2)
            k.ts("dve", nstab[st].t[:], stab[st].t[:], -1.0, None, ALU.mult, None, [stab[st].b], [nstab[st].b])
            k.ts("dve", rho[st].t[:], ones.t[:], C["rho"].t[:, st:st + 1], None, ALU.mult, None, [ones.b, C["rho"].b], [rho[st].b])

        uf = k.sbn("uf", [128, L], F32, 2)
        ub = k.sbn("ub", [128, L], BF16, 2)
        pb = [[k.ps(f"pb{r}{b}", [128, L], F32) for b in range(2)] for r in range(2)]
        py = [k.ps(f"py{b}", [128, L], F32) for b in range(2)]
        t = {n: k.sbn(n, [128, L], F32, 2) for n in ("t1", "t2", "t3", "t4", "zre", "zim", "wre", "wim", "m1", "m2", "m3", "m4")}
        xre = k.sbn("xre", [128, L], BF16, 2)
        nxim = k.sbn("nxim", [128, L], BF16, 2)
        init = [[k.sb(f"init{r}{st}", [128, 1]) for st in range(4)] for r in range(2)]
        ca = k.sbn("ca", [128, 1], F32, 2)
        yv = k.sbn("yv", [128, L], F32, 2)
        g1 = k.sbn("g1", [128, L], F32, 2)
        g2 = k.sbn("g2", [128, L], F32, 2)
        yg = k.sbn("yg", [128, L], BF16, 2)
        u_i = 0
        for s in range(NSEQ):
            for st in range(4):
                for r in range(2):
                    k.memset("pool", init[r][st].t[:], 0.0, [init[r][st].b])
            for blk in range(NB):
                ufb = uf[blk % 2]; ubb = ub[blk % 2]; pyb = py[blk % 2]
                k.dma("sp", ufb.t[:], uT_l[s][:, blk * L:(blk + 1) * L], [], [ufb.b])
                k.copy("act", ubb.t[:], ufb.t[:], [ufb.b], [ubb.b])
                for st in range(4):
                    p = u_i % 2
                    u_i += 1
                    pre, pim = pb[0][p], pb[1][p]
                    k.mm(pre.t[:], bbT[0][st].t[:], ubb.t[:], True, True, [bbT[0][st].b, ubb.b], [pre.b])
                    k.mm(pim.t[:], bbT[1][st].t[:], ubb.t[:], True, True, [bbT[1][st].b, ubb.b], [pim.b])
                    T = {n: t[n][p] for n in t}
                    c_, s_, ns_ = ctab[st], stab[st], nstab[st]
                    k.tt("dve", T["t1"].t[:], c_.t[:], pre.t[:], ALU.mult, [c_.b, pre.b], [T["t1"].b])
                    k.tt("dve", T["t2"].t[:], s_.t[:], pim.t[:], ALU.mult, [s_.b, pim.b], [T["t2"].b])
                    k.tt("dve", T["t3"].t[:], c_.t[:], pim.t[:], ALU.mult, [c_.b, pim.b], [T["t3"].b])
                    k.tt("dve", T["t4"].t[:], s_.t[:], pre.t[:], ALU.mult, [s_.b, pre.b], [T["t4"].b])
                    k.tt("dve", T["zre"].t[:], T["t1"].t[:], T["t2"].t[:], ALU.add, [T["t1"].b, T["t2"].b], [T["zre"].b])
                    k.tt("dve", T["zim"].t[:], T["t3"].t[:], T["t4"].t[:], ALU.subtract, [T["t3"].b, T["t4"].b], [T["zim"].b])
                    ir, ii = init[0][st], init[1][st]
                    k.scan(T["wre"].t[:], rho[st].t[:], T["zre"].t[:], ir.t[:, 0:1], [rho[st].b, T["zre"].b, ir.b], [T["wre"].b])
                    k.scan(T["wim"].t[:], rho[st].t[:], T["zim"].t[:], ii.t[:, 0:1], [rho[st].b, T["zim"].b, ii.b], [T["wim"].b])
                    cL = C["cL"].t[:, st:st + 1]; sL = C["sL"].t[:, st:st + 1]; nsL = C["nsL"].t[:, st:st + 1]
                    wl_re = T["wre"].t[:, L - 1:L]; wl_im = T["wim"].t[:, L - 1:L]
                    rb = [T["wre"].b, T["wim"].b, C["cL"].b, C["sL"].b, C["nsL"].b]
                    k.ts("dve", ca[0].t[:], wl_re, cL, None, ALU.mult, None, rb, [ca[0].b])
                    k.stt(ir.t[:], wl_im, nsL, ca[0].t[:], ALU.mult, ALU.add, rb + [ca[0].b], [ir.b])
                    k.ts("dve", ca[1].t[:], wl_im, cL, None, ALU.mult, None, rb, [ca[1].b])
                    k.stt(ii.t[:], wl_re, sL, ca[1].t[:], ALU.mult, ALU.add, rb + [ca[1].b], [ii.b])
                    k.tt("dve", T["m1"].t[:], c_.t[:], T["wre"].t[:], ALU.mult, [c_.b, T["wre"].b], [T["m1"].b])
                    k.tt("dve", T["m2"].t[:], s_.t[:], T["wim"].t[:], ALU.mult, [s_.b, T["wim"].b], [T["m2"].b])
                    k.tt("dve", T["m3"].t[:], ns_.t[:], T["wre"].t[:], ALU.mult, [ns_.b, T["wre"].b], [T["m3"].b])
                    k.tt("dve", T["m4"].t[:], c_.t[:], T["wim"].t[:], ALU.mult, [c_.b, T["wim"].b], [T["m4"].b])
                    xr = xre[p]; nxi = nxim[p]
                    k.tt("dve", xr.t[:], T["m1"].t[:], T["m2"].t[:], ALU.subtract, [T["m1"].b, T["m2"].b], [xr.b])
                    k.tt("dve", nxi.t[:], T["m3"].t[:], T["m4"].t[:], ALU.subtract, [T["m3"].b, T["m4"].b], [nxi.b])
                    k.mm(pyb.t[:], ctb[0][st].t[:], xr.t[:], st == 0, False, [ctb[0][st].b, xr.b], [pyb.b])
                    k.mm(pyb.t[:], ctb[1][st].t[:], nxi.t[:], False, st == 3, [ctb[1][st].b, nxi.b], [pyb.b])
                q = blk % 2
                k.stt(yv[q].t[:], ufb.t[:], dcol.t[:, 0:1], pyb.t[:], ALU.mult, ALU.add, [ufb.b, dcol.b, pyb.b], [yv[q].b])
                k.tt("dve", g1[q].t[:], yv[q].t[:], yv[q].t[:], ALU.mult, [yv[q].b], [g1[q].b])
                k.ts("dve", g1[q].t[:], g1[q].t[:], 0.044715, 1.0, ALU.mult, ALU.add, [g1[q].b], [g1[q].b])
                k.tt("dve", g2[q].t[:], g1[q].t[:], yv[q].t[:], ALU.mult, [g1[q].b, yv[q].b], [g2[q].b])
                k.act(g2[q].t[:], g2[q].t[:], AF.Sigmoid, [g2[q].b], [g2[q].b], scale=2.0 * math.sqrt(2.0 / math.pi))
                k.tt("dve", yg[q].t[:], yv[q].t[:], g2[q].t[:], ALU.mult, [yv[q].b, g2[q].b], [yg[q].b])
                k.dma("sp", ygT_l[s][:, blk * L:(blk + 1) * L], yg[q].t[:], [yg[q].b], [], final=True)


    return k.done()


def host_S5_params(core, lam_re, lam_im, log_dt, b_re, b_im, c_re, c_im, d_skip):
    g0 = 8 * core
    sl = slice(g0, g0 + 8)
    lre = np.ascontiguousarray(lam_re[sl].reshape(4, 128).T)
    lim = np.ascontiguousarray(lam_im[sl].reshape(4, 128).T)
    ldt = np.ascontiguousarray(np.repeat(log_dt[sl], 64).reshape(4, 128).T)
    bre = np.ascontiguousarray(b_re[sl].reshape(4, 128, 16))
    bim = np.ascontiguousarray(b_im[sl].reshape(4, 128, 16))
    ctre = np.zeros((4, 128, 128), np.float32)
    ctim = np.zeros((4, 128, 128), np.float32)
    for st in range(4):
        for gl in range(2):
            g = g0 + 2 * st + gl
            col = 32 * st + 16 * gl
            ctre[st, gl * 64:(gl + 1) * 64, col:col + 16] = c_re[g].T
            ctim[st, gl * 64:(gl + 1) * 64, col:col + 16] = c_im[g].T
    dcol = np.ascontiguousarray(d_skip[sl].reshape(128, 1))
    return dict(lre=lre, lim=lim, ldt=ldt, bre=bre, bim=bim, ctre=ctre, ctim=ctim, dcol=dcol)


def out_proj_tile(k, mixT, mixb, col0, wo, big, xt, hb, junk, ss, rstd, gain1, x_src, x_dst, row, xload=None):
    for cg in range(4):
        for kc in range(16):
            k.mm(big.t[:, cg * 512:(cg + 1) * 512], mixT.t[:, kc, col0:col0 + 128], wo[cg].t[:, kc, :],
                 kc == 0, kc == 15, [wo[cg].b] + mixb, [big.b])
    if xload is None:
        k.dma("sp", xt.t[:], x_src[row:row + 128, :], [], [xt.b])
    else:
        xload(xt)
    rms_stats(k, big.t[:], junk, ss, rstd, 1e-6, D, [big.b])
    k.stt(hb.t[:], big.t[:], rstd.t[:, 0:1], gain1.t[:], ALU.mult, ALU.mult, [big.b, rstd.b, gain1.b], [hb.b])
    k.tt("dve", xt.t[:], xt.t[:], hb.t[:], ALU.add, [xt.b, hb.b], [xt.b])
    k.dma("sp", x_dst[row:row + 128, :], xt.t[:], [xt.b], [], final=True)


def build_OUT0(T, k=None):
    TG = min(512, T)
    NG = T // TG
    NTG = TG // 128
    k = k or K()
    x = k.din("x", [T, D])
    attn = k.din("attn", [T, 1024], BF16)
    ygT = k.din("ygT", [8, 128, T], BF16)
    wglu = k.din("wglu", [8, 128, 8, 128])
    wout = k.din("wout", [4, 128, 16, 512])
    g1 = k.din("g1", [1, D])
    x1 = k.dout("x1", [T, D])

    gain1 = k.sb("gain1", [128, D]); k.dma("sp", gain1.t[:], g1.partition_broadcast(128), [], [gain1.b])
    wgl = k.sb("wgl", [128, 8, 8, 128], BF16)
    for cb in range(8):
        k.dma("pool", wgl.t[:, cb], wglu[cb], [], [wgl.b])
    wo = [k.sb(f"wo{cg}", [128, 16, 512], BF16) for cg in range(4)]
    for cg in range(4):
        k.dma("pool", wo[cg].t[:], wout[cg], [], [wo[cg].b])
    identb = k.make_ident(BF16, "identb")
    att_t = k.sbn("att_t", [128, 1024], BF16, 2)
    ptb = k.ps("ptb", [128, 1024], BF16)
    mixT = k.sbn("mixT", [128, 16, TG], BF16, 2)
    ygs = k.sbn("ygs", [128, 8, TG], BF16, 2)
    sg = k.sbn("sg", [128, TG], F32, 2)
    acc = [k.ps(f"acc{i}", [128, 512], F32) for i in range(2)]
    big = k.ps("big", [128, D], F32)
    xs = k.sbn("xs", [128, D], F32, 2)
    hb = k.sb("hb", [128, D], F32)
    junk = k.sb("junk", [128, D], BF16)
    ss = k.sb("ss", [128, 1]); rstd = k.sb("rstd", [128, 1])
    for g in range(NG):
        mx = mixT[g % 2]; yg = ygs[g % 2]
        for i in range(NTG):
            row = (g * NTG + i) * 128
            at = att_t[i % 2]
            k.dma("sp", at.t[:], attn[row:row + 128, :], [], [at.b])
            for c_ in range(8):
                k.tr(ptb.t[:, c_ * 128:(c_ + 1) * 128], at.t[:, c_ * 128:(c_ + 1) * 128], identb.t[:],
                     [at.b, identb.b], [ptb.b])
            k.copy("act", mx.t[:, 0:8, i * 128:(i + 1) * 128], ptb.t[:].rearrange("p (c t) -> p c t", c=8),
                   [ptb.b], [mx.b])
        k.dma("sp", yg.t[:], ygT[:, :, g * TG:(g + 1) * TG].rearrange("c p t -> p c t"), [], [yg.b])
        for cb in range(8):
            pa = acc[cb % 2]
            for kc in range(8):
                k.mm(pa.t[:, 0:TG], wgl.t[:, cb, kc, :], yg.t[:, kc, :], kc == 0, kc == 7, [wgl.b, yg.b], [pa.b])
            s = sg[cb % 2]
            k.act(s.t[:, 0:TG], pa.t[:, 0:TG], AF.Sigmoid, [pa.b], [s.b])
            k.tt("dve", mx.t[:, 8 + cb, :], s.t[:, 0:TG], yg.t[:, cb, :], ALU.mult, [s.b, yg.b], [mx.b])
        for i in range(NTG):
            row = (g * NTG + i) * 128
            out_proj_tile(k, mx, [mx.b], i * 128, wo, big, xs[i % 2], hb, junk, ss, rstd, gain1, x, x1, row)
    return k.done()


def host_OUT_weights(w_glu, w_out):
    wglu = None
    if w_glu is not None:
        wglu = np.ascontiguousarray(w_glu.reshape(8, 128, 8, 128).transpose(2, 1, 0, 3))
    wout = np.ascontiguousarray(w_out.reshape(16, 128, 4, 512).transpose(2, 1, 0, 3))
    return wglu, wout


LW_SCALE = -math.exp(-0.5)


def build_RPROJ(T, k=None):
    TG = min(512, T)
    NG = T // TG
    NTG = TG // 128
    HW = TG + 128
    k = k or K()
    x1h = k.din("x1h", [T + 128, D])
    g0 = k.din("g0", [1, D])
    mucol_d = k.din("mucol", [128, 6, 16])
    wr_d = k.din("wr", [4, 128, 16, 512]); wk_d = k.din("wk", [4, 128, 16, 512]); wv_d = k.din("wv", [4, 128, 16, 512])
    w1_d = k.din("w1", [128, 16, 96]); a1_d = k.din("a1", [128, 16, 96]); g1_d = k.din("g1w", [128, 16, 256])
    w2_d = k.din("w2", [96, D]); a2_d = k.din("a2", [96, D]); g2_d = k.din("g2w", [128, 2, D])
    w0_d = k.din("w0", [1, D]); a0_d = k.din("a0", [1, D]); kk_d = k.din("k_k", [1, D]); ka_d = k.din("k_a", [1, D])
    outs = {n: k.dout(n, [T, D]) for n in ("r", "lw", "kp", "v", "kkn", "bb", "g")}

    ident = k.make_ident(F32, "identf")
    gain = k.sb("gain", [128, D]); k.dma("sp", gain.t[:], g0.partition_broadcast(128), [], [gain.b])
    kkrow = k.sb("kkrow", [128, D]); k.dma("sp", kkrow.t[:], kk_d.partition_broadcast(128), [], [kkrow.b])
    karow = k.sb("karow", [128, D]); k.dma("sp", karow.t[:], ka_d.partition_broadcast(128), [], [karow.b])
    mu = k.sb("mu", [128, 6, 16]); k.dma("sp", mu.t[:], mucol_d[:, :, :], [], [mu.b])
    omu = k.sb("omu", [128, 6, 16])
    k.ts("dve", omu.t[:], mu.t[:], -1.0, 1.0, ALU.mult, ALU.add, [mu.b], [omu.b])
    w1s = k.sb("w1s", [128, 16, 96], BF16); k.dma("pool", w1s.t[:], w1_d[:, :, :], [], [w1s.b])
    a1s = k.sb("a1s", [128, 16, 96], BF16); k.dma("pool", a1s.t[:], a1_d[:, :, :], [], [a1s.b])
    g1s = k.sb("g1s", [128, 16, 256], BF16); k.dma("pool", g1s.t[:], g1_d[:, :, :], [], [g1s.b])
    w2s = k.sb("w2s", [96, D], BF16); k.dma("pool", w2s.t[:], w2_d[:, :], [], [w2s.b])
    a2s = k.sb("a2s", [96, D], BF16); k.dma("pool", a2s.t[:], a2_d[:, :], [], [a2s.b])
    g2s = k.sb("g2s", [128, 2, D], BF16); k.dma("pool", g2s.t[:], g2_d[:, :, :], [], [g2s.b])
    ones1 = k.sb("ones1", [33, 128], BF16); k.memset("pool", ones1.t[:], 1.0, [ones1.b])
    xs = k.sb("xs", [128, D]); hb = k.sb("hb", [128, D])
    HI = k.sb("biasHI", [33, D], BF16); LO = k.sb("biasLO", [33, D], BF16)
    bias = {}
    for nm, d_, p_ in (("w0", w0_d, 0), ("a0", a0_d, 32)):
        k.dma("sp", xs.t[p_:p_ + 1, :], d_[:, :], [], [xs.b])
        k.copy("dve", HI.t[p_:p_ + 1, :], xs.t[p_:p_ + 1, :], [xs.b], [HI.b])
        k.tt("dve", LO.t[p_:p_ + 1, :], xs.t[p_:p_ + 1, :], HI.t[p_:p_ + 1, :], ALU.subtract, [xs.b, HI.b], [LO.b])
        bias[nm] = p_

    junk = hb
    ss = k.sb("ss", [128, 1]); rstd = k.sb("rstd", [128, 1])
    hT = k.sb("hT", [128, 16, HW], F32)
    xj = k.sb("xj", [128, 16, TG], BF16)
    tmp = k.sbn("tmp", [128, TG], F32, 2)
    twT = k.sb("twT", [96, TG], BF16); taT = k.sb("taT", [96, TG], BF16); tgT = k.sb("tgT", [128, 2, TG], BF16)
    Wb = k.sbn("Wb", [128, 16, 512], BF16, 2)
    big = k.ps("big", [128, D], F32)
    acc = [k.ps(f"acc{i}", [128, 512], F32) for i in range(4)]
    st = {n: k.sbn("st_" + n, [128, 512], F32, 2) for n in ("o", "a", "k", "kkn", "bb", "kp")}
    for n in ("kk", "sq", "am"):
        t1_ = k.sb("st_" + n, [128, 512], F32)
        st[n] = [t1_, t1_]
    sm = {n: k.sbn("sm_" + n, [128, 8], F32, 2) for n in ("ssq", "rn")}
    cnt = [0]

    def mix(j):
        for kc in range(16):
            t_ = tmp[kc % 2]
            k.act(t_.t[:, 0:TG], hT.t[:, kc, 127:127 + TG], AF.Copy, [hT.b, mu.b], [t_.b], scale=mu.t[:, j, kc:kc + 1])
            k.stt(xj.t[:, kc, :], hT.t[:, kc, 128:128 + TG], omu.t[:, j, kc:kc + 1], t_.t[:, 0:TG], ALU.mult, ALU.add,
                  [hT.b, omu.b, t_.b], [xj.b])

    def emit_out(name, row, cg, src):
        k.dma("sp", outs[name][row:row + 128, cg * 512:(cg + 1) * 512], src.t[:], [src.b], [], final=True)

    for g in range(NG):
        for i in range(NTG + 1):
            row = g * TG + i * 128
            k.dma("sp", xs.t[:], x1h[row:row + 128, :], [], [xs.b])
            rms_stats(k, xs.t[:], junk, ss, rstd, 1e-6, D, [xs.b])
            k.stt(hb.t[:], xs.t[:], rstd.t[:, 0:1], gain.t[:], ALU.mult, ALU.mult, [xs.b, rstd.b, gain.b], [hb.b])
            for kc in range(16):
                k.tr(big.t[:, kc * 128:(kc + 1) * 128], hb.t[:, kc * 128:(kc + 1) * 128], ident.t[:],
                     [hb.b, ident.b], [big.b])
            k.copy("act", hT.t[:, :, i * 128:(i + 1) * 128], big.t[:].rearrange("p (c t) -> p c t", c=16),
                   [big.b], [hT.b])
        mix(1)
        pa = acc[0]
        for kc in range(16):
            k.mm(pa.t[0:96, 0:TG], w1s.t[:, kc, :], xj.t[:, kc, :], kc == 0, kc == 15, [w1s.b, xj.b], [pa.b])
        k.act(twT.t[:, :], pa.t[0:96, 0:TG], AF.Tanh, [pa.b], [twT.b])
        mix(4)
        pa = acc[1]
        for kc in range(16):
            k.mm(pa.t[0:96, 0:TG], a1s.t[:, kc, :], xj.t[:, kc, :], kc == 0, kc == 15, [a1s.b, xj.b], [pa.b])
        k.copy("act", taT.t[:, :], pa.t[0:96, 0:TG], [pa.b], [taT.b])
        mix(5)
        for c in range(2):
            pa = acc[2 + c]
            for kc in range(16):
                k.mm(pa.t[:, 0:TG], g1s.t[:, kc, c * 128:(c + 1) * 128], xj.t[:, kc, :], kc == 0, kc == 15,
                     [g1s.b, xj.b], [pa.b])
            k.act(tgT.t[:, c, :], pa.t[:, 0:TG], AF.Sigmoid, [pa.b], [tgT.b])
        for i in range(NTG):
            row = g * TG + i * 128
            for cg in range(4):
                c_ = cnt[0]; cnt[0] += 1
                pa = acc[c_ % 4]
                cs = slice(cg * 512, (cg + 1) * 512)
                k.mm(pa.t[:], twT.t[:, i * 128:(i + 1) * 128], w2s.t[:, cs], True, False, [twT.b, w2s.b], [pa.b])
                k.mm(pa.t[:], ones1.t[0:1, :], HI.t[0:1, cs], False, False, [ones1.b, HI.b], [pa.b])
                k.mm(pa.t[:], ones1.t[0:1, :], LO.t[0:1, cs], False, True, [ones1.b, LO.b], [pa.b])
                o = st["o"][c_ % 2]
                k.act(o.t[:], pa.t[:], AF.Sigmoid, [pa.b], [o.b])
                k.ts("dve", o.t[:], o.t[:], LW_SCALE, None, ALU.mult, None, [o.b], [o.b])
                emit_out("lw", row, cg, o)
                c_ = cnt[0]; cnt[0] += 1
                pa = acc[c_ % 4]
                for c in range(2):
                    k.mm(pa.t[:], tgT.t[:, c, i * 128:(i + 1) * 128], g2s.t[:, c, cs], c == 0, c == 1,
                         [tgT.b, g2s.b], [pa.b])
                o = st["o"][c_ % 2]
                k.copy("act", o.t[:], pa.t[:], [pa.b], [o.b])
                emit_out("g", row, cg, o)
        for j, wd_, nm in ((0, wr_d, "r"), (3, wv_d, "v")):
            mix(j)
            for cg in range(4):
                wt = Wb[cg % 2]
                k.dma("pool", wt.t[:], wd_[cg], [], [wt.b])
                for i in range(NTG):
                    row = g * TG + i * 128
                    c_ = cnt[0]; cnt[0] += 1
                    pa = acc[c_ % 4]
                    for kc in range(16):
                        k.mm(pa.t[:], xj.t[:, kc, i * 128:(i + 1) * 128], wt.t[:, kc, :], kc == 0, kc == 15,
                             [xj.b, wt.b], [pa.b])
                    o = st["o"][c_ % 2]
                    k.copy("act" if c_ % 2 else "dve", o.t[:], pa.t[:], [pa.b], [o.b])
                    emit_out(nm, row, cg, o)
        mix(2)
        for cg in range(4):
            wt = Wb[cg % 2]
            k.dma("pool", wt.t[:], wk_d[cg], [], [wt.b])
            cs = slice(cg * 512, (cg + 1) * 512)
            for i in range(NTG):
                row = g * TG + i * 128
                c_ = cnt[0]; cnt[0] += 1
                p = c_ % 2
                pk = acc[(c_ % 2) * 2]; pA = acc[(c_ % 2) * 2 + 1]
                for kc in range(16):
                    k.mm(pk.t[:], xj.t[:, kc, i * 128:(i + 1) * 128], wt.t[:, kc, :], kc == 0, kc == 15,
                         [xj.b, wt.b], [pk.b])
                k.mm(pA.t[:], taT.t[:, i * 128:(i + 1) * 128], a2s.t[:, cs], True, False, [taT.b, a2s.b], [pA.b])
                k.mm(pA.t[:], ones1.t[32:33, :], HI.t[32:33, cs], False, False, [ones1.b, HI.b], [pA.b])
                k.mm(pA.t[:], ones1.t[32:33, :], LO.t[32:33, cs], False, True, [ones1.b, LO.b], [pA.b])
                a_ = st["a"][p]; k_ = st["k"][p]; kk_ = st["kk"][p]; sq_ = st["sq"][p]; kkn_ = st["kkn"][p]
                bb_ = st["bb"][p]; am_ = st["am"][p]; kp_ = st["kp"][p]; ssq = sm["ssq"][p]; rn = sm["rn"][p]
                k.act(a_.t[:], pA.t[:], AF.Sigmoid, [pA.b], [a_.b])
                k.copy("act", k_.t[:], pk.t[:], [pk.b], [k_.b])
                k.tt("dve", kk_.t[:], k_.t[:], kkrow.t[:, cs], ALU.mult, [k_.b, kkrow.b], [kk_.b])
                k.tt("dve", sq_.t[:], kk_.t[:], kk_.t[:], ALU.mult, [kk_.b], [sq_.b])
                k.reduce(ssq.t[:], sq_.t[:].rearrange("p (h c) -> p h c", h=8), ALU.add, [sq_.b], [ssq.b])
                k.act(rn.t[:], ssq.t[:], AF.Sqrt, [ssq.b], [rn.b])
                k.ts("dve", rn.t[:], rn.t[:], 1e-12, None, ALU.max, None, [rn.b], [rn.b])
                k.recip(rn.t[:], rn.t[:], [rn.b], [rn.b])
                k.tt("dve", kkn_.t[:].rearrange("p (h c) -> p h c", h=8), kk_.t[:].rearrange("p (h c) -> p h c", h=8),
                     rn.t[:].unsqueeze(2).to_broadcast([128, 8, 64]), ALU.mult, [kk_.b, rn.b], [kkn_.b])
                emit_out("kkn", row, cg, kkn_)
                k.tt("dve", bb_.t[:], kkn_.t[:], a_.t[:], ALU.mult, [kkn_.b, a_.b], [bb_.b])
                emit_out("bb", row, cg, bb_)
                k.stt(am_.t[:], a_.t[:], -1.0, karow.t[:, cs], ALU.add, ALU.mult, [a_.b, karow.b], [am_.b])
                k.stt(kp_.t[:], am_.t[:], 1.0, k_.t[:], ALU.add, ALU.mult, [am_.b, k_.b], [kp_.b])
                emit_out("kp", row, cg, kp_)
    return k.done()


def host_RPROJ_weights(mu, w_r, w_k, w_v, w1, a1, g1, w2, a2, g2):
    def big(w):
        return np.ascontiguousarray(w.reshape(16, 128, 4, 512).transpose(2, 1, 0, 3))
    def s1(w):
        return np.ascontiguousarray(w.reshape(16, 128, -1).transpose(1, 0, 2))
    mucol = np.ascontiguousarray(mu.reshape(6, 16, 128).transpose(2, 0, 1))
    g2w = np.ascontiguousarray(g2.reshape(2, 128, D).transpose(1, 0, 2))
    return dict(mucol=mucol, wr=big(w_r), wk=big(w_k), wv=big(w_v), w1=s1(w1), a1=s1(a1), g1w=s1(g1),
                w2=np.ascontiguousarray(w2), a2=np.ascontiguousarray(a2), g2w=g2w)


def build_RSCAN(NCH, NS=8, k=None, passes=None):
    k = k or K()
    if passes is None:
        passes = [dict(din={n: k.din(n, [NCH, 128, NS * 64]) for n in ("r", "lw", "kp", "v", "kkn", "bb")},
                       y=k.dout("y", [NCH, 128, NS * 64]))]
    tri_d = k.din("tri", [128, 128])
    mst_d = k.din("m_strict_T", [128, 512])
    mit_d = k.din("m_incl_T", [128, 512])
    mlo_d = k.din("m_lo", [128, 512])
    HS = NS // 2
    W = HS * 64
    assert HS == 4

    ident = k.make_ident(F32, "identf")
    identb = k.make_ident(BF16, "identb")
    tri = k.sb("tri", [128, 128]); k.dma("sp", tri.t[:], tri_d[:, :], [], [tri.b])
    mst = k.sb("mst", [128, 512]); k.dma("sp", mst.t[:], mst_d[:, :], [], [mst.b])
    mit = k.sb("mit", [128, 512]); k.dma("sp", mit.t[:], mit_d[:, :], [], [mit.b])
    mlo = k.sb("mlo", [128, 512]); k.dma("sp", mlo.t[:], mlo_d[:, :], [], [mlo.b])
    onesq = k.sb("onesq", [128, 128]); k.memset("pool", onesq.t[:], 1.0, [onesq.b])

    def hs(h):
        return slice(h * 64, (h + 1) * 64)

    def v3(ap):
        return ap.rearrange("p (h t) -> p h t", h=HS)

    def r32(ap):
        return ap.bitcast(F32R)

    inp = {n: k.sbn("i_" + n, [128, 2 * W], F32, 2) for n in ("r", "lw", "kp", "v", "kkn", "bb")}
    Ysh = k.sbn("Ysh", [128, 2 * W], F32, 2)

    def stream(sid):
        sx = f"_{sid}"
        e = {n: k.sb("e_" + n + sx, [128, W]) for n in ("cum", "t1", "t2", "G", "Gi", "Gp", "Gr", "Y")}
        eb = {n: k.sb("eb_" + n + sx, [128, W], F32) for n in ("AH", "BC", "KC", "RH", "BT", "KT", "V", "W0", "U")}
        gC = k.sb("gC" + sx, [64, HS])
        ST = k.sb("ST" + sx, [64, W]); STs = k.sb("STs" + sx, [64, W]); STb = ST
        tT = {n: k.sb("T_" + n + sx, [64, HS, 128], F32) for n in ("AH", "BC", "KC", "RH")}
        A = {n: k.sb("A_" + n + sx, [128, HS, 128], F32) for n in ("abT", "rbT", "akT", "rkT", "ab")}
        Xp = [k.sb(f"X{i}" + sx, [128, HS, 128], F32) for i in range(2)]
        XTp = [k.sb(f"XT{i}" + sx, [128, HS, 128], F32) for i in range(2)]
        TT = k.sb("TT" + sx, [128, HS, 128]); TTb = TT
        pbig = [k.ps(f"pbig{i}" + sx, [128, HS * 128], F32) for i in range(2)]
        ptr = k.ps("ptr" + sx, [128, HS * 128], F32)
        pD = k.ps("pD" + sx, [128, 512], F32)
        pE = pD
        nb = [0]

        def nextbig():
            nb[0] += 1
            return pbig[nb[0] % len(pbig)]

        for P_ in passes:
            cs = slice(sid * W, (sid + 1) * W)
            k.memset("pool", ST.t[:], 0.0, [ST.b])
            for c in range(NCH):
                I = {n: Tile(inp[n][c % 2].t[:, cs], inp[n][c % 2].b) for n in inp}
                if sid == 0:
                    for n in ("lw", "kkn", "bb", "kp", "r", "v"):
                        k.dma("sp", inp[n][c % 2].t[:], P_["din"][n][c], [], [inp[n][c % 2].b])
                LW = I["lw"]
                k.mm(pD.t[:, 0:W], tri.t[:], LW.t[:], True, True, [tri.b, LW.b], [pD.b])
                pt = nextbig()
                k.mm(pt.t[:, 0:W], onesq.t[:], LW.t[:], True, True, [onesq.b, LW.b], [pt.b])
                for h in range(HS):
                    k.mm(pt.t[0:64, W + h:W + h + 1], LW.t[:, hs(h)], onesq.t[:, 0:1], True, True, [LW.b, onesq.b], [pt.b])
                yield
                k.act(gC.t[:], pt.t[0:64, W:W + HS], AF.Exp, [pt.b], [gC.b])
                k.copy("act", e["cum"].t[:], pD.t[:, 0:W], [pD.b], [e["cum"].b])
                k.act(e["G"].t[:], pD.t[:, 0:W], AF.Exp, [pD.b], [e["G"].b])
                k.act(e["Gi"].t[:], pD.t[:, 0:W], AF.Exp, [pD.b], [e["Gi"].b], scale=-1.0)
                k.tt("dve", e["t1"].t[:], e["cum"].t[:], LW.t[:], ALU.subtract, [e["cum"].b, LW.b], [e["t1"].b])
                k.act(e["Gp"].t[:], e["t1"].t[:], AF.Exp, [e["t1"].b], [e["Gp"].b])
                k.tt("dve", e["t2"].t[:], pt.t[:, 0:W], e["cum"].t[:], ALU.subtract, [pt.b, e["cum"].b], [e["t2"].b])
                k.act(e["Gr"].t[:], e["t2"].t[:], AF.Exp, [e["t2"].b], [e["Gr"].b])
                yield
                k.stt(r32(eb["AH"].t[:]), I["kkn"].t[:], -1.0, e["Gp"].t[:], ALU.mult, ALU.mult, [I["kkn"].b, e["Gp"].b], [eb["AH"].b])
                k.tt("dve", r32(eb["BC"].t[:]), I["bb"].t[:], e["Gi"].t[:], ALU.mult, [I["bb"].b, e["Gi"].b], [eb["BC"].b])
                k.tt("dve", r32(eb["KC"].t[:]), I["kp"].t[:], e["Gi"].t[:], ALU.mult, [I["kp"].b, e["Gi"].b], [eb["KC"].b])
                k.tt("dve", r32(eb["RH"].t[:]), I["r"].t[:], e["G"].t[:], ALU.mult, [I["r"].b, e["G"].b], [eb["RH"].b])
                k.tt("dve", eb["BT"].t[:], I["bb"].t[:], e["Gr"].t[:], ALU.mult, [I["bb"].b, e["Gr"].b], [eb["BT"].b])
                k.tt("dve", eb["KT"].t[:], I["kp"].t[:], e["Gr"].t[:], ALU.mult, [I["kp"].b, e["Gr"].b], [eb["KT"].b])
                k.copy("act", r32(eb["V"].t[:]), I["v"].t[:], [I["v"].b], [eb["V"].b])
                yield
                for n in ("AH", "BC", "KC", "RH"):
                    for h in range(HS):
                        k.tr(ptr.t[0:64, h * 128:(h + 1) * 128], eb[n].t[:, hs(h)], ident.t[:], [eb[n].b, ident.b], [ptr.b])
                    k.copy("act", r32(tT[n].t[:]), v3(ptr.t[0:64, :]), [ptr.b], [tT[n].b])
                    yield
                specs = (("abT", "BC", "AH", mst), ("ab", "AH", "BC", mlo), ("akT", "KC", "AH", mst),
                         ("rbT", "BC", "RH", mit), ("rkT", "KC", "RH", mit))
                for an, ln, rn, msk in specs:
                    pt = nextbig()
                    for h in range(HS):
                        k.mm(pt.t[:, h * 128:(h + 1) * 128], r32(tT[ln].t[:, h, :]), r32(tT[rn].t[:, h, :]), True, True,
                             [tT[ln].b, tT[rn].b], [pt.b])
                    k.tt("dve", r32(A[an].t[:]), v3(pt.t[:]), v3(msk.t[:]), ALU.mult, [pt.b, msk.b], [A[an].b])
                    yield
                k.tt("dve", r32(TT.t[:]), A["abT"].t[:], ident.t[:].unsqueeze(1).to_broadcast([128, HS, 128]), ALU.add,
                     [A["abT"].b, ident.b], [TT.b])
                X, XT = A["ab"], A["abT"]
                for step in range(6):
                    Xn, XTn = Xp[step % 2], XTp[step % 2]
                    pX = nextbig()
                    for h in range(HS):
                        k.mm(pX.t[:, h * 128:(h + 1) * 128], r32(XT.t[:, h, :]), r32(X.t[:, h, :]), True, True, [XT.b, X.b], [pX.b])
                    k.copy("act", r32(Xn.t[:]), v3(pX.t[:]), [pX.b], [Xn.b])
                    yield
                    if step < 5:
                        pXT = nextbig()
                        for h in range(HS):
                            k.mm(pXT.t[:, h * 128:(h + 1) * 128], r32(X.t[:, h, :]), r32(XT.t[:, h, :]), True, True, [XT.b, X.b], [pXT.b])
                        k.copy("act", r32(XTn.t[:]), v3(pXT.t[:]), [pXT.b], [XTn.b])
                        yield
                    pT = nextbig()
                    for h in range(HS):
                        k.mm(pT.t[:, h * 128:(h + 1) * 128], r32(Xn.t[:, h, :]), r32(TT.t[:, h, :]), True, True, [Xn.b, TT.b], [pT.b])
                    k.tt("dve", r32(TT.t[:]), TT.t[:], v3(pT.t[:]), ALU.add, [TT.b, pT.b], [TT.b])
                    yield
                    X, XT = Xn, XTn
                V = eb["V"]
                for h in range(HS):
                    k.mm(pD.t[:, hs(h)], r32(A["akT"].t[:, h, :]), r32(V.t[:, hs(h)]), True, False, [A["akT"].b, V.b], [pD.b])
                    k.mm(pD.t[:, hs(h)], r32(tT["AH"].t[:, h, :]), r32(ST.t[:, hs(h)]), False, True, [tT["AH"].b, ST.b], [pD.b])
                k.copy("act", r32(eb["W0"].t[:]), pD.t[:, 0:W], [pD.b], [eb["W0"].b])
                yield
                for h in range(HS):
                    k.mm(pE.t[:, hs(h)], r32(TT.t[:, h, :]), r32(eb["W0"].t[:, hs(h)]), True, True, [TT.b, eb["W0"].b], [pE.b])
                k.copy("act", r32(eb["U"].t[:]), pE.t[:, 0:W], [pE.b], [eb["U"].b])
                yield
                for h in range(HS):
                    k.mm(pD.t[:, hs(h)], r32(tT["RH"].t[:, h, :]), r32(ST.t[:, hs(h)]), True, False, [tT["RH"].b, ST.b], [pD.b])
                    k.mm(pD.t[:, hs(h)], r32(A["rbT"].t[:, h, :]), r32(eb["U"].t[:, hs(h)]), False, False, [A["rbT"].b, eb["U"].b], [pD.b])
                    k.mm(pD.t[:, hs(h)], r32(A["rkT"].t[:, h, :]), r32(V.t[:, hs(h)]), False, True, [A["rkT"].b, V.b], [pD.b])
                k.copy("act", Ysh[c % 2].t[:, cs], pD.t[:, 0:W], [pD.b], [Ysh[c % 2].b])
                if sid == 1:
                    k.dma("sp", P_["y"][c], Ysh[c % 2].t[:], [Ysh[c % 2].b], [], final=True)
                yield
                for h in range(HS):
                    k.mm(pE.t[0:64, hs(h)], eb["BT"].t[:, hs(h)], eb["U"].t[:, hs(h)], True, False, [eb["BT"].b, eb["U"].b], [pE.b])
                    k.mm(pE.t[0:64, hs(h)], eb["KT"].t[:, hs(h)], V.t[:, hs(h)], False, True, [eb["KT"].b, V.b], [pE.b])
                k.tt("dve", STs.t[:].rearrange("p (h c) -> p h c", h=HS), ST.t[:].rearrange("p (h c) -> p h c", h=HS),
                     gC.t[:].unsqueeze(2).to_broadcast([64, HS, 64]), ALU.mult, [ST.b, gC.b], [STs.b])
                k.tt("dve", r32(ST.t[:]), STs.t[:], pE.t[0:64, 0:W], ALU.add, [STs.b, pE.b], [ST.b])
                yield

    gens = [stream(0), stream(1)]
    while gens:
        for g_ in list(gens):
            try:
                next(g_)
            except StopIteration:
                gens.remove(g_)
    return k.done()


def host_RSCAN_consts():
    s = np.arange(128)[:, None]
    t = np.arange(128)[None, :]
    tri = (s <= t).astype(np.float32)
    mst = np.tile((s < t).astype(np.float32), (1, 4))
    mit = np.tile((s <= t).astype(np.float32), (1, 4))
    mlo = np.tile((t < s).astype(np.float32), (1, 4))
    return dict(tri=tri, m_strict_T=mst, m_incl_T=mit, m_lo=mlo)


GN_EPS = 64e-5


def build_OUT1(T, k=None, gather=None):
    NT = T // 128
    k = k or K()
    x1 = k.din("x1", [T, D])
    dins = {n: k.din(n, [T, D]) for n in ("y", "r", "kp", "v", "g")}
    lnw_d = k.din("ln_w", [1, D]); lnb_d = k.din("ln_b", [1, D]); rk_d = k.din("r_k", [1, D]); g1 = k.din("g1", [1, D])
    wout = k.din("wout", [4, 128, 16, 512])
    x2 = k.dout("x2", [T, D])

    ident = k.make_ident(F32, "identf")
    rows = {}
    for n, d_ in (("lnw", lnw_d), ("lnb", lnb_d), ("rk", rk_d), ("gain1", g1)):
        rows[n] = k.sb("row_" + n, [128, D]); k.dma("sp", rows[n].t[:], d_.partition_broadcast(128), [], [rows[n].b])
    wo = [k.sb(f"wo{cg}", [128, 16, 512], BF16) for cg in range(4)]
    for cg in range(4):
        k.dma("pool", wo[cg].t[:], wout[cg], [], [wo[cg].b])
    tl = {n: k.sb("t_" + n, [128, D]) for n in ("y", "r", "kp", "v", "g", "sq")}
    s32 = {n: k.sb("s_" + n, [128, 32]) for n in ("mean", "var", "s")}
    mixT = k.sbn("mixT", [128, 16, 128], BF16, 2)
    big = k.ps("big", [128, D], F32)
    xs = k.sb("xs", [128, D]); hb = k.sb("hb", [128, D]); junk = k.sb("junk", [128, D], BF16)
    ss = k.sb("ss", [128, 1]); rstd = k.sb("rstd", [128, 1])

    def v3(t_):
        return t_.t[:].rearrange("p (h c) -> p h c", h=32)

    def b3(t_):
        return t_.t[:].unsqueeze(2).to_broadcast([128, 32, 64])

    if gather is not None:
        gidx = k.sb("gidx", [128, NT], mybir.dt.int32)
        k.dma("sp", gidx.t[:], gather["idx"], [], [gidx.b])

        def gload(dst, src, i, eoff=0):
            k.S.op("pool", lambda e: e.indirect_dma_start(
                out=dst.t[:], out_offset=None, in_=src,
                in_offset=bass.IndirectOffsetOnAxis(ap=gidx.t[:, i:i + 1], axis=0), element_offset=eoff),
                [gidx.b], [dst.b], dma=True)
    for i in range(NT):
        row = i * 128
        for n in ("y", "r", "kp", "v", "g"):
            if gather is None:
                k.dma("sp", tl[n].t[:], dins[n][row:row + 128, :], [], [tl[n].b])
            else:
                gload(tl[n], dins[n], i)
        Y, R, KP, V, G, SQ = (tl[n] for n in ("y", "r", "kp", "v", "g", "sq"))
        k.reduce(s32["mean"].t[:], v3(Y), ALU.add, [Y.b], [s32["mean"].b])
        k.ts("dve", s32["mean"].t[:], s32["mean"].t[:], 1.0 / 64, None, ALU.mult, None, [s32["mean"].b], [s32["mean"].b])
        k.tt("dve", v3(Y), v3(Y), b3(s32["mean"]), ALU.subtract, [Y.b, s32["mean"].b], [Y.b])
        k.tt("dve", SQ.t[:], Y.t[:], Y.t[:], ALU.mult, [Y.b], [SQ.b])
        k.reduce(s32["var"].t[:], v3(SQ), ALU.add, [SQ.b], [s32["var"].b])
        k.act(s32["var"].t[:], s32["var"].t[:], AF.Sqrt, [s32["var"].b], [s32["var"].b], scale=1.0 / 64,
              bias=k.eps_tile(GN_EPS))
        k.recip(s32["var"].t[:], s32["var"].t[:], [s32["var"].b], [s32["var"].b])
        k.tt("dve", v3(Y), v3(Y), b3(s32["var"]), ALU.mult, [Y.b, s32["var"].b], [Y.b])
        k.tt("dve", Y.t[:], Y.t[:], rows["lnw"].t[:], ALU.mult, [Y.b, rows["lnw"].b], [Y.b])
        k.tt("dve", Y.t[:], Y.t[:], rows["lnb"].t[:], ALU.add, [Y.b, rows["lnb"].b], [Y.b])
        k.tt("dve", R.t[:], R.t[:], KP.t[:], ALU.mult, [R.b, KP.b], [R.b])
        k.tt("dve", R.t[:], R.t[:], rows["rk"].t[:], ALU.mult, [R.b, rows["rk"].b], [R.b])
        k.reduce(s32["s"].t[:], v3(R), ALU.add, [R.b], [s32["s"].b])
        k.tt("dve", v3(V), v3(V), b3(s32["s"]), ALU.mult, [V.b, s32["s"].b], [V.b])
        k.tt("dve", Y.t[:], Y.t[:], V.t[:], ALU.add, [Y.b, V.b], [Y.b])
        k.tt("dve", Y.t[:], Y.t[:], G.t[:], ALU.mult, [Y.b, G.b], [Y.b])
        mx = mixT[i % 2]
        for kc in range(16):
            k.tr(big.t[:, kc * 128:(kc + 1) * 128], Y.t[:, kc * 128:(kc + 1) * 128], ident.t[:], [Y.b, ident.b], [big.b])
        k.copy("act", mx.t[:], big.t[:].rearrange("p (c t) -> p c t", c=16), [big.b], [mx.b])
        xl = None if gather is None else (lambda xt, i=i: gload(xt, x1, i, gather["x1_eoff"]))
        out_proj_tile(k, mx, [mx.b], 0, wo, big, xs, hb, junk, ss, rstd, rows["gain1"], x1, x2, row, xload=xl)
    return k.done()


def build_MEGA(SEQ, stop_after=None):
    T = SEQ
    NQT = SEQ // 512
    ND = 4 * NQT + 3
    NCH = SEQ // 128
    NF = DFF // 128
    k = K()
    k.fused = True
    E = {}

    def ein(name, shape, dt=F32):
        E[name] = k.ext_in(name, shape, dt)
        return E[name]

    x = ein("x", [T, D])
    gains = ein("gains", [8, D])
    ein("wA", [24, 128, 16, 128]); ein("wV", [2, 128, 16, 512])
    ein("qaug8", [8, 4, SEQ], BF16); ein("kaug8", [8, 4, SEQ], BF16); ein("biastab8", [8, 128, ND])
    ein("att_tri", [128, 128], BF16); ein("lq", [1, 256]); ein("subln", [1, 128])
    ein("lre8", [8, 128, 4]); ein("lim8", [8, 128, 4]); ein("ldt8", [8, 128, 4])
    ein("bre8", [8, 4, 128, 16]); ein("bim8", [8, 4, 128, 16])
    ein("ctre8", [8, 4, 128, 128]); ein("ctim8", [8, 4, 128, 128]); ein("dcol8", [8, 128, 1])
    ein("wglu", [8, 128, 8, 128]); ein("wout0", [4, 128, 16, 512])
    for l in range(2):
        ein(f"wg{l}", [NF, 128, 16, 128]); ein(f"wu{l}", [NF, 128, 16, 128]); ein(f"wd{l}", [16, 128, NF, 128])
    ein("mucol", [128, 6, 16])
    for n in ("wr", "wk", "wv"):
        ein(n, [4, 128, 16, 512])
    ein("w1", [128, 16, 96]); ein("a1", [128, 16, 96]); ein("g1w", [128, 16, 256])
    ein("w2", [96, D]); ein("a2", [96, D]); ein("g2w", [128, 2, D])
    for n in ("w0", "a0", "k_k", "k_a", "ln_w", "ln_b", "r_k"):
        ein(n, [1, D])
    ein("sc_tri", [128, 128]); ein("m_strict_T", [128, 512]); ein("m_incl_T", [128, 512]); ein("m_lo", [128, 512])
    ein("wout1", [4, 128, 16, 512])
    TO = T // 4
    out = k.ext_out("out", [TO, D])
    ein("own_idx", [128, TO // 128], mybir.dt.int32)

    qkT = k.dint("i_qkT", [16, 128, T], BF16)
    uT = k.dint("i_uT", [8, 128, T], F32)
    vint = k.dint("i_v", [T, 1024], BF16)
    attn = k.dint("i_attn", [T, 1024], BF16)
    ygT = k.dint("i_ygT", [8, 128, T], BF16)
    x1 = k.dint("i_x1", [T, D])
    x2h = k.dint("i_x2h", [T + 128, D])
    R = {n: k.dint("i_r_" + n, [T, D]) for n in ("r", "lw", "kp", "v", "kkn", "bb", "g")}
    yint = k.dint("i_y", [T, D])
    x3 = k.dint("i_x3", [TO, D])

    def g(i):
        return gains[i:i + 1, :]

    k.begin_phase("a_", dict(x=x, g0=g(0), wA=E["wA"], wV=E["wV"], qkT=qkT, uT=uT, v=vint))
    build_L1(T, k=k)
    k.begin_phase("b_", dict(tri=E["att_tri"], lq=E["lq"], subln=E["subln"]))
    items = [dict(q=qkT[h].rearrange("(m p) t -> m p t", m=2), k=qkT[8 + h].rearrange("(m p) t -> m p t", m=2),
                  v=vint[:, h * 128:(h + 1) * 128], qaug=E["qaug8"][h], kaug=E["kaug8"][h], bt=E["biastab8"][h],
                  out=attn[:, h * 128:(h + 1) * 128], slope=2.0 ** (-(h + 1))) for h in range(8)]
    build_ATT(SEQ, 1, k=k, items=items)
    if stop_after == "ATT":
        return k.finish_fused()
    k.begin_phase("c_", {})
    blocks = [dict(uT=[uT[cb]], out=[ygT[cb]], lre=E["lre8"][cb], lim=E["lim8"][cb], ldt=E["ldt8"][cb],
                   bre=E["bre8"][cb], bim=E["bim8"][cb], ctre=E["ctre8"][cb], ctim=E["ctim8"][cb],
                   dcol=E["dcol8"][cb]) for cb in range(8)]
    build_S5(SEQ, 1, k=k, blocks=blocks)
    k.begin_phase("d_", dict(x=x, attn=attn, ygT=ygT, wglu=E["wglu"], wout=E["wout0"], g1=g(1), x1=x1))
    build_OUT0(T, k=k)
    k.begin_phase("e_", dict(x1=x1, g2=g(2), g3=g(3), wg=E["wg0"], wu=E["wu0"], wd=E["wd0"], x2=x2h[128:128 + T, :]))
    zt = k.sb("zt", [128, D])
    k.memset("pool", zt.t[:], 0.0, [zt.b])
    k.dma("sp", x2h[0:128, :], zt.t[:], [zt.b], [], final=True)
    build_FFN(T, k=k)
    io = dict(x1h=x2h, g0=g(4))
    for n in ("mucol", "wr", "wk", "wv", "w1", "a1", "g1w", "w2", "a2", "g2w", "w0", "a0", "k_k", "k_a"):
        io[n] = E[n]
    io.update(R)
    k.begin_phase("f_", io)
    build_RPROJ(T, k=k)
    if stop_after == "RPROJ":
        return k.finish_fused()
    k.begin_phase("g_", dict(tri=E["sc_tri"], m_strict_T=E["m_strict_T"], m_incl_T=E["m_incl_T"], m_lo=E["m_lo"]))

    def cv(ap, p):
        return ap.rearrange("(c t) d -> c t d", t=128)[:, :, p * 512:(p + 1) * 512]
    passes = [dict(din={n: cv(R[n], p) for n in ("r", "lw", "kp", "v", "kkn", "bb")}, y=cv(yint, p)) for p in range(4)]
    build_RSCAN(NCH, 8, k=k, passes=passes)
    if stop_after == "RSCAN":
        return k.finish_fused()
    k.begin_phase("h_", dict(x1=x2h, y=yint, r=R["r"], kp=R["kp"], v=R["v"], g=R["g"], ln_w=E["ln_w"],
                             ln_b=E["ln_b"], r_k=E["r_k"], g1=g(5), wout=E["wout1"], x2=x3))
    build_OUT1(TO, k=k, gather=dict(idx=E["own_idx"], x1_eoff=128 * D))
    k.begin_phase("i_", dict(x1=x3, g2=g(6), g3=g(7), wg=E["wg1"], wu=E["wu1"], wd=E["wd1"], x2=out))
    build_FFN(TO, k=k)
    return k.finish_fused()


_CACHE = {}
_STOP_AFTER = None


def _c(a):
    return np.ascontiguousarray(a)


def kernel(x, norm_gains, ev_w_in, ev_lambda_qk, ev_attn_subln, ev_s5_lambda_re,
           ev_s5_lambda_im, ev_s5_log_dt, ev_s5_b_re, ev_s5_b_im, ev_s5_c_re, ev_s5_c_im,
           ev_s5_d, ev_s5_w_glu, ev_w_out, od_mu, od_w_r, od_w_k, od_w_v, od_w0, od_w1,
           od_w2, od_a0, od_a1, od_a2, od_g1, od_g2, od_k_k, od_k_a, od_r_k, od_ln_w,
           od_ln_b, od_w_o, ffn_w_gate, ffn_w_up, ffn_w_down):
    f32 = lambda a: np.asarray(a, dtype=np.float32)
    x = f32(x)
    B, SEQ, _ = x.shape
    if SEQ not in _CACHE:
        _CACHE[SEQ] = build_MEGA(SEQ, stop_after=_STOP_AFTER)
    nc = _CACHE[SEQ]
    W = {}
    W["gains"] = _c(f32(norm_gains).reshape(8, D))
    W["wA"], W["wV"] = host_L1_weights(f32(ev_w_in)[0])
    cons = [host_ATT_consts(h, SEQ) for h in range(8)]
    W["qaug8"] = np.stack([c[0] for c in cons]); W["kaug8"] = np.stack([c[1] for c in cons])
    W["biastab8"] = np.stack([c[2] for c in cons]); W["att_tri"] = cons[0][3]
    W["lq"] = _c(f32(ev_lambda_qk)[0].reshape(1, 256)); W["subln"] = _c(f32(ev_attn_subln)[0].reshape(1, 128))
    s5 = [host_S5_params(c, f32(ev_s5_lambda_re)[0], f32(ev_s5_lambda_im)[0], f32(ev_s5_log_dt)[0],
                         f32(ev_s5_b_re)[0], f32(ev_s5_b_im)[0], f32(ev_s5_c_re)[0], f32(ev_s5_c_im)[0],
                         f32(ev_s5_d)[0]) for c in range(8)]
    for n in ("lre", "lim", "ldt", "bre", "bim", "ctre", "ctim", "dcol"):
        W[n + "8"] = np.stack([d[n] for d in s5])
    W["wglu"], W["wout0"] = host_OUT_weights(f32(ev_s5_w_glu)[0], f32(ev_w_out)[0])
    for l in range(2):
        W[f"wg{l}"], W[f"wu{l}"], W[f"wd{l}"] = host_FFN_weights(f32(ffn_w_gate)[l], f32(ffn_w_up)[l], f32(ffn_w_down)[l])
    W.update(host_RPROJ_weights(f32(od_mu)[0], f32(od_w_r)[0], f32(od_w_k)[0], f32(od_w_v)[0], f32(od_w1)[0],
                                f32(od_a1)[0], f32(od_g1)[0], f32(od_w2)[0], f32(od_a2)[0], f32(od_g2)[0]))
    for n, a in (("w0", od_w0), ("a0", od_a0), ("k_k", od_k_k), ("k_a", od_k_a), ("ln_w", od_ln_w), ("ln_b", od_ln_b),
                 ("r_k", od_r_k)):
        W[n] = _c(f32(a)[0].reshape(1, D))
    rc = host_RSCAN_consts()
    W["sc_tri"] = rc["tri"]; W["m_strict_T"] = rc["m_strict_T"]; W["m_incl_T"] = rc["m_incl_T"]; W["m_lo"] = rc["m_lo"]
    _, W["wout1"] = host_OUT_weights(None, f32(od_w_o)[0])
    CPB = NCORES // B
    in_maps = []
    TO = SEQ // CPB
    for c in range(NCORES):
        d = dict(W)
        d["x"] = _c(x[c // CPB])
        rows = (c % CPB) * TO + np.arange(TO, dtype=np.int32)
        d["own_idx"] = _c(rows.reshape(TO // 128, 128).T)
        in_maps.append(d)
    res = run(nc, in_maps)
    return np.concatenate([res[c]["out"] for c in range(NCORES)], 0).reshape(B, SEQ, D)
```

```python
import math
from contextlib import ExitStack
import numpy as np
import ml_dtypes
import concourse.bass as bass
import concourse.mybir as mybir
from concourse.bass_utils import run_bass_kernel_spmd

F32 = mybir.dt.float32
BF16 = mybir.dt.bfloat16
F32R = mybir.dt.float32r
AF = mybir.ActivationFunctionType
ALU = mybir.AluOpType
AX = mybir.AxisListType
NPBF = ml_dtypes.bfloat16

D = 2048
NCORES = 8
DFF = 5632
NDMA_SEM = 8


class Buf:
    __slots__ = ("name", "w", "r")

    def __init__(self, name):
        self.name = name
        self.w = None
        self.r = []


class Tile:
    __slots__ = ("t", "b")

    def __init__(self, t, b):
        self.t = t
        self.b = b


class Sched:
    def __init__(self, nc):
        self.nc = nc
        self.q = {e: [] for e in ("pe", "dve", "act", "pool", "sp")}
        self.sems = {}
        self.val = {}
        for e in ("pe", "dve", "act", "pool"):
            self._mksem(e)
        for i in range(NDMA_SEM):
            self._mksem(("spd", i))
            self._mksem(("poold", i))
            self._mksem(("actd", i))
        self.dma_rr = {"sp": 0, "pool": 0, "act": 0}
        self.waited = {e: {} for e in self.q}

    def _mksem(self, key):
        name = "s_" + (key if isinstance(key, str) else f"{key[0]}{key[1]}")
        self.sems[key] = self.nc.alloc_semaphore(name=name)
        self.val[key] = 0

    def _deps(self, eng, reads, writes):
        need = {}

        def add(dep):
            if dep is None:
                return
            deng, key, v = dep
            if deng == "pe" and eng == "pe" and key == "pe":
                return
            if need.get(key, 0) < v:
                need[key] = v

        for b in reads:
            add(b.w)
        for b in writes:
            add(b.w)
            for r in b.r:
                add(r)
        out = []
        for key, v in need.items():
            if self.waited[eng].get(key, 0) >= v:
                continue
            self.waited[eng][key] = v
            out.append((key, v))
        return out

    def op(self, eng, fn, reads=(), writes=(), dma=False):
        reads = [b for b in reads if b is not None]
        writes = [b for b in writes if b is not None]
        waits = self._deps(eng, reads, writes)
        if dma:
            qn = {"sp": "spd", "pool": "poold", "act": "actd"}[eng]
            i = self.dma_rr[eng]
            self.dma_rr[eng] = (i + 1) % NDMA_SEM
            key = (qn, i)
            inc = 16
        else:
            key = eng
            inc = 1
        self.val[key] += inc
        v = self.val[key]
        sem = self.sems[key]
        wl = [(self.sems[k], vv) for k, vv in waits]

        def emit(e):
            for s, vv in wl:
                e.wait_ge(s, vv)
            fn(e).then_inc(sem, inc)

        self.q[eng].append(emit)
        tag = (eng, key, v)
        for b in reads:
            b.r.append(tag)
        for b in writes:
            b.w = tag
            b.r = []
        return tag

    def barrier(self):
        snap = [(k_, self.sems[k_], v) for k_, v in self.val.items() if v > 0]
        for eng in self.q:
            wl = []
            for k_, sem, v in snap:
                if self.waited[eng].get(k_, 0) >= v:
                    continue
                self.waited[eng][k_] = v
                wl.append((sem, v))

            def emit(e, wl=wl):
                for sm, v in wl:
                    e.wait_ge(sm, v)
            self.q[eng].append(emit)

    def finish(self, final_bufs):
        need = {}
        for b in final_bufs:
            if b.w is not None:
                _, key, v = b.w
                need[key] = max(need.get(key, 0), v)
        wl = [(self.sems[k], v) for k, v in need.items()]

        def emit(e):
            for s, v in wl:
                e.wait_ge(s, v)
        self.q["sp"].append(emit)

    def emit_all(self):
        nc = self.nc
        q = self.q
        with nc.Block() as block:
            @block.tensor
            def _(e):
                for f in q["pe"]:
                    f(e)

            @block.vector
            def _(e):
                for f in q["dve"]:
                    f(e)

            @block.scalar
            def _(e):
                for f in q["act"]:
                    f(e)

            @block.gpsimd
            def _(e):
                for f in q["pool"]:
                    f(e)

            @block.sync
            def _(e):
                for f in q["sp"]:
                    f(e)


class K:
    def __init__(self, name="k"):
        self.nc = bass.Bass("TRN2", target_bir_lowering=False)
        self.es = ExitStack()
        self.S = Sched(self.nc)
        self.outs = []
        self.n = 0
        self.fused = False
        self.io = {}
        self.prefix = ""
        self.tiles = {}
        self._eps = {}

    def ext_in(self, name, shape, dt=F32):
        return self.nc.dram_tensor(name, list(shape), dt, kind="ExternalInput").ap()

    def ext_out(self, name, shape, dt=F32):
        return self.nc.dram_tensor(name, list(shape), dt, kind="ExternalOutput").ap()

    def dint(self, name, shape, dt=F32):
        return self.nc.dram_tensor(name, list(shape), dt).ap()

    def din(self, name, shape, dt=F32):
        if name in self.io:
            return self.io[name]
        assert not self.fused, name
        return self.ext_in(name, shape, dt)

    def dout(self, name, shape, dt=F32):
        if name in self.io:
            return self.io[name]
        assert not self.fused, name
        return self.ext_out(name, shape, dt)

    def begin_phase(self, prefix, io):
        self.prefix = prefix
        self.io = io
        self.tiles = {}
        self._eps = {}
        self.es = ExitStack()

    def end_phase(self):
        self.S.barrier()
        self.es.close()

    def buf(self, name=None):
        self.n += 1
        return Buf(name or f"b{self.n}")

    def cbuf(self, name):
        key = ("b", name)
        if key not in self.tiles:
            self.tiles[key] = self.buf(name)
        return self.tiles[key]

    def sb(self, name, shape, dt=F32):
        key = ("s", name)
        if key in self.tiles:
            return self.tiles[key]
        t = self.es.enter_context(self.nc.sbuf_tensor("s_" + self.prefix + name, list(shape), dt))
        self.tiles[key] = Tile(t, self.buf(name))
        return self.tiles[key]

    def sbn(self, name, shape, dt, n):
        return [self.sb(f"{name}{i}", shape, dt) for i in range(n)]

    def ps(self, name, shape, dt=F32):
        key = ("p", name)
        if key in self.tiles:
            return self.tiles[key]
        t = self.es.enter_context(self.nc.psum_tensor("p_" + self.prefix + name, list(shape), dt))
        self.tiles[key] = Tile(t, self.buf(name))
        return self.tiles[key]

    def dma(self, eng, out, in_, reads=(), writes=(), final=False):
        tag_b = None
        if final:
            tag_b = self.buf("out")
            self.outs.append(tag_b)
            writes = list(writes) + [tag_b]
        self.S.op(eng, lambda e: e.dma_start(out=out, in_=in_), reads, writes, dma=True)

    def mm(self, out, lhsT, rhs, start, stop, reads, writes, **kw):
        self.S.op("pe", lambda e: e.matmul(out, lhsT=lhsT, rhs=rhs, start=start, stop=stop, **kw),
                  reads, writes)

    def tr(self, out, in_, ident, reads, writes):
        self.S.op("pe", lambda e: e.transpose(out=out, in_=in_, identity=ident), reads, writes)

    def act(self, out, in_, func, reads, writes, **kw):
        self.S.op("act", lambda e: e.activation(out=out, in_=in_, func=func, **kw), reads, writes)

    def tt(self, eng, out, in0, in1, op, reads, writes):
        self.S.op(eng, lambda e: e.tensor_tensor(out=out, in0=in0, in1=in1, op=op), reads, writes)

    def ts(self, eng, out, in0, s1, s2, op0, op1, reads, writes, **kw):
        if op1 is None:
            self.S.op(eng, lambda e: e.tensor_scalar(out=out, in0=in0, scalar1=s1, scalar2=None,
                                                     op0=op0, **kw), reads, writes)
        else:
            self.S.op(eng, lambda e: e.tensor_scalar(out=out, in0=in0, scalar1=s1, scalar2=s2,
                                                     op0=op0, op1=op1, **kw), reads, writes)

    def stt(self, out, in0, scalar, in1, op0, op1, reads, writes):
        self.S.op("dve", lambda e: e.scalar_tensor_tensor(out=out, in0=in0, scalar=scalar, in1=in1,
                                                          op0=op0, op1=op1), reads, writes)

    def copy(self, eng, out, in_, reads, writes):
        if eng == "act":
            self.S.op("act", lambda e: e.copy(out=out, in_=in_), reads, writes)
        else:
            self.S.op(eng, lambda e: e.tensor_copy(out=out, in_=in_), reads, writes)

    def memset(self, eng, ap, val, writes):
        self.S.op(eng, lambda e: e.memset(ap, val), (), writes)

    def recip(self, out, in_, reads, writes):
        self.S.op("dve", lambda e: e.reciprocal(out=out, in_=in_), reads, writes)

    def reduce(self, out, in_, op, reads, writes, axis=AX.X):
        self.S.op("dve", lambda e: e.tensor_reduce(out=out, in_=in_, axis=axis, op=op), reads, writes)

    def scan(self, out, d0, d1, init, reads, writes):
        self.S.op("dve", lambda e: e.tensor_tensor_scan(out=out, data0=d0, data1=d1, initial=init,
                                                        op0=ALU.mult, op1=ALU.add), reads, writes)

    def make_ident(self, dt=F32, name="ident"):
        idt = self.sb(name, [128, 128], dt)
        self.memset("pool", idt.t[:], 1.0, [idt.b])
        self.S.op("pool", lambda e: e.affine_select(out=idt.t[:], in_=idt.t[:], pattern=[[1, 128]],
                                                    compare_op=ALU.is_equal, fill=0.0, base=0,
                                                    channel_multiplier=-1), [idt.b], [idt.b])
        return idt

    def done(self):
        if self.fused:
            self.end_phase()
            return None
        self.S.finish(self.outs)
        self.S.emit_all()
        self.es.close()
        return self.nc

    def finish_fused(self):
        self.S.finish(self.outs)
        self.S.emit_all()
        return self.nc


def run(nc, in_maps):
    res = run_bass_kernel_spmd(nc, in_maps, core_ids=list(range(len(in_maps))))
    return res.results


def rms_stats(k, src_ap, junk, ss, rstd, eps, n, reads, tmpname=""):
    k.act(junk.t[:, 0:n], src_ap, AF.Square, reads, [junk.b, ss.b], accum_out=ss.t[:, 0:1])
    k.act(rstd.t[:, 0:1], ss.t[:, 0:1], AF.Sqrt, [ss.b], [rstd.b], scale=1.0 / n, bias=k.eps_tile(eps))
    k.recip(rstd.t[:, 0:1], rstd.t[:, 0:1], [rstd.b], [rstd.b])


def rms_stats_dve(k, src_ap, junk_ap, junk_b, ss, rstd, eps, n, reads):
    k.S.op("dve", lambda e: e.scalar_tensor_tensor(out=junk_ap, in0=src_ap, scalar=1.0, in1=src_ap, op0=ALU.mult,
                                                   op1=ALU.mult, accum_out=ss.t[:, 0:1]), reads, [junk_b, ss.b])
    k.act(rstd.t[:, 0:1], ss.t[:, 0:1], AF.Ln, [ss.b], [rstd.b], scale=1.0 / n, bias=k.eps_tile(eps))
    k.act(rstd.t[:, 0:1], rstd.t[:, 0:1], AF.Exp, [rstd.b], [rstd.b], scale=-0.5)


def _eps_tile(self, eps):
    if eps not in self._eps:
        t = self.sb(f"eps{len(self._eps)}", [128, 1], F32)
        self.memset("pool", t.t[:], float(eps), [t.b])
        self._eps[eps] = t
    return self._eps[eps].t[:, 0:1]


K.eps_tile = _eps_tile


def norm_transpose_tile(k, x_ap, xb, gain, hb, junk, ss, rstd, ptr, hT_ap_fn, ident, reads_x, hT_buf,
                        eps=1e-6, evac_eng="act"):
    rms_stats(k, x_ap, junk, ss, rstd, eps, D, reads_x)
    k.stt(hb.t[:], x_ap, rstd.t[:, 0:1], gain.t[:], ALU.mult, ALU.mult, reads_x + [rstd.b, gain.b], [hb.b])
    for kc in range(16):
        k.tr(ptr.t[:, kc * 128:(kc + 1) * 128], hb.t[:, kc * 128:(kc + 1) * 128], ident.t[:],
             [hb.b, ident.b], [ptr.b])
    k.copy(evac_eng, hT_ap_fn(), ptr.t[:].rearrange("p (c t) -> p c t", c=16), [ptr.b], [hT_buf])


def build_L1(T, k=None):
    NT = T // 128
    TG = min(512, T)
    NG = T // TG
    k = k or K()
    x = k.din("x", [T, D])
    g0 = k.din("g0", [1, D])
    wA = k.din("wA", [24, 128, 16, 128])
    wV = k.din("wV", [2, 128, 16, 512])
    qkT = k.dout("qkT", [16, 128, T], BF16)
    uT = k.dout("uT", [8, 128, T], F32)
    v = k.dout("v", [T, 1024], BF16)

    TS = min(T, 2048)
    for sg in range(T // TS):
        _L1_group(k, TS, x[sg * TS:(sg + 1) * TS, :], g0, wA, wV, qkT[:, :, sg * TS:(sg + 1) * TS],
                  uT[:, :, sg * TS:(sg + 1) * TS], v[sg * TS:(sg + 1) * TS, :])
    return k.done()


def _L1_group(k, T, x, g0, wA, wV, qkT, uT, v):
    NT = T // 128
    TG = min(512, T)
    NG = T // TG
    ident = k.make_ident(BF16, "identb")
    gain = k.sb("gain", [128, D])
    k.dma("sp", gain.t[:], g0.partition_broadcast(128), [], [gain.b])
    hT = k.sb("hT", [128, 16, T], BF16)
    hTb = [k.cbuf(f"hT{i}") for i in range(NT)]
    xs = k.sbn("xs", [128, D], F32, 2)
    hbs = k.sbn("hb", [128, D], BF16, 2)
    junk = k.sb("junk", [128, D], BF16)
    sss = k.sbn("ss", [128, 1], F32, 2)
    rstds = k.sbn("rstd", [128, 1], F32, 2)
    ptrs = [k.ps(f"ptr{i}", [128, D], BF16) for i in range(2)]

    for i in range(NT):
        xt = xs[i % 2]
        k.dma("sp", xt.t[:], x[i * 128:(i + 1) * 128, :], [], [xt.b])
        norm_transpose_tile(k, xt.t[:], xt, gain, hbs[i % 2], junk, sss[i % 2], rstds[i % 2], ptrs[i % 2],
                            lambda i=i: hT.t[:, :, i * 128:(i + 1) * 128], ident, [xt.b], hTb[i])

    wts = k.sbn("wa", [128, 16, 128], BF16, 3)
    pacc = [k.ps(f"pacc{i}", [128, 512], F32) for i in range(2)]
    obf = k.sbn("obf", [128, 512], BF16, 2)
    of32 = k.sbn("of32", [128, 512], F32, 2)
    cnt = 0
    for cb in range(24):
        wt = wts[cb % 3]
        k.dma("pool", wt.t[:], wA[cb], [], [wt.b])
        for tg in range(NG):
            pa = pacc[cnt % 2]
            rd = [wt.b] + hTb[tg * (TG // 128):(tg + 1) * (TG // 128)]
            for kc in range(16):
                k.mm(pa.t[:, 0:TG], wt.t[:, kc, :], hT.t[:, kc, tg * TG:(tg + 1) * TG], kc == 0, kc == 15,
                     rd, [pa.b])
            if cb < 16:
                o = obf[cnt % 2]
                k.copy("act" if cnt % 2 else "dve", o.t[:, 0:TG], pa.t[:, 0:TG], [pa.b], [o.b])
                k.dma("sp", qkT[cb, :, tg * TG:(tg + 1) * TG], o.t[:, 0:TG], [o.b], [], final=True)
            else:
                o = of32[cnt % 2]
                k.copy("act" if cnt % 2 else "dve", o.t[:, 0:TG], pa.t[:, 0:TG], [pa.b], [o.b])
                k.dma("sp", uT[cb - 16, :, tg * TG:(tg + 1) * TG], o.t[:, 0:TG], [o.b], [], final=True)
            cnt += 1
    wvs = k.sbn("wv", [128, 16, 512], BF16, 2)
    for cg in range(2):
        wt = wvs[cg]
        k.dma("pool", wt.t[:], wV[cg], [], [wt.b])
        for i in range(NT):
            pa = pacc[cnt % 2]
            for kc in range(16):
                k.mm(pa.t[:], hT.t[:, kc, i * 128:(i + 1) * 128], wt.t[:, kc, :], kc == 0, kc == 15,
                     [wt.b, hTb[i]], [pa.b])
            o = obf[cnt % 2]
            k.copy("act" if cnt % 2 else "dve", o.t[:], pa.t[:], [pa.b], [o.b])
            k.dma("sp", v[i * 128:(i + 1) * 128, cg * 512:(cg + 1) * 512], o.t[:], [o.b], [], final=True)
            cnt += 1


def host_L1_weights(w_in):
    w = w_in.reshape(16, 128, 4096)
    qku = np.concatenate([w[:, :, 0:2048], w[:, :, 3072:4096]], axis=2)
    wA = np.ascontiguousarray(qku.reshape(16, 128, 24, 128).transpose(2, 1, 0, 3))
    wv = w[:, :, 2048:3072]
    wV = np.ascontiguousarray(wv.reshape(16, 128, 2, 512).transpose(2, 1, 0, 3))
    return wA, wV


def build_FFN(T, wdma="pool", k=None):
    TG = min(512, T)
    NG = T // TG
    NTG = TG // 128
    NF = DFF // 128
    k = k or K()
    x1 = k.din("x1", [T, D])
    g2 = k.din("g2", [1, D])
    g3 = k.din("g3", [1, D])
    wg = k.din("wg", [NF, 128, 16, 128])
    wu = k.din("wu", [NF, 128, 16, 128])
    wd = k.din("wd", [16, 128, NF, 128])
    x2 = k.dout("x2", [T, D])

    ident = k.make_ident(F32, "identf")
    gain2 = k.sb("gain2", [128, D])
    gain3 = k.sb("gain3", [128, D])
    k.dma("sp", gain2.t[:], g2.partition_broadcast(128), [], [gain2.b])
    k.dma("sp", gain3.t[:], g3.partition_broadcast(128), [], [gain3.b])
    xs = k.sbn("xs", [128, D], F32, 2)
    hb = k.sb("hb", [128, D], F32)
    junk = k.sb("junk", [128, D], BF16)
    ss = k.sb("ss", [128, 1]); rstd = k.sb("rstd", [128, 1])
    h2T = k.sb("h2T", [128, 16, TG], BF16)
    actT = k.sb("actT", [128, NF, TG], BF16)
    actb = [k.buf(f"act{i}") for i in range(NF)]
    wgs = k.sbn("wgs", [128, 16, 128], BF16, 2)
    wus = k.sbn("wus", [128, 16, 128], BF16, 2)
    wds = k.sbn("wds", [128, NF, 128], BF16, 2)
    sg = k.sbn("sg", [128, TG], F32, 2)
    fT = k.sb("fT", [128, 16, TG], F32)
    fTb = [k.buf(f"fT{i}") for i in range(16)]
    acc = [k.ps(f"acc{i}", [128, 512], F32) for i in range(4)]
    big = k.ps("big", [128, D], F32)

    for g in range(NG):
        for i in range(NTG):
            row = (g * NTG + i) * 128
            xt = xs[i % 2]
            k.dma("sp", xt.t[:], x1[row:row + 128, :], [], [xt.b])
            rms_stats(k, xt.t[:], junk, ss, rstd, 1e-6, D, [xt.b])
            k.stt(hb.t[:], xt.t[:], rstd.t[:, 0:1], gain2.t[:], ALU.mult, ALU.mult, [xt.b, rstd.b, gain2.b], [hb.b])
            for kc in range(16):
                k.tr(big.t[:, kc * 128:(kc + 1) * 128], hb.t[:, kc * 128:(kc + 1) * 128], ident.t[:],
                     [hb.b, ident.b], [big.b])
            k.copy("act", h2T.t[:, :, i * 128:(i + 1) * 128], big.t[:].rearrange("p (c t) -> p c t", c=16),
                   [big.b], [h2T.b])
        for fc in range(NF):
            wgt = wgs[fc % 2]; wut = wus[fc % 2]
            k.dma(wdma, wgt.t[:], wg[fc], [], [wgt.b])
            k.dma(wdma, wut.t[:], wu[fc], [], [wut.b])
            pg = acc[(fc % 2) * 2]; pu = acc[(fc % 2) * 2 + 1]
            for kc in range(16):
                k.mm(pg.t[:, 0:TG], wgt.t[:, kc, :], h2T.t[:, kc, :], kc == 0, kc == 15, [wgt.b, h2T.b], [pg.b])
            for kc in range(16):
                k.mm(pu.t[:, 0:TG], wut.t[:, kc, :], h2T.t[:, kc, :], kc == 0, kc == 15, [wut.b, h2T.b], [pu.b])
            s = sg[fc % 2]
            k.act(s.t[:, 0:TG], pg.t[:, 0:TG], AF.Silu, [pg.b], [s.b])
            k.tt("dve", actT.t[:, fc, :], s.t[:, 0:TG], pu.t[:, 0:TG], ALU.mult, [s.b, pu.b], [actb[fc]])
        for fb in range(16):
            wdt = wds[fb % 2]
            k.dma(wdma, wdt.t[:], wd[fb], [], [wdt.b])
            pa = acc[fb % 4]
            for kc in range(NF):
                k.mm(pa.t[:, 0:TG], wdt.t[:, kc, :], actT.t[:, kc, :], kc == 0, kc == NF - 1,
                     [wdt.b, actb[kc]], [pa.b])
            k.copy("act" if fb % 2 else "dve", fT.t[:, fb, :], pa.t[:, 0:TG], [pa.b], [fTb[fb]])
        for i in range(NTG):
            row = (g * NTG + i) * 128
            xt = xs[i % 2]
            k.dma("sp", xt.t[:], x1[row:row + 128, :], [], [xt.b])
            for fb in range(16):
                k.tr(big.t[:, fb * 128:(fb + 1) * 128], fT.t[:, fb, i * 128:(i + 1) * 128], ident.t[:],
                     [fTb[fb], ident.b], [big.b])
            rms_stats(k, big.t[:], junk, ss, rstd, 1e-6, D, [big.b])
            k.stt(hb.t[:], big.t[:], rstd.t[:, 0:1], gain3.t[:], ALU.mult, ALU.mult, [big.b, rstd.b, gain3.b], [hb.b])
            k.tt("dve", xt.t[:], xt.t[:], hb.t[:], ALU.add, [xt.b, hb.b], [xt.b])
            k.dma("sp", x2[row:row + 128, :], xt.t[:], [xt.b], [], final=True)
    return k.done()


def host_FFN_weights(w_gate, w_up, w_down):
    NF = DFF // 128
    def gu(w):
        return np.ascontiguousarray(w.reshape(16, 128, NF, 128).transpose(2, 1, 0, 3))
    wd = np.ascontiguousarray(w_down.reshape(NF, 128, 16, 128).transpose(2, 1, 0, 3))
    return gu(w_gate), gu(w_up), wd


LAMBDA_INIT0 = 0.8 - 0.6 * math.exp(-0.3 * 0)
ATT_SCALE = 64 ** -0.5
ATT_SKIP = 80.0


def build_ATT(SEQ, NSEQ, k=None, items=None, slope=None):
    NQT = SEQ // 512
    NKT = SEQ // 128
    ND = 4 * NQT + 3
    k = k or K()
    tri_d = k.din("tri", [128, 128], BF16)
    lq_d = k.din("lq", [1, 256], F32)
    subln_d = k.din("subln", [1, 128], F32)
    if items is None:
        qT = k.din("qT", [NSEQ, 2, 64, SEQ], BF16)
        kT = k.din("kT", [NSEQ, 2, 64, SEQ], BF16)
        v = k.din("v", [NSEQ, SEQ, 128], BF16)
        qaug = k.din("qaug", [4, SEQ], BF16)
        kaug = k.din("kaug", [4, SEQ], BF16)
        biastab = k.din("biastab", [128, ND], F32)
        attn = k.dout("attn", [NSEQ, SEQ, 128], BF16)
        items = [dict(q=qT[s_], k=kT[s_], v=v[s_], qaug=qaug, kaug=kaug, bt=biastab, out=attn[s_], slope=slope) for s_ in range(NSEQ)]

    bt = k.sb("bt", [128, ND])
    tri = k.sb("tri", [128, 128], BF16); k.dma("sp", tri.t[:], tri_d[:, :], [], [tri.b])
    lq = k.sb("lq", [128, 256]); k.dma("sp", lq.t[:], lq_d.partition_broadcast(128), [], [lq.b])
    sub = k.sb("sub", [128, 128]); k.dma("sp", sub.t[:], subln_d.partition_broadcast(128), [], [sub.b])
    k.ts("dve", sub.t[:], sub.t[:], 1.0 - LAMBDA_INIT0, None, ALU.mult, None, [sub.b], [sub.b])
    lp = k.sb("lp", [128, 2, 64]); l2 = k.sb("l2", [128, 2]); nlam = k.sb("nlam", [128, 1])
    lqv = lq.t[:].rearrange("p (a b c) -> p a b c", a=2, b=2)
    k.tt("dve", lp.t[:], lqv[:, :, 0, :], lqv[:, :, 1, :], ALU.mult, [lq.b], [lp.b])
    k.reduce(l2.t[:], lp.t[:], ALU.add, [lp.b], [l2.b])
    k.act(l2.t[:], l2.t[:], AF.Exp, [l2.b], [l2.b])
    k.tt("dve", nlam.t[:], l2.t[:, 1:2], l2.t[:, 0:1], ALU.subtract, [l2.b], [nlam.b])
    k.ts("dve", nlam.t[:], nlam.t[:], -LAMBDA_INIT0, None, ALU.add, None, [nlam.b], [nlam.b])

    qa = [k.sb(f"qa{m}", [68, SEQ], BF16) for m in range(2)]
    ka = [k.sb(f"ka{m}", [68, SEQ], BF16) for m in range(2)]
    va = k.sb("va", [128, NKT, 129], BF16)
    pS = [[k.ps(f"pS{m}{b}", [128, 512], F32) for b in range(2)] for m in range(2)]
    pO = [k.ps(f"pO{q}", [128, 512], F32) for q in range(4)]
    P = [[k.sb(f"P{m}{b}", [128, 512], BF16) for b in range(2)] for m in range(2)]
    rec = k.sbn("rec", [128, 2], F32, 2)
    rl = k.sbn("rl", [128, 1], F32, 2)
    t1 = k.sbn("t1", [128, 128], F32, 2)
    dd = k.sbn("dd", [128, 128], F32, 2)
    junkf = k.sb("junkf", [128, 128], F32)
    ss = k.sbn("ss", [128, 1], F32, 2)
    rstd = k.sbn("rstd", [128, 1], F32, 2)
    ob = k.sbn("ob", [128, 128], BF16, 2)
    fin = 0
    for it in items:
        k.dma("sp", bt.t[:], it["bt"], [], [bt.b])
        for m in range(2):
            k.dma("sp", qa[m].t[0:64, :], it["q"][m], [], [qa[m].b])
            k.dma("sp", qa[m].t[64:68, :], it["qaug"], [], [qa[m].b])
            k.dma("sp", ka[m].t[0:64, :], it["k"][m], [], [ka[m].b])
            k.dma("sp", ka[m].t[64:68, :], it["kaug"], [], [ka[m].b])
        k.memset("pool", va.t[:, :, 128:129], 1.0, [va.b])
        k.dma("sp", va.t[:, :, 0:128], it["v"].rearrange("(j p) e -> p j e", p=128), [], [va.b])
        slope_ = it.get("slope")
        pairs = []
        for i in range(NQT):
            js = [j for j in range(4 * i + 4)
                  if slope_ is None or slope_ * (512 * i - (128 * j + 127)) <= ATT_SKIP]
            pairs += [(i, j, j == js[0]) for j in js]

        def emit_qk(n):
            i, j, _ = pairs[n]
            q0 = max(j - 4 * i, 0) * 128
            for m in range(2):
                ps_ = pS[m][n % 2]
                k.mm(ps_.t[:, q0:512], ka[m].t[:, j * 128:(j + 1) * 128],
                     qa[m].t[:, i * 512 + q0:(i + 1) * 512], True, True, [ka[m].b, qa[m].b], [ps_.b])

        emit_qk(0)
        for n, (i, j, first) in enumerate(pairs):
            if n + 1 < len(pairs):
                emit_qk(n + 1)
            jj = j - 4 * i
            q0 = max(jj, 0) * 128
            d = 4 * i - j + 3
            for m in range(2):
                ps_ = pS[m][n % 2]
                pt = P[m][n % 2]
                k.act(pt.t[:, q0:512], ps_.t[:, q0:512], AF.Exp, [ps_.b, bt.b], [pt.b],
                      scale=ATT_SCALE, bias=bt.t[:, d:d + 1])
                if jj >= 0:
                    k.tt("dve", pt.t[:, q0:q0 + 128], pt.t[:, q0:q0 + 128], tri.t[:], ALU.mult,
                         [pt.b, tri.b], [pt.b])
                for qq in range(max(jj, 0), 4):
                    k.mm(pO[qq].t[:, m * 129:(m + 1) * 129], pt.t[:, qq * 128:(qq + 1) * 128],
                         va.t[:, j, :], (first and m == 0), (j == 4 * i + qq and m == 1),
                         [pt.b, va.b], [pO[qq].b], skip_group_check=True)
            if jj >= 0:
                qq = jj
                f = fin % 2
                fin += 1
                po = pO[qq]
                k.recip(rec[f].t[:], po.t[:, 128:258:129], [po.b], [rec[f].b])
                k.ts("dve", rl[f].t[:], rec[f].t[:, 1:2], nlam.t[:, 0:1], None, ALU.mult, None,
                     [rec[f].b, nlam.b], [rl[f].b])
                k.ts("dve", t1[f].t[:], po.t[:, 0:128], rec[f].t[:, 0:1], None, ALU.mult, None,
                     [po.b, rec[f].b], [t1[f].b])
                k.stt(dd[f].t[:], po.t[:, 129:257], rl[f].t[:, 0:1], t1[f].t[:], ALU.mult, ALU.add,
                      [po.b, rl[f].b, t1[f].b], [dd[f].b])
                rms_stats_dve(k, dd[f].t[:], junkf.t[:], junkf.b, ss[f], rstd[f], 1e-5, 128, [dd[f].b])
                k.stt(ob[f].t[:], dd[f].t[:], rstd[f].t[:, 0:1], sub.t[:], ALU.mult, ALU.mult,
                      [dd[f].b, rstd[f].b, sub.b], [ob[f].b])
                r0 = i * 512 + qq * 128
                k.dma("sp", it["out"][r0:r0 + 128, :], ob[f].t[:], [ob[f].b], [], final=True)
    return k.done()


def split_bf16(x, n=2):
    parts = []
    r = np.asarray(x, np.float64)
    for _ in range(n):
        p = r.astype(np.float32).astype(NPBF)
        parts.append(p)
        r = r - p.astype(np.float64)
    return parts


def host_ATT_consts(head, SEQ):
    slope = 2.0 ** (-8.0 * (head + 1) / 8)
    t = np.arange(SEQ)
    a = -slope * (t % 512) / ATT_SCALE
    b = slope * (t % 128) / ATT_SCALE
    ah, al = split_bf16(a)
    bh, bl = split_bf16(b)
    one = np.ones(SEQ, NPBF)
    qaug = np.stack([ah, al, one, one])
    kaug = np.stack([one, one, bh, bl])
    ND = 4 * (SEQ // 512) + 3
    biastab = np.broadcast_to((-slope * 128.0 * (np.arange(ND) - 3)).astype(np.float32)[None, :], (128, ND)).copy()
    tri = (np.arange(128)[None, :] >= np.arange(128)[:, None]).astype(NPBF)
    return qaug, kaug, biastab, tri


TWO_PI = 2.0 * math.pi
MAGIC = 12582912.0


def _range_reduce_sin(k, out, ang, tmp, shape_reads, writes, shift=0.0):
    a_ap, a_b = ang
    o_ap, o_b = out
    t_ap, t_b = tmp
    k.ts("dve", t_ap, a_ap, shift, 1.0 / TWO_PI, ALU.add, ALU.mult, [a_b], [t_b])
    k.ts("dve", o_ap, t_ap, MAGIC, None, ALU.add, None, [t_b], [o_b])
    k.ts("dve", o_ap, o_ap, -MAGIC, None, ALU.add, None, [o_b], [o_b])
    k.tt("dve", t_ap, t_ap, o_ap, ALU.subtract, [t_b, o_b], [t_b])
    k.ts("dve", t_ap, t_ap, TWO_PI, 3.1415925, ALU.mult, ALU.min, [t_b], [t_b])
    k.ts("dve", t_ap, t_ap, -3.1415925, None, ALU.max, None, [t_b], [t_b])
    k.act(o_ap, t_ap, AF.Sin, [t_b], [o_b])


def build_S5(SEQ, NSEQ, k=None, blocks=None):
    L = 512
    NB = SEQ // L
    k = k or K()
    if blocks is None:
        uT_ = k.din("uT", [NSEQ, 128, SEQ])
        ygT_ = k.dout("ygT", [NSEQ, 128, SEQ], BF16)
        blocks = [dict(uT=[uT_[s_] for s_ in range(NSEQ)], out=[ygT_[s_] for s_ in range(NSEQ)],
                       lre=k.din("lre", [128, 4]), lim=k.din("lim", [128, 4]), ldt=k.din("ldt", [128, 4]),
                       bre=k.din("bre", [4, 128, 16]), bim=k.din("bim", [4, 128, 16]),
                       ctre=k.din("ctre", [4, 128, 128]), ctim=k.din("ctim", [4, 128, 128]),
                       dcol=k.din("dcol", [128, 1]))]
    for B_ in blocks:
        lre_d, lim_d, ldt_d, bre_d, bim_d = B_["lre"], B_["lim"], B_["ldt"], B_["bre"], B_["bim"]
        ctre_d, ctim_d, dcol_d = B_["ctre"], B_["ctim"], B_["dcol"]
        uT_l, ygT_l = B_["uT"], B_["out"]
        NSEQ = len(uT_l)
        ident = k.make_ident(F32, "identf")
        lre = k.sb("lre", [128, 4]); lim = k.sb("lim", [128, 4]); ldt = k.sb("ldt", [128, 4])
        dcol = k.sb("dcol", [128, 1])
        for t_, d_ in ((lre, lre_d), (lim, lim_d), (ldt, ldt_d), (dcol, dcol_d)):
            k.dma("sp", t_.t[:], d_, [], [t_.b])
        cols = {n: k.sb(n, [128, 4]) for n in
                "lr dt rho th c1 s1 tmp tmp2 abre abim nre den fre fim cL sL nsL angL".split()}
        C = cols

        def T2(n):
            return C[n].t[:], C[n].b
        k.ts("dve", C["lr"].t[:], lre.t[:], -1e-4, None, ALU.min, None, [lre.b], [C["lr"].b])
        k.act(C["dt"].t[:], ldt.t[:], AF.Exp, [ldt.b], [C["dt"].b])
        k.tt("dve", C["tmp"].t[:], C["lr"].t[:], C["dt"].t[:], ALU.mult, [C["lr"].b, C["dt"].b], [C["tmp"].b])
        k.act(C["rho"].t[:], C["tmp"].t[:], AF.Exp, [C["tmp"].b], [C["rho"].b])
        k.tt("dve", C["th"].t[:], lim.t[:], C["dt"].t[:], ALU.mult, [lim.b, C["dt"].b], [C["th"].b])
        _range_reduce_sin(k, T2("s1"), T2("th"), T2("tmp"), None, None, 0.0)
        _range_reduce_sin(k, T2("c1"), T2("th"), T2("tmp"), None, None, math.pi / 2)
        k.ts("dve", C["angL"].t[:], C["th"].t[:], float(L), None, ALU.mult, None, [C["th"].b], [C["angL"].b])
        _range_reduce_sin(k, T2("sL"), T2("angL"), T2("tmp"), None, None, 0.0)
        _range_reduce_sin(k, T2("cL"), T2("angL"), T2("tmp"), None, None, math.pi / 2)
        k.ts("dve", C["nsL"].t[:], C["sL"].t[:], -1.0, None, ALU.mult, None, [C["sL"].b], [C["nsL"].b])
        k.tt("dve", C["abre"].t[:], C["rho"].t[:], C["c1"].t[:], ALU.mult, [C["rho"].b, C["c1"].b], [C["abre"].b])
        k.tt("dve", C["abim"].t[:], C["rho"].t[:], C["s1"].t[:], ALU.mult, [C["rho"].b, C["s1"].b], [C["abim"].b])
        k.ts("dve", C["nre"].t[:], C["abre"].t[:], -1.0, None, ALU.add, None, [C["abre"].b], [C["nre"].b])
        k.tt("dve", C["den"].t[:], C["lr"].t[:], C["lr"].t[:], ALU.mult, [C["lr"].b], [C["den"].b])
        k.tt("dve", C["tmp"].t[:], lim.t[:], lim.t[:], ALU.mult, [lim.b], [C["tmp"].b])
        k.tt("dve", C["den"].t[:], C["den"].t[:], C["tmp"].t[:], ALU.add, [C["den"].b, C["tmp"].b], [C["den"].b])
        k.recip(C["den"].t[:], C["den"].t[:], [C["den"].b], [C["den"].b])
        k.tt("dve", C["tmp"].t[:], C["nre"].t[:], C["lr"].t[:], ALU.mult, [C["nre"].b, C["lr"].b], [C["tmp"].b])
        k.tt("dve", C["tmp2"].t[:], C["abim"].t[:], lim.t[:], ALU.mult, [C["abim"].b, lim.b], [C["tmp2"].b])
        k.tt("dve", C["fre"].t[:], C["tmp"].t[:], C["tmp2"].t[:], ALU.add, [C["tmp"].b, C["tmp2"].b], [C["fre"].b])
        k.tt("dve", C["fre"].t[:], C["fre"].t[:], C["den"].t[:], ALU.mult, [C["fre"].b, C["den"].b], [C["fre"].b])
        k.tt("dve", C["tmp"].t[:], C["abim"].t[:], C["lr"].t[:], ALU.mult, [C["abim"].b, C["lr"].b], [C["tmp"].b])
        k.tt("dve", C["tmp2"].t[:], C["nre"].t[:], lim.t[:], ALU.mult, [C["nre"].b, lim.b], [C["tmp2"].b])
        k.tt("dve", C["fim"].t[:], C["tmp"].t[:], C["tmp2"].t[:], ALU.subtract, [C["tmp"].b, C["tmp2"].b], [C["fim"].b])
        k.tt("dve", C["fim"].t[:], C["fim"].t[:], C["den"].t[:], ALU.mult, [C["fim"].b, C["den"].b], [C["fim"].b])

        iota_i = k.sb("iota_i", [128, L], mybir.dt.int32)
        k.S.op("pool", lambda e: e.iota(iota_i.t[:], pattern=[[1, L]], base=0, channel_multiplier=0), [], [iota_i.b])
        iota = k.sb("iota", [128, L])
        k.copy("dve", iota.t[:], iota_i.t[:], [iota_i.b], [iota.b])
        ones = k.sb("ones", [128, L]); k.memset("pool", ones.t[:], 1.0, [ones.b])
        ptr = k.ps("ptr", [128, 128], F32)
        bbT = [[k.sb(f"bbT{r}{st}", [128, 128], BF16) for st in range(4)] for r in range(2)]
        ctb = [[k.sb(f"ctb{r}{st}", [128, 128], BF16) for st in range(4)] for r in range(2)]
        nctb = [[k.sb(f"nctb{r}{st}", [128, 128], BF16) for st in range(4)] for r in range(2)]
        mb = {n: k.sbn("mb_" + n, [128, L], BF16, 2) for n in ("m1", "m2", "m3", "m4")}
        ctab = [k.sb(f"ctab{st}", [128, L]) for st in range(4)]
        stab = [k.sb(f"stab{st}", [128, L]) for st in range(4)]
        nstab = [k.sb(f"nstab{st}", [128, L]) for st in range(4)]
        rho = [k.sb(f"rho{st}", [128, L]) for st in range(4)]
        ang = k.sb("ang", [128, L]); tmpL = k.sb("tmpL", [128, L])
        br = k.sb("br", [128, 16]); bi = k.sb("bi", [128, 16]); bt1 = k.sb("bt1", [128, 16]); bb = k.sb("bb", [128, 16])
        Z = k.sb("Z", [128, 128]); ctf = k.sb("ctf", [128, 128])
        for st in range(4):
            k.dma("sp", br.t[:], bre_d[st], [], [br.b])
            k.dma("sp", bi.t[:], bim_d[st], [], [bi.b])
            for r in range(2):
                a_, b_ = (br, bi) if r == 0 else (bi, br)
                k.ts("dve", bt1.t[:], b_.t[:], C["fim"].t[:, st:st + 1], None, ALU.mult, None, [b_.b, C["fim"].b], [bt1.b])
                if r == 0:
                    k.ts("dve", bt1.t[:], bt1.t[:], -1.0, None, ALU.mult, None, [bt1.b], [bt1.b])
                k.stt(bb.t[:], a_.t[:], C["fre"].t[:, st:st + 1], bt1.t[:], ALU.mult, ALU.add,
                      [a_.b, C["fre"].b, bt1.b], [bb.b])
                k.memset("pool", Z.t[:], 0.0, [Z.b])
                k.copy("dve", Z.t[0:64, 32 * st:32 * st + 16], bb.t[0:64, :], [bb.b], [Z.b])
                k.copy("dve", Z.t[64:128, 32 * st + 16:32 * st + 32], bb.t[64:128, :], [bb.b], [Z.b])
                k.tr(ptr.t[:], Z.t[:], ident.t[:], [Z.b, ident.b], [ptr.b])
                k.copy("dve", bbT[r][st].t[:], ptr.t[:], [ptr.b], [bbT[r][st].b])
                k.dma("sp", ctf.t[:], (ctre_d if r == 0 else ctim_d)[st], [], [ctf.b])
                k.copy("dve", ctb[r][st].t[:], ctf.t[:], [ctf.b], [ctb[r][st].b])
                k.ts("dve", nctb[r][st].t[:], ctf.t[:], -1.0, None, ALU.mult, None, [ctf.b], [nctb[r][st].b])
            k.ts("dve", ang.t[:], iota.t[:], C["th"].t[:, st:st + 1], None, ALU.mult, None, [iota.b, C["th"].b], [ang.b])
            _range_reduce_sin(k, (stab[st].t[:], stab[st].b), (ang.t[:], ang.b), (tmpL.t[:], tmpL.b), None, None, 0.0)
            _range_reduce_sin(k, (ctab[st].t[:], ctab[st].b), (ang.t[:], ang.b), (tmpL.t[:], tmpL.b), None, None, math.pi / 2)
            k.ts("dve", nstab[st].t[:], stab[st].t[:], -1.0, None, ALU.mult, None, [stab[st].b], [nstab[st].b])
            k.ts("dve", rho[st].t[:], ones.t[:], C["rho"].t[:, st:st + 1], None, ALU.mult, None, [ones.b, C["rho"].b], [rho[st].b])

        uf = k.sbn("uf", [128, L], F32, 2)
        ub = k.sbn("ub", [128, L], BF16, 2)
        pb = [[k.ps(f"pb{r}{b}", [128, L], F32) for b in range(2)] for r in range(2)]
        py = [k.ps(f"py{b}", [128, L], F32) for b in range(2)]
        t = {n: k.sbn(n, [128, L], F32, 2) for n in ("t1", "t2", "t3", "t4", "zre", "zim", "wre", "wim")}
        xre = k.sbn("xre", [128, L], BF16, 2)
        nxim = k.sbn("nxim", [128, L], BF16, 2)
        init = [[k.sb(f"init{r}{st}", [128, 1]) for st in range(4)] for r in range(2)]
        ca = k.sbn("ca", [128, 1], F32, 2)
        yv = k.sbn("yv", [128, L], F32, 2)
        g1 = k.sbn("g1", [128, L], F32, 2)
        g2 = k.sbn("g2", [128, L], F32, 2)
        yg = k.sbn("yg", [128, L], BF16, 2)
        u_i = 0
        for s in range(NSEQ):
            for st in range(4):
                for r in range(2):
                    k.memset("pool", init[r][st].t[:], 0.0, [init[r][st].b])
            for blk in range(NB):
                ufb = uf[blk % 2]; ubb = ub[blk % 2]; pyb = py[blk % 2]
                k.dma("sp", ufb.t[:], uT_l[s][:, blk * L:(blk + 1) * L], [], [ufb.b])
                k.copy("act", ubb.t[:], ufb.t[:], [ufb.b], [ubb.b])
                for st in range(4):
                    p = u_i % 2
                    u_i += 1
                    pre, pim = pb[0][p], pb[1][p]
                    k.mm(pre.t[:], bbT[0][st].t[:], ubb.t[:], True, True, [bbT[0][st].b, ubb.b], [pre.b])
                    k.mm(pim.t[:], bbT[1][st].t[:], ubb.t[:], True, True, [bbT[1][st].b, ubb.b], [pim.b])
                    T = {n: t[n][p] for n in t}
                    c_, s_, ns_ = ctab[st], stab[st], nstab[st]
                    k.tt("dve", T["t1"].t[:], c_.t[:], pre.t[:], ALU.mult, [c_.b, pre.b], [T["t1"].b])
                    k.tt("dve", T["t2"].t[:], s_.t[:], pim.t[:], ALU.mult, [s_.b, pim.b], [T["t2"].b])
                    k.tt("dve", T["t3"].t[:], c_.t[:], pim.t[:], ALU.mult, [c_.b, pim.b], [T["t3"].b])
                    k.tt("dve", T["t4"].t[:], s_.t[:], pre.t[:], ALU.mult, [s_.b, pre.b], [T["t4"].b])
                    k.tt("dve", T["zre"].t[:], T["t1"].t[:], T["t2"].t[:], ALU.add, [T["t1"].b, T["t2"].b], [T["zre"].b])
                    k.tt("dve", T["zim"].t[:], T["t3"].t[:], T["t4"].t[:], ALU.subtract, [T["t3"].b, T["t4"].b], [T["zim"].b])
                    ir, ii = init[0][st], init[1][st]
                    k.scan(T["wre"].t[:], rho[st].t[:], T["zre"].t[:], ir.t[:, 0:1], [rho[st].b, T["zre"].b, ir.b], [T["wre"].b])
                    k.scan(T["wim"].t[:], rho[st].t[:], T["zim"].t[:], ii.t[:, 0:1], [rho[st].b, T["zim"].b, ii.b], [T["wim"].b])
                    cL = C["cL"].t[:, st:st + 1]; sL = C["sL"].t[:, st:st + 1]; nsL = C["nsL"].t[:, st:st + 1]
                    wl_re = T["wre"].t[:, L - 1:L]; wl_im = T["wim"].t[:, L - 1:L]
                    rb = [T["wre"].b, T["wim"].b, C["cL"].b, C["sL"].b, C["nsL"].b]
                    k.ts("dve", ca[0].t[:], wl_re, cL, None, ALU.mult, None, rb, [ca[0].b])
                    k.stt(ir.t[:], wl_im, nsL, ca[0].t[:], ALU.mult, ALU.add, rb + [ca[0].b], [ir.b])
                    k.ts("dve", ca[1].t[:], wl_im, cL, None, ALU.mult, None, rb, [ca[1].b])
                    k.stt(ii.t[:], wl_re, sL, ca[1].t[:], ALU.mult, ALU.add, rb + [ca[1].b], [ii.b])
                    M1, M2, M3, M4 = (mb[n][p] for n in ("m1", "m2", "m3", "m4"))
                    k.tt("dve", M1.t[:], c_.t[:], T["wre"].t[:], ALU.mult, [c_.b, T["wre"].b], [M1.b])
                    k.tt("dve", M2.t[:], s_.t[:], T["wim"].t[:], ALU.mult, [s_.b, T["wim"].b], [M2.b])
                    k.tt("dve", M3.t[:], ns_.t[:], T["wre"].t[:], ALU.mult, [ns_.b, T["wre"].b], [M3.b])
                    k.tt("pool", M4.t[:], c_.t[:], T["wim"].t[:], ALU.mult, [c_.b, T["wim"].b], [M4.b])
                    k.mm(pyb.t[:], ctb[0][st].t[:], M1.t[:], st == 0, False, [ctb[0][st].b, M1.b], [pyb.b])
                    k.mm(pyb.t[:], nctb[0][st].t[:], M2.t[:], False, False, [nctb[0][st].b, M2.b], [pyb.b])
                    k.mm(pyb.t[:], ctb[1][st].t[:], M3.t[:], False, False, [ctb[1][st].b, M3.b], [pyb.b])
                    k.mm(pyb.t[:], nctb[1][st].t[:], M4.t[:], False, st == 3, [nctb[1][st].b, M4.b], [pyb.b])
                q = blk % 2
                k.stt(yv[q].t[:], ufb.t[:], dcol.t[:, 0:1], pyb.t[:], ALU.mult, ALU.add, [ufb.b, dcol.b, pyb.b], [yv[q].b])
                k.tt("dve", g1[q].t[:], yv[q].t[:], yv[q].t[:], ALU.mult, [yv[q].b], [g1[q].b])
                k.ts("dve", g1[q].t[:], g1[q].t[:], 0.044715, 1.0, ALU.mult, ALU.add, [g1[q].b], [g1[q].b])
                k.tt("dve", g2[q].t[:], g1[q].t[:], yv[q].t[:], ALU.mult, [g1[q].b, yv[q].b], [g2[q].b])
                k.act(g2[q].t[:], g2[q].t[:], AF.Sigmoid, [g2[q].b], [g2[q].b], scale=2.0 * math.sqrt(2.0 / math.pi))
                k.tt("dve", yg[q].t[:], yv[q].t[:], g2[q].t[:], ALU.mult, [yv[q].b, g2[q].b], [yg[q].b])
                k.dma("sp", ygT_l[s][:, blk * L:(blk + 1) * L], yg[q].t[:], [yg[q].b], [], final=True)


    return k.done()


def host_S5_params(core, lam_re, lam_im, log_dt, b_re, b_im, c_re, c_im, d_skip):
    g0 = 8 * core
    sl = slice(g0, g0 + 8)
    lre = np.ascontiguousarray(lam_re[sl].reshape(4, 128).T)
    lim = np.ascontiguousarray(lam_im[sl].reshape(4, 128).T)
    ldt = np.ascontiguousarray(np.repeat(log_dt[sl], 64).reshape(4, 128).T)
    bre = np.ascontiguousarray(b_re[sl].reshape(4, 128, 16))
    bim = np.ascontiguousarray(b_im[sl].reshape(4, 128, 16))
    ctre = np.zeros((4, 128, 128), np.float32)
    ctim = np.zeros((4, 128, 128), np.float32)
    for st in range(4):
        for gl in range(2):
            g = g0 + 2 * st + gl
            col = 32 * st + 16 * gl
            ctre[st, gl * 64:(gl + 1) * 64, col:col + 16] = c_re[g].T
            ctim[st, gl * 64:(gl + 1) * 64, col:col + 16] = c_im[g].T
    dcol = np.ascontiguousarray(d_skip[sl].reshape(128, 1))
    return dict(lre=lre, lim=lim, ldt=ldt, bre=bre, bim=bim, ctre=ctre, ctim=ctim, dcol=dcol)


def out_proj_tile(k, mixT, mixb, col0, wo, big, xt, hb, junk, ss, rstd, gain1, x_src, x_dst, row, xload=None):
    for cg in range(4):
        for kc in range(16):
            k.mm(big.t[:, cg * 512:(cg + 1) * 512], mixT.t[:, kc, col0:col0 + 128], wo[cg].t[:, kc, :],
                 kc == 0, kc == 15, [wo[cg].b] + mixb, [big.b])
    if xload is None:
        k.dma("sp", xt.t[:], x_src[row:row + 128, :], [], [xt.b])
    else:
        xload(xt)
    rms_stats(k, big.t[:], junk, ss, rstd, 1e-6, D, [big.b])
    k.stt(hb.t[:], big.t[:], rstd.t[:, 0:1], gain1.t[:], ALU.mult, ALU.mult, [big.b, rstd.b, gain1.b], [hb.b])
    k.tt("dve", xt.t[:], xt.t[:], hb.t[:], ALU.add, [xt.b, hb.b], [xt.b])
    k.dma("sp", x_dst[row:row + 128, :], xt.t[:], [xt.b], [], final=True)


def build_OUT0(T, k=None):
    TG = min(512, T)
    NG = T // TG
    NTG = TG // 128
    k = k or K()
    x = k.din("x", [T, D])
    attn = k.din("attn", [T, 1024], BF16)
    ygT = k.din("ygT", [8, 128, T], BF16)
    wglu = k.din("wglu", [8, 128, 8, 128])
    wout = k.din("wout", [4, 128, 16, 512])
    g1 = k.din("g1", [1, D])
    x1 = k.dout("x1", [T, D])

    gain1 = k.sb("gain1", [128, D]); k.dma("sp", gain1.t[:], g1.partition_broadcast(128), [], [gain1.b])
    wgl = k.sb("wgl", [128, 8, 8, 128], BF16)
    for cb in range(8):
        k.dma("pool", wgl.t[:, cb], wglu[cb], [], [wgl.b])
    wo = [k.sb(f"wo{cg}", [128, 16, 512], BF16) for cg in range(4)]
    for cg in range(4):
        k.dma("pool", wo[cg].t[:], wout[cg], [], [wo[cg].b])
    identb = k.make_ident(BF16, "identb")
    att_t = k.sbn("att_t", [128, 1024], BF16, 2)
    ptb = k.ps("ptb", [128, 1024], BF16)
    mixT = k.sbn("mixT", [128, 16, TG], BF16, 2)
    ygs = k.sbn("ygs", [128, 8, TG], BF16, 2)
    sg = k.sbn("sg", [128, TG], F32, 2)
    acc = [k.ps(f"acc{i}", [128, 512], F32) for i in range(2)]
    big = k.ps("big", [128, D], F32)
    xs = k.sbn("xs", [128, D], F32, 2)
    hb = k.sb("hb", [128, D], F32)
    junk = k.sb("junk", [128, D], BF16)
    ss = k.sb("ss", [128, 1]); rstd = k.sb("rstd", [128, 1])
    for g in range(NG):
        mx = mixT[g % 2]; yg = ygs[g % 2]
        for i in range(NTG):
            row = (g * NTG + i) * 128
            at = att_t[i % 2]
            k.dma("sp", at.t[:], attn[row:row + 128, :], [], [at.b])
            for c_ in range(8):
                k.tr(ptb.t[:, c_ * 128:(c_ + 1) * 128], at.t[:, c_ * 128:(c_ + 1) * 128], identb.t[:],
                     [at.b, identb.b], [ptb.b])
            k.copy("act", mx.t[:, 0:8, i * 128:(i + 1) * 128], ptb.t[:].rearrange("p (c t) -> p c t", c=8),
                   [ptb.b], [mx.b])
        k.dma("sp", yg.t[:], ygT[:, :, g * TG:(g + 1) * TG].rearrange("c p t -> p c t"), [], [yg.b])
        for cb in range(8):
            pa = acc[cb % 2]
            for kc in range(8):
                k.mm(pa.t[:, 0:TG], wgl.t[:, cb, kc, :], yg.t[:, kc, :], kc == 0, kc == 7, [wgl.b, yg.b], [pa.b])
            s = sg[cb % 2]
            k.act(s.t[:, 0:TG], pa.t[:, 0:TG], AF.Sigmoid, [pa.b], [s.b])
            k.tt("dve", mx.t[:, 8 + cb, :], s.t[:, 0:TG], yg.t[:, cb, :], ALU.mult, [s.b, yg.b], [mx.b])
        for i in range(NTG):
            row = (g * NTG + i) * 128
            out_proj_tile(k, mx, [mx.b], i * 128, wo, big, xs[i % 2], hb, junk, ss, rstd, gain1, x, x1, row)
    return k.done()


def host_OUT_weights(w_glu, w_out):
    wglu = None
    if w_glu is not None:
        wglu = np.ascontiguousarray(w_glu.reshape(8, 128, 8, 128).transpose(2, 1, 0, 3))
    wout = np.ascontiguousarray(w_out.reshape(16, 128, 4, 512).transpose(2, 1, 0, 3))
    return wglu, wout


LW_SCALE = -math.exp(-0.5)


def build_RPROJ(T, k=None):
    TG = min(512, T)
    NG = T // TG
    NTG = TG // 128
    HW = TG + 128
    k = k or K()
    x1h = k.din("x1h", [T + 128, D])
    g0 = k.din("g0", [1, D])
    mucol_d = k.din("mucol", [128, 6, 16])
    wr_d = k.din("wr", [4, 128, 16, 512]); wk_d = k.din("wk", [4, 128, 16, 512]); wv_d = k.din("wv", [4, 128, 16, 512])
    w1_d = k.din("w1", [128, 16, 96]); a1_d = k.din("a1", [128, 16, 96]); g1_d = k.din("g1w", [128, 16, 256])
    w2_d = k.din("w2", [96, D]); a2_d = k.din("a2", [96, D]); g2_d = k.din("g2w", [128, 2, D])
    w0_d = k.din("w0", [1, D]); a0_d = k.din("a0", [1, D]); kk_d = k.din("k_k", [1, D]); ka_d = k.din("k_a", [1, D])
    outs = {n: k.dout(n, [T, D]) for n in ("r", "lw", "kp", "v", "kkn", "bb", "g")}

    ident = k.make_ident(F32, "identf")
    gain = k.sb("gain", [128, D]); k.dma("sp", gain.t[:], g0.partition_broadcast(128), [], [gain.b])
    kkrow = k.sb("kkrow", [128, D]); k.dma("sp", kkrow.t[:], kk_d.partition_broadcast(128), [], [kkrow.b])
    karow = k.sb("karow", [128, D]); k.dma("sp", karow.t[:], ka_d.partition_broadcast(128), [], [karow.b])
    mu = k.sb("mu", [128, 6, 16]); k.dma("sp", mu.t[:], mucol_d[:, :, :], [], [mu.b])
    omu = k.sb("omu", [128, 6, 16])
    k.ts("dve", omu.t[:], mu.t[:], -1.0, 1.0, ALU.mult, ALU.add, [mu.b], [omu.b])
    w1s = k.sb("w1s", [128, 16, 96], BF16); k.dma("pool", w1s.t[:], w1_d[:, :, :], [], [w1s.b])
    a1s = k.sb("a1s", [128, 16, 96], BF16); k.dma("pool", a1s.t[:], a1_d[:, :, :], [], [a1s.b])
    g1s = k.sb("g1s", [128, 16, 256], BF16); k.dma("pool", g1s.t[:], g1_d[:, :, :], [], [g1s.b])
    w2s = k.sb("w2s", [96, D], BF16); k.dma("pool", w2s.t[:], w2_d[:, :], [], [w2s.b])
    a2s = k.sb("a2s", [96, D], BF16); k.dma("pool", a2s.t[:], a2_d[:, :], [], [a2s.b])
    g2s = k.sb("g2s", [128, 2, D], BF16); k.dma("pool", g2s.t[:], g2_d[:, :, :], [], [g2s.b])
    ones1 = k.sb("ones1", [33, 128], BF16); k.memset("pool", ones1.t[:], 1.0, [ones1.b])
    xs = k.sb("xs", [128, D]); hb = k.sb("hb", [128, D])
    HI = k.sb("biasHI", [33, D], BF16); LO = k.sb("biasLO", [33, D], BF16)
    bias = {}
    for nm, d_, p_ in (("w0", w0_d, 0), ("a0", a0_d, 32)):
        k.dma("sp", xs.t[p_:p_ + 1, :], d_[:, :], [], [xs.b])
        k.copy("dve", HI.t[p_:p_ + 1, :], xs.t[p_:p_ + 1, :], [xs.b], [HI.b])
        k.tt("dve", LO.t[p_:p_ + 1, :], xs.t[p_:p_ + 1, :], HI.t[p_:p_ + 1, :], ALU.subtract, [xs.b, HI.b], [LO.b])
        bias[nm] = p_

    junk = hb
    ss = k.sb("ss", [128, 1]); rstd = k.sb("rstd", [128, 1])
    hT = k.sb("hT", [128, 16, HW], F32)
    xjs = k.sbn("xj", [128, 16, TG], BF16, 1)
    xjc = [0]
    tmp = k.sbn("tmp", [128, TG], F32, 2)
    twT = k.sb("twT", [96, TG], BF16); taT = k.sb("taT", [96, TG], BF16); tgT = k.sb("tgT", [128, 2, TG], BF16)
    Wb = k.sbn("Wb", [128, 16, 512], BF16, 2)
    big = k.ps("big", [128, D], F32)
    acc = [k.ps(f"acc{i}", [128, 512], F32) for i in range(4)]
    st = {n: k.sbn("st_" + n, [128, 512], F32, 2) for n in ("o", "a", "k", "kkn", "bb", "kp")}
    for n in ("kk", "sq", "am"):
        t1_ = k.sb("st_" + n, [128, 512], F32)
        st[n] = [t1_, t1_]
    sm = {n: k.sbn("sm_" + n, [128, 8], F32, 2) for n in ("ssq", "rn")}
    cnt = [0]

    def mix(j):
        xjc[0] += 1
        xj = xjs[0]
        for kc in range(16):
            t_ = tmp[kc % 2]
            k.act(t_.t[:, 0:TG], hT.t[:, kc, 127:127 + TG], AF.Copy, [hT.b, mu.b], [t_.b], scale=mu.t[:, j, kc:kc + 1])
            k.stt(xj.t[:, kc, :], hT.t[:, kc, 128:128 + TG], omu.t[:, j, kc:kc + 1], t_.t[:, 0:TG], ALU.mult, ALU.add,
                  [hT.b, omu.b, t_.b], [xj.b])
        return xj

    def emit_out(name, row, cg, src):
        k.dma("sp", outs[name][row:row + 128, cg * 512:(cg + 1) * 512], src.t[:], [src.b], [], final=True)

    for g in range(NG):
        for i in range(NTG + 1):
            row = g * TG + i * 128
            k.dma("sp", xs.t[:], x1h[row:row + 128, :], [], [xs.b])
            rms_stats(k, xs.t[:], junk, ss, rstd, 1e-6, D, [xs.b])
            k.stt(hb.t[:], xs.t[:], rstd.t[:, 0:1], gain.t[:], ALU.mult, ALU.mult, [xs.b, rstd.b, gain.b], [hb.b])
            for kc in range(16):
                k.tr(big.t[:, kc * 128:(kc + 1) * 128], hb.t[:, kc * 128:(kc + 1) * 128], ident.t[:],
                     [hb.b, ident.b], [big.b])
            k.copy("act", hT.t[:, :, i * 128:(i + 1) * 128], big.t[:].rearrange("p (c t) -> p c t", c=16),
                   [big.b], [hT.b])
        xj = mix(1)
        pa = acc[0]
        for kc in range(16):
            k.mm(pa.t[0:96, 0:TG], w1s.t[:, kc, :], xj.t[:, kc, :], kc == 0, kc == 15, [w1s.b, xj.b], [pa.b])
        k.act(twT.t[:, :], pa.t[0:96, 0:TG], AF.Tanh, [pa.b], [twT.b])
        xj = mix(4)
        pa = acc[1]
        for kc in range(16):
            k.mm(pa.t[0:96, 0:TG], a1s.t[:, kc, :], xj.t[:, kc, :], kc == 0, kc == 15, [a1s.b, xj.b], [pa.b])
        k.copy("act", taT.t[:, :], pa.t[0:96, 0:TG], [pa.b], [taT.b])
        xj = mix(5)
        for c in range(2):
            pa = acc[2 + c]
            for kc in range(16):
                k.mm(pa.t[:, 0:TG], g1s.t[:, kc, c * 128:(c + 1) * 128], xj.t[:, kc, :], kc == 0, kc == 15,
                     [g1s.b, xj.b], [pa.b])
            k.act(tgT.t[:, c, :], pa.t[:, 0:TG], AF.Sigmoid, [pa.b], [tgT.b])
        for i in range(NTG):
            row = g * TG + i * 128
            for cg in range(4):
                c_ = cnt[0]; cnt[0] += 1
                pa = acc[c_ % 4]
                cs = slice(cg * 512, (cg + 1) * 512)
                k.mm(pa.t[:], twT.t[:, i * 128:(i + 1) * 128], w2s.t[:, cs], True, False, [twT.b, w2s.b], [pa.b])
                k.mm(pa.t[:], ones1.t[0:1, :], HI.t[0:1, cs], False, False, [ones1.b, HI.b], [pa.b])
                k.mm(pa.t[:], ones1.t[0:1, :], LO.t[0:1, cs], False, True, [ones1.b, LO.b], [pa.b])
                o = st["o"][c_ % 2]
                k.act(o.t[:], pa.t[:], AF.Sigmoid, [pa.b], [o.b])
                k.ts("dve", o.t[:], o.t[:], LW_SCALE, None, ALU.mult, None, [o.b], [o.b])
                emit_out("lw", row, cg, o)
                c_ = cnt[0]; cnt[0] += 1
                pa = acc[c_ % 4]
                for c in range(2):
                    k.mm(pa.t[:], tgT.t[:, c, i * 128:(i + 1) * 128], g2s.t[:, c, cs], c == 0, c == 1,
                         [tgT.b, g2s.b], [pa.b])
                o = st["o"][c_ % 2]
                k.copy("act", o.t[:], pa.t[:], [pa.b], [o.b])
                emit_out("g", row, cg, o)
        for j, wd_, nm in ((0, wr_d, "r"), (3, wv_d, "v")):
            xj = mix(j)
            for cg in range(4):
                wt = Wb[cg % 2]
                k.dma("pool", wt.t[:], wd_[cg], [], [wt.b])
                for i in range(NTG):
                    row = g * TG + i * 128
                    c_ = cnt[0]; cnt[0] += 1
                    pa = acc[c_ % 4]
                    for kc in range(16):
                        k.mm(pa.t[:], xj.t[:, kc, i * 128:(i + 1) * 128], wt.t[:, kc, :], kc == 0, kc == 15,
                             [xj.b, wt.b], [pa.b])
                    o = st["o"][c_ % 2]
                    k.copy("act" if c_ % 2 else "dve", o.t[:], pa.t[:], [pa.b], [o.b])
                    emit_out(nm, row, cg, o)
        xj = mix(2)
        for cg in range(4):
            wt = Wb[cg % 2]
            k.dma("pool", wt.t[:], wk_d[cg], [], [wt.b])
            cs = slice(cg * 512, (cg + 1) * 512)
            for i in range(NTG):
                row = g * TG + i * 128
                c_ = cnt[0]; cnt[0] += 1
                p = c_ % 2
                pk = acc[(c_ % 2) * 2]; pA = acc[(c_ % 2) * 2 + 1]
                for kc in range(16):
                    k.mm(pk.t[:], xj.t[:, kc, i * 128:(i + 1) * 128], wt.t[:, kc, :], kc == 0, kc == 15,
                         [xj.b, wt.b], [pk.b])
                k.mm(pA.t[:], taT.t[:, i * 128:(i + 1) * 128], a2s.t[:, cs], True, False, [taT.b, a2s.b], [pA.b])
                k.mm(pA.t[:], ones1.t[32:33, :], HI.t[32:33, cs], False, False, [ones1.b, HI.b], [pA.b])
                k.mm(pA.t[:], ones1.t[32:33, :], LO.t[32:33, cs], False, True, [ones1.b, LO.b], [pA.b])
                a_ = st["a"][p]; k_ = st["k"][p]; kk_ = st["kk"][p]; sq_ = st["sq"][p]; kkn_ = st["kkn"][p]
                bb_ = st["bb"][p]; am_ = st["am"][p]; kp_ = st["kp"][p]; ssq = sm["ssq"][p]; rn = sm["rn"][p]
                k.act(a_.t[:], pA.t[:], AF.Sigmoid, [pA.b], [a_.b])
                k.copy("act", k_.t[:], pk.t[:], [pk.b], [k_.b])
                k.tt("dve", kk_.t[:], k_.t[:], kkrow.t[:, cs], ALU.mult, [k_.b, kkrow.b], [kk_.b])
                k.tt("dve", sq_.t[:], kk_.t[:], kk_.t[:], ALU.mult, [kk_.b], [sq_.b])
                k.reduce(ssq.t[:], sq_.t[:].rearrange("p (h c) -> p h c", h=8), ALU.add, [sq_.b], [ssq.b])
                k.act(rn.t[:], ssq.t[:], AF.Sqrt, [ssq.b], [rn.b])
                k.ts("dve", rn.t[:], rn.t[:], 1e-12, None, ALU.max, None, [rn.b], [rn.b])
                k.recip(rn.t[:], rn.t[:], [rn.b], [rn.b])
                k.tt("dve", kkn_.t[:].rearrange("p (h c) -> p h c", h=8), kk_.t[:].rearrange("p (h c) -> p h c", h=8),
                     rn.t[:].unsqueeze(2).to_broadcast([128, 8, 64]), ALU.mult, [kk_.b, rn.b], [kkn_.b])
                emit_out("kkn", row, cg, kkn_)
                k.tt("dve", bb_.t[:], kkn_.t[:], a_.t[:], ALU.mult, [kkn_.b, a_.b], [bb_.b])
                emit_out("bb", row, cg, bb_)
                k.stt(am_.t[:], a_.t[:], -1.0, karow.t[:, cs], ALU.add, ALU.mult, [a_.b, karow.b], [am_.b])
                k.stt(kp_.t[:], am_.t[:], 1.0, k_.t[:], ALU.add, ALU.mult, [am_.b, k_.b], [kp_.b])
                emit_out("kp", row, cg, kp_)
    return k.done()


def host_RPROJ_weights(mu, w_r, w_k, w_v, w1, a1, g1, w2, a2, g2):
    def big(w):
        return np.ascontiguousarray(w.reshape(16, 128, 4, 512).transpose(2, 1, 0, 3))
    def s1(w):
        return np.ascontiguousarray(w.reshape(16, 128, -1).transpose(1, 0, 2))
    mucol = np.ascontiguousarray(mu.reshape(6, 16, 128).transpose(2, 0, 1))
    g2w = np.ascontiguousarray(g2.reshape(2, 128, D).transpose(1, 0, 2))
    return dict(mucol=mucol, wr=big(w_r), wk=big(w_k), wv=big(w_v), w1=s1(w1), a1=s1(a1), g1w=s1(g1),
                w2=np.ascontiguousarray(w2), a2=np.ascontiguousarray(a2), g2w=g2w)


def build_RSCAN(NCH, NS=8, k=None, passes=None):
    k = k or K()
    if passes is None:
        passes = [dict(din={n: k.din(n, [NCH, 128, NS * 64]) for n in ("r", "lw", "kp", "v", "kkn", "bb")},
                       y=k.dout("y", [NCH, 128, NS * 64]))]
    tri_d = k.din("tri", [128, 128])
    mst_d = k.din("m_strict_T", [128, 512])
    mit_d = k.din("m_incl_T", [128, 512])
    mlo_d = k.din("m_lo", [128, 512])
    HS = NS // 2
    W = HS * 64
    assert HS == 4

    ident = k.make_ident(F32, "identf")
    identb = k.make_ident(BF16, "identb")
    tri = k.sb("tri", [128, 128]); k.dma("sp", tri.t[:], tri_d[:, :], [], [tri.b])
    mst = k.sb("mst", [128, 512]); k.dma("sp", mst.t[:], mst_d[:, :], [], [mst.b])
    mit = k.sb("mit", [128, 512]); k.dma("sp", mit.t[:], mit_d[:, :], [], [mit.b])
    mlo = k.sb("mlo", [128, 512]); k.dma("sp", mlo.t[:], mlo_d[:, :], [], [mlo.b])
    onesq = k.sb("onesq", [128, 128]); k.memset("pool", onesq.t[:], 1.0, [onesq.b])

    def hs(h):
        return slice(h * 64, (h + 1) * 64)

    def v3(ap):
        return ap.rearrange("p (h t) -> p h t", h=HS)

    def r32(ap):
        return ap.bitcast(F32R)

    inp = {n: k.sbn("i_" + n, [128, 2 * W], F32, 2) for n in ("r", "lw", "kp", "v", "kkn", "bb")}
    Ysh = k.sbn("Ysh", [128, 2 * W], F32, 2)

    def stream(sid):
        sx = f"_{sid}"
        e = {n: k.sb("e_" + n + sx, [128, W]) for n in ("cum", "t1", "t2", "G", "Gi", "Gp", "Gr", "Y")}
        eb = {n: k.sb("eb_" + n + sx, [128, W], F32) for n in ("AH", "BC", "KC", "RH", "BT", "KT", "V", "W0", "U")}
        gC = k.sb("gC" + sx, [64, HS])
        ST = k.sb("ST" + sx, [64, W]); STs = k.sb("STs" + sx, [64, W]); STb = ST
        tT = {n: k.sb("T_" + n + sx, [64, HS, 128], F32) for n in ("AH", "BC", "KC", "RH")}
        A = {n: k.sb("A_" + n + sx, [128, HS, 128], F32) for n in ("abT", "rbT", "akT", "rkT", "ab")}
        Xp = [k.sb(f"X{i}" + sx, [128, HS, 128], F32) for i in range(2)]
        XTp = [k.sb(f"XT{i}" + sx, [128, HS, 128], F32) for i in range(2)]
        TT = k.sb("TT" + sx, [128, HS, 128]); TTb = TT
        pbig = [k.ps(f"pbig{i}" + sx, [128, HS * 128], F32) for i in range(2)]
        ptr = k.ps("ptr" + sx, [128, HS * 128], F32)
        pD = k.ps("pD" + sx, [128, 512], F32)
        pE = pD
        nb = [0]

        def nextbig():
            nb[0] += 1
            return pbig[nb[0] % len(pbig)]

        for P_ in passes:
            cs = slice(sid * W, (sid + 1) * W)
            k.memset("pool", ST.t[:], 0.0, [ST.b])
            for c in range(NCH):
                I = {n: Tile(inp[n][c % 2].t[:, cs], inp[n][c % 2].b) for n in inp}
                if sid == 0:
                    for n in ("lw", "kkn", "bb", "kp", "r", "v"):
                        k.dma("sp", inp[n][c % 2].t[:], P_["din"][n][c], [], [inp[n][c % 2].b])
                LW = I["lw"]
                k.mm(pD.t[:, 0:W], tri.t[:], LW.t[:], True, True, [tri.b, LW.b], [pD.b])
                pt = nextbig()
                k.mm(pt.t[:, 0:W], onesq.t[:], LW.t[:], True, True, [onesq.b, LW.b], [pt.b])
                for h in range(HS):
                    k.mm(pt.t[0:64, W + h:W + h + 1], LW.t[:, hs(h)], onesq.t[:, 0:1], True, True, [LW.b, onesq.b], [pt.b])
                yield
                k.act(gC.t[:], pt.t[0:64, W:W + HS], AF.Exp, [pt.b], [gC.b])
                k.copy("act", e["cum"].t[:], pD.t[:, 0:W], [pD.b], [e["cum"].b])
                k.act(e["G"].t[:], pD.t[:, 0:W], AF.Exp, [pD.b], [e["G"].b])
                k.act(e["Gi"].t[:], pD.t[:, 0:W], AF.Exp, [pD.b], [e["Gi"].b], scale=-1.0)
                k.tt("dve", e["t1"].t[:], e["cum"].t[:], LW.t[:], ALU.subtract, [e["cum"].b, LW.b], [e["t1"].b])
                k.act(e["Gp"].t[:], e["t1"].t[:], AF.Exp, [e["t1"].b], [e["Gp"].b])
                k.tt("dve", e["t2"].t[:], pt.t[:, 0:W], e["cum"].t[:], ALU.subtract, [pt.b, e["cum"].b], [e["t2"].b])
                k.act(e["Gr"].t[:], e["t2"].t[:], AF.Exp, [e["t2"].b], [e["Gr"].b])
                yield
                k.stt(r32(eb["AH"].t[:]), I["kkn"].t[:], -1.0, e["Gp"].t[:], ALU.mult, ALU.mult, [I["kkn"].b, e["Gp"].b], [eb["AH"].b])
                k.tt("dve", r32(eb["BC"].t[:]), I["bb"].t[:], e["Gi"].t[:], ALU.mult, [I["bb"].b, e["Gi"].b], [eb["BC"].b])
                k.tt("dve", r32(eb["KC"].t[:]), I["kp"].t[:], e["Gi"].t[:], ALU.mult, [I["kp"].b, e["Gi"].b], [eb["KC"].b])
                k.tt("dve", r32(eb["RH"].t[:]), I["r"].t[:], e["G"].t[:], ALU.mult, [I["r"].b, e["G"].b], [eb["RH"].b])
                k.tt("dve", eb["BT"].t[:], I["bb"].t[:], e["Gr"].t[:], ALU.mult, [I["bb"].b, e["Gr"].b], [eb["BT"].b])
                k.tt("dve", eb["KT"].t[:], I["kp"].t[:], e["Gr"].t[:], ALU.mult, [I["kp"].b, e["Gr"].b], [eb["KT"].b])
                k.copy("act", r32(eb["V"].t[:]), I["v"].t[:], [I["v"].b], [eb["V"].b])
                yield
                for n in ("AH", "BC", "KC", "RH"):
                    for h in range(HS):
                        k.tr(ptr.t[0:64, h * 128:(h + 1) * 128], eb[n].t[:, hs(h)], ident.t[:], [eb[n].b, ident.b], [ptr.b])
                    k.copy("act", r32(tT[n].t[:]), v3(ptr.t[0:64, :]), [ptr.b], [tT[n].b])
                    yield
                specs = (("abT", "BC", "AH", mst), ("ab", "AH", "BC", mlo), ("akT", "KC", "AH", mst),
                         ("rbT", "BC", "RH", mit), ("rkT", "KC", "RH", mit))
                for an, ln, rn, msk in specs:
                    pt = nextbig()
                    for h in range(HS):
                        k.mm(pt.t[:, h * 128:(h + 1) * 128], r32(tT[ln].t[:, h, :]), r32(tT[rn].t[:, h, :]), True, True,
                             [tT[ln].b, tT[rn].b], [pt.b])
                    k.tt("dve", r32(A[an].t[:]), v3(pt.t[:]), v3(msk.t[:]), ALU.mult, [pt.b, msk.b], [A[an].b])
                    yield
                k.tt("dve", r32(TT.t[:]), A["abT"].t[:], ident.t[:].unsqueeze(1).to_broadcast([128, HS, 128]), ALU.add,
                     [A["abT"].b, ident.b], [TT.b])
                X, XT = A["ab"], A["abT"]
                for step in range(6):
                    Xn, XTn = Xp[step % 2], XTp[step % 2]
                    pX = nextbig()
                    for h in range(HS):
                        k.mm(pX.t[:, h * 128:(h + 1) * 128], r32(XT.t[:, h, :]), r32(X.t[:, h, :]), True, True, [XT.b, X.b], [pX.b])
                    k.copy("act", r32(Xn.t[:]), v3(pX.t[:]), [pX.b], [Xn.b])
                    yield
                    if step < 5:
                        pXT = nextbig()
                        for h in range(HS):
                            k.mm(pXT.t[:, h * 128:(h + 1) * 128], r32(X.t[:, h, :]), r32(XT.t[:, h, :]), True, True, [XT.b, X.b], [pXT.b])
                        k.copy("act", r32(XTn.t[:]), v3(pXT.t[:]), [pXT.b], [XTn.b])
                        yield
                    pT = nextbig()
                    for h in range(HS):
                        k.mm(pT.t[:, h * 128:(h + 1) * 128], r32(Xn.t[:, h, :]), r32(TT.t[:, h, :]), True, True, [Xn.b, TT.b], [pT.b])
                    k.tt("dve", r32(TT.t[:]), TT.t[:], v3(pT.t[:]), ALU.add, [TT.b, pT.b], [TT.b])
                    yield
                    X, XT = Xn, XTn
                V = eb["V"]
                for h in range(HS):
                    k.mm(pD.t[:, hs(h)], r32(A["akT"].t[:, h, :]), r32(V.t[:, hs(h)]), True, False, [A["akT"].b, V.b], [pD.b])
                    k.mm(pD.t[:, hs(h)], r32(tT["AH"].t[:, h, :]), r32(ST.t[:, hs(h)]), False, True, [tT["AH"].b, ST.b], [pD.b])
                k.copy("act", r32(eb["W0"].t[:]), pD.t[:, 0:W], [pD.b], [eb["W0"].b])
                yield
                for h in range(HS):
                    k.mm(pE.t[:, hs(h)], r32(TT.t[:, h, :]), r32(eb["W0"].t[:, hs(h)]), True, True, [TT.b, eb["W0"].b], [pE.b])
                k.copy("act", r32(eb["U"].t[:]), pE.t[:, 0:W], [pE.b], [eb["U"].b])
                yield
                for h in range(HS):
                    k.mm(pD.t[:, hs(h)], r32(tT["RH"].t[:, h, :]), r32(ST.t[:, hs(h)]), True, False, [tT["RH"].b, ST.b], [pD.b])
                    k.mm(pD.t[:, hs(h)], r32(A["rbT"].t[:, h, :]), r32(eb["U"].t[:, hs(h)]), False, False, [A["rbT"].b, eb["U"].b], [pD.b])
                    k.mm(pD.t[:, hs(h)], r32(A["rkT"].t[:, h, :]), r32(V.t[:, hs(h)]), False, True, [A["rkT"].b, V.b], [pD.b])
                k.copy("act", Ysh[c % 2].t[:, cs], pD.t[:, 0:W], [pD.b], [Ysh[c % 2].b])
                if sid == 1:
                    k.dma("sp", P_["y"][c], Ysh[c % 2].t[:], [Ysh[c % 2].b], [], final=True)
                yield
                for h in range(HS):
                    k.mm(pE.t[0:64, hs(h)], eb["BT"].t[:, hs(h)], eb["U"].t[:, hs(h)], True, False, [eb["BT"].b, eb["U"].b], [pE.b])
                    k.mm(pE.t[0:64, hs(h)], eb["KT"].t[:, hs(h)], V.t[:, hs(h)], False, True, [eb["KT"].b, V.b], [pE.b])
                k.tt("dve", STs.t[:].rearrange("p (h c) -> p h c", h=HS), ST.t[:].rearrange("p (h c) -> p h c", h=HS),
                     gC.t[:].unsqueeze(2).to_broadcast([64, HS, 64]), ALU.mult, [ST.b, gC.b], [STs.b])
                k.tt("dve", r32(ST.t[:]), STs.t[:], pE.t[0:64, 0:W], ALU.add, [STs.b, pE.b], [ST.b])
                yield

    gens = [stream(0), stream(1)]
    while gens:
        for g_ in list(gens):
            try:
                next(g_)
            except StopIteration:
                gens.remove(g_)
    return k.done()


def host_RSCAN_consts():
    s = np.arange(128)[:, None]
    t = np.arange(128)[None, :]
    tri = (s <= t).astype(np.float32)
    mst = np.tile((s < t).astype(np.float32), (1, 4))
    mit = np.tile((s <= t).astype(np.float32), (1, 4))
    mlo = np.tile((t < s).astype(np.float32), (1, 4))
    return dict(tri=tri, m_strict_T=mst, m_incl_T=mit, m_lo=mlo)


GN_EPS = 64e-5


def build_OUT1(T, k=None, gather=None):
    NT = T // 128
    k = k or K()
    x1 = k.din("x1", [T, D])
    dins = {n: k.din(n, [T, D]) for n in ("y", "r", "kp", "v", "g")}
    lnw_d = k.din("ln_w", [1, D]); lnb_d = k.din("ln_b", [1, D]); rk_d = k.din("r_k", [1, D]); g1 = k.din("g1", [1, D])
    wout = k.din("wout", [4, 128, 16, 512])
    x2 = k.dout("x2", [T, D])

    ident = k.make_ident(F32, "identf")
    rows = {}
    for n, d_ in (("lnw", lnw_d), ("lnb", lnb_d), ("rk", rk_d), ("gain1", g1)):
        rows[n] = k.sb("row_" + n, [128, D]); k.dma("sp", rows[n].t[:], d_.partition_broadcast(128), [], [rows[n].b])
    wo = [k.sb(f"wo{cg}", [128, 16, 512], BF16) for cg in range(4)]
    for cg in range(4):
        k.dma("pool", wo[cg].t[:], wout[cg], [], [wo[cg].b])
    tl = {n: k.sb("t_" + n, [128, D]) for n in ("y", "r", "kp", "v", "g", "sq")}
    s32 = {n: k.sb("s_" + n, [128, 32]) for n in ("mean", "var", "s")}
    mixT = k.sbn("mixT", [128, 16, 128], BF16, 2)
    big = k.ps("big", [128, D], F32)
    xs = k.sb("xs", [128, D]); hb = k.sb("hb", [128, D]); junk = k.sb("junk", [128, D], BF16)
    ss = k.sb("ss", [128, 1]); rstd = k.sb("rstd", [128, 1])

    def v3(t_):
        return t_.t[:].rearrange("p (h c) -> p h c", h=32)

    def b3(t_):
        return t_.t[:].unsqueeze(2).to_broadcast([128, 32, 64])

    if gather is not None:
        gidx = k.sb("gidx", [128, NT], mybir.dt.int32)
        k.dma("sp", gidx.t[:], gather["idx"], [], [gidx.b])

        def gload(dst, src, i, eoff=0):
            k.S.op("pool", lambda e: e.indirect_dma_start(
                out=dst.t[:], out_offset=None, in_=src,
                in_offset=bass.IndirectOffsetOnAxis(ap=gidx.t[:, i:i + 1], axis=0), element_offset=eoff),
                [gidx.b], [dst.b], dma=True)
    for i in range(NT):
        row = i * 128
        for n in ("y", "r", "kp", "v", "g"):
            if gather is None:
                k.dma("sp", tl[n].t[:], dins[n][row:row + 128, :], [], [tl[n].b])
            else:
                gload(tl[n], dins[n], i)
        Y, R, KP, V, G, SQ = (tl[n] for n in ("y", "r", "kp", "v", "g", "sq"))
        k.reduce(s32["mean"].t[:], v3(Y), ALU.add, [Y.b], [s32["mean"].b])
        k.ts("dve", s32["mean"].t[:], s32["mean"].t[:], 1.0 / 64, None, ALU.mult, None, [s32["mean"].b], [s32["mean"].b])
        k.tt("dve", v3(Y), v3(Y), b3(s32["mean"]), ALU.subtract, [Y.b, s32["mean"].b], [Y.b])
        k.tt("dve", SQ.t[:], Y.t[:], Y.t[:], ALU.mult, [Y.b], [SQ.b])
        k.reduce(s32["var"].t[:], v3(SQ), ALU.add, [SQ.b], [s32["var"].b])
        k.act(s32["var"].t[:], s32["var"].t[:], AF.Sqrt, [s32["var"].b], [s32["var"].b], scale=1.0 / 64,
              bias=k.eps_tile(GN_EPS))
        k.recip(s32["var"].t[:], s32["var"].t[:], [s32["var"].b], [s32["var"].b])
        k.tt("dve", v3(Y), v3(Y), b3(s32["var"]), ALU.mult, [Y.b, s32["var"].b], [Y.b])
        k.tt("dve", Y.t[:], Y.t[:], rows["lnw"].t[:], ALU.mult, [Y.b, rows["lnw"].b], [Y.b])
        k.tt("dve", Y.t[:], Y.t[:], rows["lnb"].t[:], ALU.add, [Y.b, rows["lnb"].b], [Y.b])
        k.tt("dve", R.t[:], R.t[:], KP.t[:], ALU.mult, [R.b, KP.b], [R.b])
        k.tt("dve", R.t[:], R.t[:], rows["rk"].t[:], ALU.mult, [R.b, rows["rk"].b], [R.b])
        k.reduce(s32["s"].t[:], v3(R), ALU.add, [R.b], [s32["s"].b])
        k.tt("dve", v3(V), v3(V), b3(s32["s"]), ALU.mult, [V.b, s32["s"].b], [V.b])
        k.tt("dve", Y.t[:], Y.t[:], V.t[:], ALU.add, [Y.b, V.b], [Y.b])
        k.tt("dve", Y.t[:], Y.t[:], G.t[:], ALU.mult, [Y.b, G.b], [Y.b])
        mx = mixT[i % 2]
        for kc in range(16):
            k.tr(big.t[:, kc * 128:(kc + 1) * 128], Y.t[:, kc * 128:(kc + 1) * 128], ident.t[:], [Y.b, ident.b], [big.b])
        k.copy("act", mx.t[:], big.t[:].rearrange("p (c t) -> p c t", c=16), [big.b], [mx.b])
        xl = None if gather is None else (lambda xt, i=i: gload(xt, x1, i, gather["x1_eoff"]))
        out_proj_tile(k, mx, [mx.b], 0, wo, big, xs, hb, junk, ss, rstd, rows["gain1"], x1, x2, row, xload=xl)
    return k.done()


def build_MEGA(SEQ, stop_after=None):
    T = SEQ
    NQT = SEQ // 512
    ND = 4 * NQT + 3
    NCH = SEQ // 128
    NF = DFF // 128
    k = K()
    k.fused = True
    E = {}

    def ein(name, shape, dt=F32):
        E[name] = k.ext_in(name, shape, dt)
        return E[name]

    x = ein("x", [T, D])
    gains = ein("gains", [8, D])
    ein("wA", [24, 128, 16, 128]); ein("wV", [2, 128, 16, 512])
    ein("qaug8", [8, 4, SEQ], BF16); ein("kaug8", [8, 4, SEQ], BF16); ein("biastab8", [8, 128, ND])
    ein("att_tri", [128, 128], BF16); ein("lq", [1, 256]); ein("subln", [1, 128])
    ein("lre8", [8, 128, 4]); ein("lim8", [8, 128, 4]); ein("ldt8", [8, 128, 4])
    ein("bre8", [8, 4, 128, 16]); ein("bim8", [8, 4, 128, 16])
    ein("ctre8", [8, 4, 128, 128]); ein("ctim8", [8, 4, 128, 128]); ein("dcol8", [8, 128, 1])
    ein("wglu", [8, 128, 8, 128]); ein("wout0", [4, 128, 16, 512])
    for l in range(2):
        ein(f"wg{l}", [NF, 128, 16, 128]); ein(f"wu{l}", [NF, 128, 16, 128]); ein(f"wd{l}", [16, 128, NF, 128])
    ein("mucol", [128, 6, 16])
    for n in ("wr", "wk", "wv"):
        ein(n, [4, 128, 16, 512])
    ein("w1", [128, 16, 96]); ein("a1", [128, 16, 96]); ein("g1w", [128, 16, 256])
    ein("w2", [96, D]); ein("a2", [96, D]); ein("g2w", [128, 2, D])
    for n in ("w0", "a0", "k_k", "k_a", "ln_w", "ln_b", "r_k"):
        ein(n, [1, D])
    ein("sc_tri", [128, 128]); ein("m_strict_T", [128, 512]); ein("m_incl_T", [128, 512]); ein("m_lo", [128, 512])
    ein("wout1", [4, 128, 16, 512])
    TO = T // 4
    out = k.ext_out("out", [TO, D])
    ein("own_idx", [128, TO // 128], mybir.dt.int32)

    qkT = k.dint("i_qkT", [16, 128, T], BF16)
    uT = k.dint("i_uT", [8, 128, T], F32)
    vint = k.dint("i_v", [T, 1024], BF16)
    attn = k.dint("i_attn", [T, 1024], BF16)
    ygT = k.dint("i_ygT", [8, 128, T], BF16)
    x1 = k.dint("i_x1", [T, D])
    x2h = k.dint("i_x2h", [T + 128, D])
    R = {n: k.dint("i_r_" + n, [T, D]) for n in ("r", "lw", "kp", "v", "kkn", "bb", "g")}
    yint = k.dint("i_y", [T, D])
    x3 = k.dint("i_x3", [TO, D])

    def g(i):
        return gains[i:i + 1, :]

    k.begin_phase("a_", dict(x=x, g0=g(0), wA=E["wA"], wV=E["wV"], qkT=qkT, uT=uT, v=vint))
    build_L1(T, k=k)
    k.begin_phase("b_", dict(tri=E["att_tri"], lq=E["lq"], subln=E["subln"]))
    items = [dict(q=qkT[h].rearrange("(m p) t -> m p t", m=2), k=qkT[8 + h].rearrange("(m p) t -> m p t", m=2),
                  v=vint[:, h * 128:(h + 1) * 128], qaug=E["qaug8"][h], kaug=E["kaug8"][h], bt=E["biastab8"][h],
                  out=attn[:, h * 128:(h + 1) * 128], slope=2.0 ** (-(h + 1))) for h in range(8)]
    build_ATT(SEQ, 1, k=k, items=items)
    if stop_after == "ATT":
        return k.finish_fused()
    k.begin_phase("c_", {})
    blocks = [dict(uT=[uT[cb]], out=[ygT[cb]], lre=E["lre8"][cb], lim=E["lim8"][cb], ldt=E["ldt8"][cb],
                   bre=E["bre8"][cb], bim=E["bim8"][cb], ctre=E["ctre8"][cb], ctim=E["ctim8"][cb],
                   dcol=E["dcol8"][cb]) for cb in range(8)]
    build_S5(SEQ, 1, k=k, blocks=blocks)
    k.begin_phase("d_", dict(x=x, attn=attn, ygT=ygT, wglu=E["wglu"], wout=E["wout0"], g1=g(1), x1=x1))
    build_OUT0(T, k=k)
    k.begin_phase("e_", dict(x1=x1, g2=g(2), g3=g(3), wg=E["wg0"], wu=E["wu0"], wd=E["wd0"], x2=x2h[128:128 + T, :]))
    zt = k.sb("zt", [128, D])
    k.memset("pool", zt.t[:], 0.0, [zt.b])
    k.dma("sp", x2h[0:128, :], zt.t[:], [zt.b], [], final=True)
    build_FFN(T, k=k)
    io = dict(x1h=x2h, g0=g(4))
    for n in ("mucol", "wr", "wk", "wv", "w1", "a1", "g1w", "w2", "a2", "g2w", "w0", "a0", "k_k", "k_a"):
        io[n] = E[n]
    io.update(R)
    k.begin_phase("f_", io)
    build_RPROJ(T, k=k)
    if stop_after == "RPROJ":
        return k.finish_fused()
    k.begin_phase("g_", dict(tri=E["sc_tri"], m_strict_T=E["m_strict_T"], m_incl_T=E["m_incl_T"], m_lo=E["m_lo"]))

    def cv(ap, p):
        return ap.rearrange("(c t) d -> c t d", t=128)[:, :, p * 512:(p + 1) * 512]
    passes = [dict(din={n: cv(R[n], p) for n in ("r", "lw", "kp", "v", "kkn", "bb")}, y=cv(yint, p)) for p in range(4)]
    build_RSCAN(NCH, 8, k=k, passes=passes)
    if stop_after == "RSCAN":
        return k.finish_fused()
    k.begin_phase("h_", dict(x1=x2h, y=yint, r=R["r"], kp=R["kp"], v=R["v"], g=R["g"], ln_w=E["ln_w"],
                             ln_b=E["ln_b"], r_k=E["r_k"], g1=g(5), wout=E["wout1"], x2=x3))
    build_OUT1(TO, k=k, gather=dict(idx=E["own_idx"], x1_eoff=128 * D))
    k.begin_phase("i_", dict(x1=x3, g2=g(6), g3=g(7), wg=E["wg1"], wu=E["wu1"], wd=E["wd1"], x2=out))
    build_FFN(TO, k=k)
    return k.finish_fused()


_CACHE = {}
_STOP_AFTER = None


def _c(a):
    return np.ascontiguousarray(a)


def kernel(x, norm_gains, ev_w_in, ev_lambda_qk, ev_attn_subln, ev_s5_lambda_re,
           ev_s5_lambda_im, ev_s5_log_dt, ev_s5_b_re, ev_s5_b_im, ev_s5_c_re, ev_s5_c_im,
           ev_s5_d, ev_s5_w_glu, ev_w_out, od_mu, od_w_r, od_w_k, od_w_v, od_w0, od_w1,
           od_w2, od_a0, od_a1, od_a2, od_g1, od_g2, od_k_k, od_k_a, od_r_k, od_ln_w,
           od_ln_b, od_w_o, ffn_w_gate, ffn_w_up, ffn_w_down):
    f32 = lambda a: np.asarray(a, dtype=np.float32)
    x = f32(x)
    B, SEQ, _ = x.shape
    if SEQ not in _CACHE:
        _CACHE[SEQ] = build_MEGA(SEQ, stop_after=_STOP_AFTER)
    nc = _CACHE[SEQ]
    W = {}
    W["gains"] = _c(f32(norm_gains).reshape(8, D))
    W["wA"], W["wV"] = host_L1_weights(f32(ev_w_in)[0])
    cons = [host_ATT_consts(h, SEQ) for h in range(8)]
    W["qaug8"] = np.stack([c[0] for c in cons]); W["kaug8"] = np.stack([c[1] for c in cons])
    W["biastab8"] = np.stack([c[2] for c in cons]); W["att_tri"] = cons[0][3]
    W["lq"] = _c(f32(ev_lambda_qk)[0].reshape(1, 256)); W["subln"] = _c(f32(ev_attn_subln)[0].reshape(1, 128))
    s5 = [host_S5_params(c, f32(ev_s5_lambda_re)[0], f32(ev_s5_lambda_im)[0], f32(ev_s5_log_dt)[0],
                         f32(ev_s5_b_re)[0], f32(ev_s5_b_im)[0], f32(ev_s5_c_re)[0], f32(ev_s5_c_im)[0],
                         f32(ev_s5_d)[0]) for c in range(8)]
    for n in ("lre", "lim", "ldt", "bre", "bim", "ctre", "ctim", "dcol"):
        W[n + "8"] = np.stack([d[n] for d in s5])
    W["wglu"], W["wout0"] = host_OUT_weights(f32(ev_s5_w_glu)[0], f32(ev_w_out)[0])
    for l in range(2):
        W[f"wg{l}"], W[f"wu{l}"], W[f"wd{l}"] = host_FFN_weights(f32(ffn_w_gate)[l], f32(ffn_w_up)[l], f32(ffn_w_down)[l])
    W.update(host_RPROJ_weights(f32(od_mu)[0], f32(od_w_r)[0], f32(od_w_k)[0], f32(od_w_v)[0], f32(od_w1)[0],
                                f32(od_a1)[0], f32(od_g1)[0], f32(od_w2)[0], f32(od_a2)[0], f32(od_g2)[0]))
    for n, a in (("w0", od_w0), ("a0", od_a0), ("k_k", od_k_k), ("k_a", od_k_a), ("ln_w", od_ln_w), ("ln_b", od_ln_b),
                 ("r_k", od_r_k)):
        W[n] = _c(f32(a)[0].reshape(1, D))
    rc = host_RSCAN_consts()
    W["sc_tri"] = rc["tri"]; W["m_strict_T"] = rc["m_strict_T"]; W["m_incl_T"] = rc["m_incl_T"]; W["m_lo"] = rc["m_lo"]
    _, W["wout1"] = host_OUT_weights(None, f32(od_w_o)[0])
    CPB = NCORES // B
    in_maps = []
    TO = SEQ // CPB
    for c in range(NCORES):
        d = dict(W)
        d["x"] = _c(x[c // CPB])
        rows = (c % CPB) * TO + np.arange(TO, dtype=np.int32)
        d["own_idx"] = _c(rows.reshape(TO // 128, 128).T)
        in_maps.append(d)
    res = run(nc, in_maps)
    return np.concatenate([res[c]["out"] for c in range(NCORES)], 0).reshape(B, SEQ, D)
```
